# Optimizing a Trainium2 kernel written in Bass

```python
import math
import jax, jax.numpy as jnp
from jax import lax
import numpy as np

D_MODEL = 1024
BATCH = 8
SEQ = 4096
DEPTH = 1

DA_HEADS = 4
DA_HEAD_DIM = 64
DA_V_DIM = 2 * DA_HEAD_DIM
DA_WIDTH = DA_HEADS * DA_V_DIM
QK_WIDTH = DA_HEADS * 2 * DA_HEAD_DIM
FT_GROUPS = 4
FT_GROUP_DIM = 128
FT_WIDTH = FT_GROUPS * FT_GROUP_DIM
MIX_WIDTH = DA_WIDTH + FT_WIDTH
IN_PROJ_WIDTH = 2 * QK_WIDTH + DA_WIDTH + FT_WIDTH
D_FF = 2816
CONV_WIDTH = 3
Q_BLOCK = 128
EPS = 1e-6
SUBLN_EPS = 1e-5

kernel_name = "hybrid_diffattn_fnet_convffn_encoder"


def rmsnorm(x, g, eps=EPS):
    xf = x.astype(jnp.float32)
    y = xf * lax.rsqrt(jnp.mean(xf * xf, axis=-1, keepdims=True) + eps) * g.astype(jnp.float32)
    return y.astype(x.dtype)


def alibi_slopes(n_heads):
    return jnp.array([2.0 ** (-8.0 * (h + 1) / n_heads) for h in range(n_heads)], dtype=jnp.float32)


def diff_attention(q, k, v, lam, slopes):
    B, S = q.shape[0], q.shape[1]
    nb = S // Q_BLOCK
    scale = DA_HEAD_DIM ** -0.5
    qb = q.reshape(B, nb, Q_BLOCK, DA_HEADS, 2, DA_HEAD_DIM).transpose(1, 0, 2, 3, 4, 5)
    pos_k = jnp.arange(S, dtype=jnp.int32)

    def block(args):
        qi, i = args
        pos_q = i * Q_BLOCK + jnp.arange(Q_BLOCK, dtype=jnp.int32)
        dist = jnp.abs(pos_q[:, None] - pos_k[None, :]).astype(jnp.float32)
        s = jnp.einsum('bqhcd,bkhcd->bhcqk', qi, k,
                       preferred_element_type=jnp.float32) * scale
        s = s - slopes[None, :, None, None, None] * dist[None, None, None]
        p = jax.nn.softmax(s, axis=-1)
        a = p[:, :, 0] - lam * p[:, :, 1]
        return jnp.einsum('bhqk,bkhv->bqhv', a.astype(v.dtype), v)

    o = lax.map(block, (qb, jnp.arange(nb, dtype=jnp.int32)))
    return o.transpose(1, 0, 2, 3, 4).reshape(B, S, DA_HEADS, DA_V_DIM)


def fourier_mix(u, w_ft):
    B, S = u.shape[0], u.shape[1]
    ug = u.reshape(B, S, FT_GROUPS, FT_GROUP_DIM).astype(jnp.float32)
    f = jnp.fft.fft2(ug, axes=(1, 3), norm="ortho").real
    out = jnp.einsum('bsgc,gcd->bsgd', f.astype(u.dtype), w_ft)
    return out.reshape(B, S, FT_WIDTH)


def centred_dwconv(h, w_conv, b_conv):
    hp = jnp.pad(h, ((0, 0), (1, 1), (0, 0)))
    return (w_conv[0] * hp[:, :-2] + w_conv[1] * hp[:, 1:-1] + w_conv[2] * hp[:, 2:] + b_conv)


def setup_inputs(seed: int = 0) -> dict:
    key = jax.random.key(seed)
    ks = jax.random.split(key, 16)
    f32 = jnp.float32
    nrm = lambda k, shape, s: (jax.random.normal(k, shape, f32) * s)
    return {
        "x": jax.random.normal(ks[0], (BATCH, SEQ, D_MODEL), f32),
        "g_mix": 1.0 + nrm(ks[1], (DEPTH, D_MODEL), 0.02),
        "w_in": nrm(ks[2], (DEPTH, D_MODEL, IN_PROJ_WIDTH), D_MODEL ** -0.5),
        "lambda_q1": nrm(ks[3], (DEPTH, DA_HEAD_DIM), 0.1),
        "lambda_k1": nrm(ks[4], (DEPTH, DA_HEAD_DIM), 0.1),
        "lambda_q2": nrm(ks[5], (DEPTH, DA_HEAD_DIM), 0.1),
        "lambda_k2": nrm(ks[6], (DEPTH, DA_HEAD_DIM), 0.1),
        "g_subln": 1.0 + nrm(ks[7], (DEPTH, DA_V_DIM), 0.02),
        "w_ft": nrm(ks[8], (DEPTH, FT_GROUPS, FT_GROUP_DIM, FT_GROUP_DIM), FT_GROUP_DIM ** -0.5),
        "w_out": nrm(ks[9], (DEPTH, MIX_WIDTH, D_MODEL), MIX_WIDTH ** -0.5),
        "g_ffn": 1.0 + nrm(ks[10], (DEPTH, D_MODEL), 0.02),
        "w_up": nrm(ks[11], (DEPTH, D_MODEL, 2 * D_FF), D_MODEL ** -0.5),
        "w_conv": nrm(ks[12], (DEPTH, CONV_WIDTH, 2 * D_FF), CONV_WIDTH ** -0.5),
        "b_conv": nrm(ks[13], (DEPTH, 2 * D_FF), 0.01),
        "w_down": nrm(ks[14], (DEPTH, D_FF, D_MODEL), D_FF ** -0.5),
        "g_final": 1.0 + nrm(ks[15], (D_MODEL,), 0.02),
    }


def reference(x, g_mix, w_in, lambda_q1, lambda_k1, lambda_q2, lambda_k2, g_subln, w_ft,
              w_out, g_ffn, w_up, w_conv, b_conv, w_down, g_final):
    B, S = x.shape[0], x.shape[1]
    slopes = alibi_slopes(DA_HEADS)
    for l in range(DEPTH):
        lambda_init = 0.8 - 0.6 * math.exp(-0.3 * l)
        h = rmsnorm(x, g_mix[l])
        z = h @ w_in[l]
        q = z[..., :QK_WIDTH].reshape(B, S, DA_HEADS, 2, DA_HEAD_DIM)
        k = z[..., QK_WIDTH:2 * QK_WIDTH].reshape(B, S, DA_HEADS, 2, DA_HEAD_DIM)
        v = z[..., 2 * QK_WIDTH:2 * QK_WIDTH + DA_WIDTH].reshape(B, S, DA_HEADS, DA_V_DIM)
        u = z[..., 2 * QK_WIDTH + DA_WIDTH:]
        lam = (jnp.exp(jnp.sum(lambda_q1[l].astype(jnp.float32) * lambda_k1[l].astype(jnp.float32)))
               - jnp.exp(jnp.sum(lambda_q2[l].astype(jnp.float32) * lambda_k2[l].astype(jnp.float32)))
               + lambda_init)
        o_da = diff_attention(q, k, v, lam, slopes)
        o_da = rmsnorm(o_da, g_subln[l], SUBLN_EPS) * (1.0 - lambda_init)
        o_ft = fourier_mix(u, w_ft[l])
        mixed = jnp.concatenate([o_da.reshape(B, S, DA_WIDTH), o_ft], axis=-1)
        x = x + mixed @ w_out[l]
        h = rmsnorm(x, g_ffn[l])
        up = centred_dwconv(h @ w_up[l], w_conv[l], b_conv[l])
        gate, val = up[..., :D_FF], up[..., D_FF:]
        x = x + (jax.nn.silu(gate) * val) @ w_down[l]
    return rmsnorm(x, g_final)
```

```python
import math
from contextlib import ExitStack
import numpy as np
import concourse.bass as bass
import concourse.mybir as mybir
from concourse.bass_utils import run_bass_kernel_spmd

F32 = mybir.dt.float32
BF16 = mybir.dt.bfloat16
F16 = mybir.dt.float16
I32 = mybir.dt.int32
AF = mybir.ActivationFunctionType
ALU = mybir.AluOpType
AX = mybir.AxisListType


class _Res:
    def __init__(self, name, sem, attr=None):
        self.name, self.sem, self.attr = name, sem, attr
        self.count = 0
        self.pos = 0
        self.flushed = 0
        self.prog = []
        self.seen = {}


class _Buf:
    def __init__(self, name):
        self.name = name
        self.w = None
        self.r = []


class Sched:
    def __init__(self, nc, stack):
        self.nc, self.st = nc, stack
        mk = lambda n, a: _Res(n, stack.enter_context(nc.semaphore("s_" + n)), a)
        self.PE, self.ACT, self.DVE = mk("pe", "tensor"), mk("act", "scalar"), mk("dve", "vector")
        self.POOL, self.SP = mk("pool", "gpsimd"), mk("sp", "sync")
        self.engines = [self.PE, self.ACT, self.DVE, self.POOL, self.SP]
        self.n_instr = 0

    def sb(self, name, shape, dt):
        return self.st.enter_context(self.nc.sbuf_tensor(name, shape, dt))

    def ps(self, name, shape, dt):
        return self.st.enter_context(self.nc.psum_tensor(name, shape, dt))

    def buf(self, name="b"):
        return _Buf(name)

    def dsem(self, name):
        return _Res(name, self.st.enter_context(self.nc.semaphore("d_" + name)))

    def _waits(self, E, reads, writes, extra=()):
        deps = {}
        def add(tok):
            if tok is None:
                return
            r, v = tok
            if r is E and E is self.PE:
                return
            if deps.get(r, 0) < v:
                deps[r] = v
        for b in reads:
            add(b.w)
        for b in writes:
            add(b.w)
            for t in b.r:
                add(t)
        for t in extra:
            add(t)
        for r, v in deps.items():
            if E.seen.get(r, 0) < v:
                E.seen[r] = v
                E.prog.append(("w", r, v))

    def _mark(self, tok, reads, writes):
        for b in writes:
            b.w = tok
            b.r = []
        for b in reads:
            if b not in writes:
                b.r.append(tok)
        self.n_instr += 1

    def op(self, E, fn, reads, writes, inc=True):
        self._waits(E, reads, writes)
        E.pos += 1
        tok = (E, E.pos)
        E.prog.append(("i", fn, E.pos))
        self._mark(tok, reads, writes)
        return tok

    def dma(self, E, out, in_, reads, writes, ds, **kw):
        prev = (ds, ds.count) if ds.count else None
        self._waits(E, reads, writes, extra=(prev,) if prev else ())
        ds.count += 16
        tok = (ds, ds.count)
        E.prog.append(("d", lambda e: e.dma_start(out=out, in_=in_, **kw), ds))
        self._mark(tok, reads, writes)
        return tok

    def barrier(self):
        for E in self.engines:
            for R in self.engines:
                if R is self.SP or (R is E and E is self.PE) or R.pos == 0:
                    continue
                if R.prog and R.prog[-1][0] == "d":
                    pass
                if E.seen.get(R, 0) < R.pos:
                    E.seen[R] = R.pos
                    E.prog.append(("w", R, R.pos))

    def flush(self, final_dsems=()):
        self.barrier()
        for ds in final_dsems:
            if ds.count:
                self.SP.prog.append(("w", ds, ds.count))
        nc = self.nc
        comp = [E for E in self.engines]
        needed = {E: set() for E in comp}
        for E in comp:
            for it in E.prog:
                if it[0] == "w" and it[1] in needed and it[2] > it[1].flushed:
                    needed[it[1]].add(it[2])
        rank = {}
        for E in comp:
            for k, p in enumerate(sorted(needed[E])):
                rank[(E, p)] = E.count + k + 1
        progs = {}
        for E in comp:
            out = []
            for it in E.prog:
                if it[0] == "w":
                    r, v = it[1], it[2]
                    if r in needed:
                        if v <= r.flushed:
                            continue
                        out.append(("w", r.sem, rank[(r, v)]))
                    else:
                        out.append(("w", r.sem, v))
                elif it[0] == "i":
                    out.append(("i", it[1], E.sem if (E, it[2]) in rank else None, 1))
                else:
                    out.append(("i", it[1], it[2].sem, 16))
            progs[E.attr] = out
        for E in comp:
            E.count += len(needed[E])
            E.flushed = E.pos
            E.prog = []

        def runner(prog):
            def _(eng):
                for it in prog:
                    if it[0] == "w":
                        eng.wait_ge(it[1], it[2])
                    else:
                        ins = it[1](eng)
                        if it[2] is not None:
                            ins.then_inc(it[2], it[3])
            return _
        with nc.Block() as block:
            block.tensor(runner(progs["tensor"]))
            block.scalar(runner(progs["scalar"]))
            block.vector(runner(progs["vector"]))
            block.gpsimd(runner(progs["gpsimd"]))
            block.sync(runner(progs["sync"]))

    def finish(self, dsems):
        self.flush(dsems)


S_LEN = 4096
D = 1024
NT = S_LEN // 128
KC = D // 128
NH = 4
DFF = 2816
NCH = DFF // 128
EPS = 1e-6
SUBLN_EPS = 1e-5
LAMBDA_INIT = 0.8 - 0.6 * math.exp(0.0)
SLOPES = [2.0 ** (-8.0 * (h + 1) / NH) for h in range(NH)]
PI_IN = 3.1415925
FT_NORM = 1.0 / math.sqrt(4096.0 * 128.0)


def build_program(debug=None):
    nc = bass.Bass("TRN2", target_bir_lowering=False)
    dr = lambda n, shp, dt=F32, kind="ExternalInput": nc.dram_tensor(n, shp, dt, kind=kind).ap()
    x_d = dr("x", [S_LEN, D])
    gmix_d = dr("g_mix", [D]); win_d = dr("w_in", [D, 2048])
    lq1_d, lk1_d, lq2_d, lk2_d = (dr(n, [64]) for n in ("lambda_q1", "lambda_k1", "lambda_q2", "lambda_k2"))
    gsub_d = dr("g_subln", [128]); wft_d = dr("w_ft", [4, 128, 128]); wout_d = dr("w_out", [D, D])
    gffn_d = dr("g_ffn", [D]); wup_d = dr("w_up", [D, 2 * DFF]); wconv_d = dr("w_conv", [3, 2 * DFF])
    bconv_d = dr("b_conv", [2 * DFF]); wdn_d = dr("w_down", [DFF, D]); gfin_d = dr("g_final", [D])
    y_d = dr("y", [S_LEN, D], F32, "ExternalOutput")
    wup_s = nc.dram_tensor("wup_s", [NCH, 128, 2 * KC * 128], BF16).ap()
    dbg = {}
    if debug:
        for n, (shp, dt) in debug.items():
            dbg[n] = dr("dbg_" + n, shp, dt, "ExternalOutput")

    with ExitStack() as st:
        S = Sched(nc, st)
        PE, ACT, DVE, POOL, SP = S.PE, S.ACT, S.DVE, S.POOL, S.SP
        out_sems = []

        mixT = S.sb("mixT", [128, 8, S_LEN], BF16)
        b_mix = [S.buf("mix%d" % i) for i in range(8)]
        ident = S.sb("ident", [128, 128], BF16)
        iota_i = S.sb("iota_i", [128, 128], I32)
        gsub = S.sb("gsub", [128, 128], F32)
        wcv = S.sb("wcv", [128, 4, 2 * NCH], F32)
        gcol = S.sb("gcol", [128, 2, KC], F32)
        lamt = S.sb("lamt", [128, 4, 64], F32)
        lams = S.sb("lams", [128, 8], F32)
        wft = S.sb("wft", [128, 4, 128], BF16)
        nbias = S.sb("nbias", [128, 1], F32)
        b_const = S.buf("const")
        pt = S.ps("pt", [128, 1024], BF16)
        pa = S.ps("pa", [128, 1024], F32)
        pb = S.ps("pb", [128, 1024], F32)
        pc = S.ps("pc", [128, 1536], F32)
        b_pt, b_pa, b_pb, b_pc = S.buf("pt"), S.buf("pa"), S.buf("pb"), S.buf("pc")
        b_pa2 = [S.buf("pa0"), S.buf("pa1")]
        b_pb2 = [S.buf("pb0"), S.buf("pb1")]
        b_pc3 = [S.buf("pc0"), S.buf("pc1"), S.buf("pc2")]
        dc = [S.dsem("c%d" % i) for i in range(4)]

        S.op(DVE, lambda e: e.memset(nbias[:], -PI_IN), [], [b_const])
        S.op(POOL, lambda e: e.iota(iota_i[:], pattern=[[1, 128]], base=0, channel_multiplier=-1), [], [b_const])
        S.op(DVE, lambda e: e.tensor_single_scalar(out=ident[:], in_=iota_i[:], scalar=0, op=ALU.is_equal), [b_const], [b_const])
        S.dma(SP, gsub[:], gsub_d.partition_broadcast(128), [], [b_const], dc[0])
        S.dma(SP, wcv[:, 0:3, :], wconv_d.rearrange("t (c p) -> p t c", p=128), [], [b_const], dc[2], allow_slow_non_contiguous=True)
        S.dma(SP, wcv[:, 3, :], bconv_d.rearrange("(c p) -> p c", p=128), [], [b_const], dc[3], allow_slow_non_contiguous=True)
        S.dma(SP, gcol[:, 0, :], gmix_d.rearrange("(c p) -> p c", p=128), [], [b_const], dc[0], allow_slow_non_contiguous=True)
        S.dma(SP, gcol[:, 1, :], gffn_d.rearrange("(c p) -> p c", p=128), [], [b_const], dc[1], allow_slow_non_contiguous=True)
        for i, ld in enumerate((lq1_d, lk1_d, lq2_d, lk2_d)):
            S.dma(SP, lamt[:, i, :], ld.partition_broadcast(128), [], [b_const], dc[2 + i % 2])
        S.op(DVE, lambda e: e.tensor_scalar(out=gsub[:], in0=gsub[:], scalar1=1.0 - LAMBDA_INIT, scalar2=None, op0=ALU.mult), [b_const], [b_const])
        S.op(DVE, lambda e: e.tensor_tensor(out=lamt[:, 0, :], in0=lamt[:, 0, :], in1=lamt[:, 1, :], op=ALU.mult), [b_const], [b_const])
        S.op(DVE, lambda e: e.tensor_tensor(out=lamt[:, 2, :], in0=lamt[:, 2, :], in1=lamt[:, 3, :], op=ALU.mult), [b_const], [b_const])
        S.op(DVE, lambda e: e.reduce_sum(out=lams[:, 0:1], in_=lamt[:, 0, :], axis=AX.X), [b_const], [b_const])
        S.op(DVE, lambda e: e.reduce_sum(out=lams[:, 1:2], in_=lamt[:, 2, :], axis=AX.X), [b_const], [b_const])
        S.op(ACT, lambda e: e.activation(out=lams[:, 2:4], in_=lams[:, 0:2], func=AF.Exp), [b_const], [b_const])
        S.op(DVE, lambda e: e.scalar_tensor_tensor(out=lams[:, 4:5], in0=lams[:, 3:4], scalar=-LAMBDA_INIT, in1=lams[:, 2:3], op0=ALU.add, op1=ALU.subtract), [b_const], [b_const])

        with ExitStack() as ph:
            S.st = ph
            stg = [S.sb("stg%d" % i, [128, 2 * DFF], F32) for i in range(2)]
            wbig = S.sb("wbig", [128, NCH, 2, KC, 128], BF16)
            b_stg = [S.buf(), S.buf()]; b_wbig = S.buf()
            dl = [S.dsem("wl0"), S.dsem("wl1")]; dsv = [S.dsem("ws0"), S.dsem("ws1")]
            S.dma(SP, stg[0][:, 0:512].rearrange("p (g d) -> p g d", g=4), wft_d.rearrange("g c d -> c g d"), [], [b_stg[0]], dl[0])
            S.op(DVE, lambda e: e.tensor_copy(out=wft[:].rearrange("p g d -> p (g d)"), in_=stg[0][:, 0:512]), [b_stg[0]], [b_const])
            for kc in range(KC):
                s_ = kc % 2
                S.dma(SP, stg[s_][:], wup_d[kc * 128:(kc + 1) * 128, :], [], [b_stg[s_]], dl[s_])
                src = lambda gv, s_=s_: stg[s_][:, gv * DFF:(gv + 1) * DFF].rearrange("p (i ch) -> p i ch", ch=128)
                S.op(DVE, lambda e, kc=kc, s_=s_: e.tensor_scalar(out=wbig[:, :, 0, kc, :], in0=stg[s_][:, 0:DFF].rearrange("p (i ch) -> p i ch", ch=128), scalar1=gcol[:, 1, kc:kc + 1], scalar2=None, op0=ALU.mult),
                     [b_stg[s_], b_const], [b_wbig])
                S.op(ACT, lambda e, kc=kc, s_=s_: e.activation(out=wbig[:, :, 1, kc, :], in_=stg[s_][:, DFF:2 * DFF].rearrange("p (i ch) -> p i ch", ch=128), func=AF.Copy, scale=gcol[:, 1, kc:kc + 1]),
                     [b_stg[s_], b_const], [b_wbig])
            for q in range(2):
                lo, hi = q * (NCH // 2), (q + 1) * (NCH // 2)
                S.dma(SP, wup_s[lo:hi, :, :].rearrange("i p n -> p i n"), wbig[:, lo:hi, :, :, :].rearrange("p i a k c -> p i (a k c)"), [b_wbig], [], dsv[q])
            win_sb = mixT[:, 0:4, :].rearrange("p a (b n) -> p (a b) n", n=2048)
            for kc in range(KC):
                s_ = kc % 2
                S.dma(SP, stg[s_][:, 0:2048], win_d[kc * 128:(kc + 1) * 128, :], [], [b_stg[s_]], dl[s_])
                if s_ == 0:
                    S.op(DVE, lambda e, kc=kc, s_=s_: e.tensor_scalar(out=win_sb[:, kc, :], in0=stg[s_][:, 0:2048], scalar1=gcol[:, 0, kc:kc + 1], scalar2=None, op0=ALU.mult),
                         [b_stg[s_], b_const], b_mix[0:4])
                else:
                    S.op(ACT, lambda e, kc=kc, s_=s_: e.activation(out=win_sb[:, kc, :], in_=stg[s_][:, 0:2048], func=AF.Copy, scale=gcol[:, 0, kc:kc + 1]),
                         [b_stg[s_], b_const], b_mix[0:4])
            wups_done = [(dsv[0], dsv[0].count), (dsv[1], dsv[1].count)]
            S.barrier()
            for E in S.engines:
                for r, v in wups_done:
                    E.seen[r] = v
                    E.prog.append(("w", r, v))
            S.flush()
        S.st = st

        def emit_rstd(dst, ss, dim, eps, bufs):
            S.op(DVE, lambda e: e.tensor_scalar(out=dst, in0=ss, scalar1=1.0 / dim, scalar2=eps, op0=ALU.mult, op1=ALU.add), bufs, bufs)
            S.op(ACT, lambda e: e.activation(out=dst, in_=dst, func=AF.Ln), bufs, bufs)
            S.op(ACT, lambda e: e.activation(out=dst, in_=dst, func=AF.Exp, scale=-0.5), bufs, bufs)

        with ExitStack() as phab:
            S.st = phab
            QT = S.sb("QT", [128, NH, S_LEN], BF16)
            KT = S.sb("KT", [128, NH, S_LEN], BF16)
            VA = S.sb("VA", [128, NT, NH, 129], BF16)
            UT = mixT[:, 4:8, :]
            b_qk = S.buf("qk"); b_v = S.buf("v"); b_ut = b_mix[4:8]
            S.op(POOL, lambda e: e.memset(VA[:, :, :, 128:129], 1.0), [], [b_v])
            with ExitStack() as pha:
                S.st = pha
                xt = [S.sb("xt%d" % i, [128, D], F32) for i in range(2)]
                xn = [S.sb("xn%d" % i, [128, D], BF16) for i in range(2)]
                junk = S.sb("junkA", [128, D], BF16)
                ssq = S.sb("ssqA", [128, 2], F32)
                hT = [S.sb("hT%d" % i, [128, KC, 512], BF16) for i in range(2)]
                b_xt = [S.buf(), S.buf()]; b_xn = [S.buf(), S.buf()]; b_hT = [S.buf(), S.buf()]
                b_junk = S.buf(); b_ss = [S.buf(), S.buf()]
                dx = [S.dsem("xa0"), S.dsem("xa1")]
                win_sb = mixT[:, 0:4, :].rearrange("p a (b n) -> p (a b) n", n=2048)
                b_win = b_mix[0:4]
                rot = [0]
                def big_group(sp, oi):
                    hs = sp % 2
                    kind, hh = oi // 4, oi % 4
                    col0 = (0, 512, 1536)[kind] + hh * 128
                    bank = rot[0] % 3; rot[0] += 1
                    for kc in range(KC):
                        S.op(PE, lambda e, kc=kc: e.matmul(pc[:, bank * 512:(bank + 1) * 512], lhsT=win_sb[:, kc, col0:col0 + 128], rhs=hT[hs][:, kc, :], start=(kc == 0), stop=(kc == KC - 1)),
                             [b_hT[hs]] + b_win, [b_pc3[bank]])
                    dst = (QT, KT, None)[kind]
                    if kind < 2:
                        if oi % 2 == 0:
                            S.op(ACT, lambda e: e.activation(out=dst[:, hh, sp * 512:(sp + 1) * 512], in_=pc[:, bank * 512:(bank + 1) * 512], func=AF.Copy),
                                 [b_pc3[bank]], [b_qk])
                        else:
                            S.op(DVE, lambda e: e.tensor_copy(out=dst[:, hh, sp * 512:(sp + 1) * 512], in_=pc[:, bank * 512:(bank + 1) * 512]),
                                 [b_pc3[bank]], [b_qk])
                    else:
                        S.op(DVE, lambda e: e.tensor_copy(out=mixT[:, 4 + hh, sp * 512:(sp + 1) * 512], in_=pc[:, bank * 512:(bank + 1) * 512]),
                             [b_pc3[bank]], [b_mix[4 + hh]])

                def tile_chain(sp, tt):
                    hs = sp % 2
                    t = sp * 4 + tt
                    s_ = t % 2
                    S.dma(SP, xt[s_][:], x_d[t * 128:(t + 1) * 128, :], [], [b_xt[s_]], dx[s_])
                    S.op(ACT, lambda e: e.activation(out=junk[:], in_=xt[s_][:], func=AF.Square, accum_out=ssq[:, s_:s_ + 1]),
                         [b_xt[s_]], [b_junk, b_ss[s_]])
                    emit_rstd(ssq[:, s_:s_ + 1], ssq[:, s_:s_ + 1], D, EPS, [b_ss[s_]])
                    S.op(DVE, lambda e: e.tensor_scalar(out=xn[s_][:], in0=xt[s_][:], scalar1=ssq[:, s_:s_ + 1], scalar2=None, op0=ALU.mult),
                         [b_xt[s_], b_ss[s_]], [b_xn[s_]])
                    for kc in range(KC):
                        S.op(PE, lambda e, kc=kc: e.transpose(pt[:, kc * 128:(kc + 1) * 128], xn[s_][:, kc * 128:(kc + 1) * 128], ident[:]),
                             [b_xn[s_], b_const], [b_pt])
                    S.op(ACT, lambda e: e.activation(out=hT[hs][:, :, tt * 128:(tt + 1) * 128], in_=pt[:].rearrange("p (k n) -> p k n", k=KC), func=AF.Copy),
                         [b_pt], [b_hT[hs]])

                def v_tile(sp, tt):
                    hs = sp % 2
                    t = sp * 4 + tt
                    pv, bpv = (pa, b_pa2) if t % 2 == 0 else (pb, b_pb2)
                    for kc in range(KC):
                        S.op(PE, lambda e, kc=kc: e.matmul(pv[:, 0:512], lhsT=hT[hs][:, kc, tt * 128:(tt + 1) * 128], rhs=win_sb[:, kc, 1024:1536], start=(kc == 0), stop=(kc == KC - 1)),
                             [b_hT[hs]] + b_win, [bpv[0]])
                    S.op(DVE, lambda e: e.tensor_copy(out=VA[:, t, :, 0:128], in_=pv[:, 0:512].rearrange("p (h v) -> p h v", h=NH)),
                         [bpv[0]], [b_v])

                for sp in range(NT // 4 + 1):
                    for tt in range(4):
                        if sp < NT // 4:
                            tile_chain(sp, tt)
                        if sp > 0:
                            for oi in range(tt * 3, tt * 3 + 3):
                                big_group(sp - 1, oi)
                        if sp < NT // 4:
                            v_tile(sp, tt)
                S.barrier()
                if debug and "QT" in debug:
                    dd = S.dsem("dbgA")
                    S.dma(SP, dbg["QT"], QT[:], [b_qk], [], dd); out_sems.append(dd)
                    dd = S.dsem("dbgA2")
                    S.dma(SP, dbg["KT"], KT[:], [b_qk], [], dd); out_sems.append(dd)
                    dd = S.dsem("dbgA3")
                    S.dma(SP, dbg["VA"], VA[:], [b_v], [], dd); out_sems.append(dd)
                    dd = S.dsem("dbgA4")
                    S.dma(SP, dbg["UT"], mixT[:, 4:8, :], b_mix[4:8], [], dd); out_sems.append(dd)
                    S.barrier()
                S.flush()
            S.st = phab
            if debug and debug.get("_stop") == "A":
                S.barrier(); S.flush(out_sems)
                return nc
            with ExitStack() as phb:
                S.st = phb
                Rcol = S.sb("Rcol", [128, 60], I32)
                colb = S.sb("colb", [128, NH, 60], F32)
                R2 = S.sb("R2", [128, 4], I32)
                R2f = S.sb("R2f", [128, 4], F32)
                fpm = S.sb("fpm", [128, NH, 2, 4], F32)
                Qr = S.sb("Qr", [128, 512], F32)
                Ti = S.sb("Ti", [128, 512], I32)
                Tf = S.sb("Tf", [128, 512], F32)
                Bt = S.sb("Bt", [128, 4, 512], F16)
                ident16 = S.sb("ident16", [128, 128], F16)
                Eb = S.sb("Eb", [128, 3, 2, 512], BF16)
                Oul = S.sb("Oul", [128, 2, 3, 387], F32)
                num = S.sb("num", [128, 8, 129], F32)
                rden = S.sb("rden", [128, 8], F32)
                Pn = S.sb("Pn", [128, 8, 128], F32)
                o_t = S.sb("o_t", [128, 4, 128], F32)
                junkB = S.sb("junkB", [128, 128], F32)
                ssb = S.sb("ssb", [128, 4], F32)
                on_t = S.sb("on_t", [128, 4, 128], BF16)
                b_tab = S.buf("tab"); b_Bt = S.buf("Bt"); b_T = S.buf("T")
                b_E = [[S.buf(), S.buf()] for _ in range(3)]
                b_O = [S.buf("Oup"), S.buf("Olo")]
                b_num, b_P, b_o, b_ss, b_on = S.buf(), S.buf(), S.buf(), S.buf(), S.buf()
                zcol_i = 28

                S.op(POOL, lambda e: e.iota(Rcol[:], pattern=[[128, 60]], base=-28 * 128 - 256, channel_multiplier=1), [], [b_tab])
                S.op(POOL, lambda e: e.iota(R2[:], pattern=[[128, 4]], base=-256, channel_multiplier=1), [], [b_tab])
                S.op(POOL, lambda e: e.iota(Ti[:], pattern=[[1, 512]], base=-256, channel_multiplier=0), [], [b_T])
                S.op(DVE, lambda e: e.tensor_copy(out=Qr[:], in_=Ti[:]), [b_T], [b_tab, b_T])
                S.op(DVE, lambda e: e.tensor_copy(out=R2f[:], in_=R2[:]), [b_tab], [b_tab])
                S.op(DVE, lambda e: e.tensor_copy(out=ident16[:], in_=ident[:]), [b_const], [b_tab])
                for h in range(NH):
                    sl = SLOPES[h]
                    S.op(DVE, lambda e, h=h, sl=sl: e.tensor_scalar(out=colb[:, h, 0:28], in0=Rcol[:, 0:28], scalar1=sl, scalar2=None, op0=ALU.mult), [b_tab], [b_tab])
                    S.op(DVE, lambda e, h=h, sl=sl: e.tensor_scalar(out=colb[:, h, 28:60], in0=Rcol[:, 28:60], scalar1=-sl, scalar2=None, op0=ALU.mult), [b_tab], [b_tab])
                    S.op(DVE, lambda e, h=h: e.memset(colb[:, h, 28:32], 0.0), [b_tab], [b_tab])
                    S.op(ACT, lambda e, h=h, sl=sl: e.activation(out=fpm[:, h, 0, :], in_=R2f[:], func=AF.Exp, scale=sl), [b_tab], [b_tab])
                    S.op(ACT, lambda e, h=h, sl=sl: e.activation(out=fpm[:, h, 1, :], in_=R2f[:], func=AF.Exp, scale=-sl), [b_tab], [b_tab])

                def acc_ap(c, j, w=129):
                    idx = c * 4 + j
                    o0 = (idx // 3) * 512 + (idx % 3) * 129
                    return pc[:, o0:o0 + w], b_pc3[idx // 3]

                def reg(t, c, j, lo=0, hi=129):
                    idx = c * 4 + j
                    return Oul[:, t, idx // 3, (idx % 3) * 129 + lo:(idx % 3) * 129 + hi]

                def comb1(h, r):
                    for c in range(2):
                        for j in range(4):
                            idx = c * 4 + j
                            S.op(DVE, lambda e, c=c, j=j, idx=idx: e.tensor_scalar(out=num[:, idx, :], in0=reg(0, c, j), scalar1=fpm[:, h, 0, j:j + 1], scalar2=None, op0=ALU.mult),
                                 [b_O[0], b_tab], [b_num])
                            if r > 0:
                                S.op(DVE, lambda e, c=c, j=j, idx=idx: e.scalar_tensor_tensor(out=num[:, idx, :], in0=reg(1, c, j), scalar=fpm[:, h, 1, j:j + 1], in1=num[:, idx, :], op0=ALU.mult, op1=ALU.add),
                                     [b_O[1], b_tab, b_num], [b_num])
                    S.op(DVE, lambda e: e.reciprocal(out=rden[:], in_=num[:, :, 128]), [b_num], [b_num])
                    for idx in range(8):
                        S.op(DVE, lambda e, idx=idx: e.tensor_scalar(out=Pn[:, idx, :], in0=num[:, idx, 0:128], scalar1=rden[:, idx:idx + 1], scalar2=None, op0=ALU.mult),
                             [b_num], [b_P])
                    S.op(DVE, lambda e: e.scalar_tensor_tensor(out=o_t[:], in0=Pn[:, 4:8, :], scalar=lams[:, 4:5], in1=Pn[:, 0:4, :], op0=ALU.mult, op1=ALU.add),
                         [b_P, b_const], [b_o])
                    for j in range(4):
                        S.op(DVE, lambda e, j=j: e.scalar_tensor_tensor(out=junkB[:], in0=o_t[:, j, :], scalar=1.0, in1=o_t[:, j, :], op0=ALU.mult, op1=ALU.mult, accum_out=ssb[:, j:j + 1]),
                             [b_o], [b_ss])
                    S.op(DVE, lambda e: e.tensor_scalar(out=ssb[:], in0=ssb[:], scalar1=1.0 / 128, scalar2=SUBLN_EPS, op0=ALU.mult, op1=ALU.add), [b_ss], [b_ss])

                def comb2(h, r):
                    S.op(ACT, lambda e: e.activation(out=ssb[:], in_=ssb[:], func=AF.Ln), [b_ss], [b_ss])
                    S.op(ACT, lambda e: e.activation(out=ssb[:], in_=ssb[:], func=AF.Exp, scale=-0.5), [b_ss], [b_ss])

                def comb3(h, r):
                    for j in range(4):
                        S.op(DVE, lambda e, j=j: e.scalar_tensor_tensor(out=on_t[:, j, :], in0=o_t[:, j, :], scalar=ssb[:, j:j + 1], in1=gsub[:], op0=ALU.mult, op1=ALU.mult),
                             [b_o, b_ss, b_const], [b_on])
                    for j in range(4):
                        S.op(PE, lambda e, j=j: e.transpose(pt[:, j * 128:(j + 1) * 128], on_t[:, j, :], ident[:]), [b_on, b_const], [b_pt], inc=(j == 3))
                    S.op(DVE, lambda e: e.tensor_copy(out=mixT[:, h, r * 512:(r + 1) * 512], in_=pt[:, 0:512]), [b_pt], [b_mix[h]])

                pend = [None]
                def gen_tables(h):
                    sl = SLOPES[h]
                    for a in range(4):
                        S.op(POOL, lambda e, a=a: e.iota(Ti[:], pattern=[[-1, 512]], base=128 * a, channel_multiplier=1), [], [b_T])
                        S.op(DVE, lambda e: e.tensor_copy(out=Tf[:], in_=Ti[:]), [b_T], [b_T])
                        S.op(DVE, lambda e: e.scalar_tensor_tensor(out=Tf[:], in0=Tf[:], scalar=-1.0, in1=Tf[:], op0=ALU.mult, op1=ALU.max), [b_T], [b_T])
                        S.op(DVE, lambda e: e.tensor_tensor(out=Tf[:], in0=Tf[:], in1=Qr[:], op=ALU.add), [b_T, b_tab], [b_T])
                        S.op(DVE, lambda e: e.tensor_scalar(out=Tf[:], in0=Tf[:], scalar1=-8.0 * sl, scalar2=None, op0=ALU.mult), [b_T], [b_T])
                        S.op(DVE, lambda e, a=a: e.tensor_copy(out=Bt[:, a, :], in_=Tf[:]), [b_T], [b_Bt])

                def qk(h, r, kt, sc):
                    a = kt - 4 * r
                    inr = 0 <= a <= 3
                    ps, bps = (pa, b_pa2) if sc == 0 else (pb, b_pb2)
                    for c in range(2):
                        S.op(PE, lambda e, c=c: e.matmul(ps[:, c * 512:(c + 1) * 512], lhsT=KT[c * 64:(c + 1) * 64, h, kt * 128:(kt + 1) * 128],
                                                         rhs=QT[c * 64:(c + 1) * 64, h, r * 512:(r + 1) * 512], start=True, stop=not inr),
                             [b_qk], [bps[c]])
                    if inr:
                        for c in range(2):
                            S.op(PE, lambda e, c=c: e.matmul(ps[:, c * 512:(c + 1) * 512], lhsT=ident16[:], rhs=Bt[:, a, :], start=False, stop=True),
                                 [b_Bt, b_tab], [bps[c]])

                def act_(h, r, kt, sc, es):
                    a = kt - 4 * r
                    ps, bps = (pa, b_pa2) if sc == 0 else (pb, b_pb2)
                    S.op(ACT, lambda e: e.activation(out=Eb[:, es, :, :].rearrange("p c n -> p (c n)"), in_=ps[:, 0:1024], func=AF.Exp,
                                                     bias=colb[:, h, a + 28:a + 29], scale=0.125),
                         [bps[0], bps[1], b_tab], [b_E[es][0], b_E[es][1]])

                def av_(h, r, kt, es):
                    grp_first = (kt == 0) or (kt == 4 * r)
                    grp_last = (kt == 4 * r - 1) or (kt == NT - 1)
                    for c in range(2):
                        for j in range(4):
                            ap, bb = acc_ap(c, j)
                            S.op(PE, lambda e, c=c, j=j, ap=ap: e.matmul(ap, lhsT=Eb[:, es, c, j * 128:(j + 1) * 128], rhs=VA[:, kt, h, :], start=(grp_first and (c * 4 + j) % 3 == 0), stop=(grp_last and (c * 4 + j) in (2, 5, 7))),
                                 [b_E[es][c], b_v], [bb])
                    if grp_last:
                        t = 0 if kt == NT - 1 else 1
                        for bk in range(3):
                            w_ = 387 if bk < 2 else 258
                            S.op(DVE, lambda e, bk=bk, w_=w_: e.tensor_copy(out=Oul[:, t, bk, 0:w_], in_=pc[:, bk * 512:bk * 512 + w_]),
                                 [b_pc3[bk]], [b_O[t]])

                steps = [(h, r, kt) for h in range(NH) for r in (range(8) if h % 2 == 0 else range(7, -1, -1)) for kt in range(NT)]
                last_inr = {h: ((7, 31) if h % 2 == 0 else (0, 3)) for h in range(NH)}
                gen_tables(0)
                qk(0, 0, 0, 0)
                qk(0, 0, 1, 1)
                for si, (h, r, kt) in enumerate(steps):
                    sc, es = si % 2, si % 3
                    act_(h, r, kt, sc, es)
                    if si + 2 < len(steps):
                        h2_, r2_, kt2_ = steps[si + 2]
                        qk(h2_, r2_, kt2_, sc)
                        if (r2_, kt2_) == last_inr[h2_] and h2_ + 1 < NH:
                            gen_tables(h2_ + 1)
                    av_(h, r, kt, es)
                    if pend[0] is not None:
                        if kt == 1:
                            comb1(*pend[0])
                        elif kt == 22:
                            comb2(*pend[0])
                        elif kt == 26:
                            comb3(*pend[0])
                            pend[0] = None
                    if kt == NT - 1:
                        pend[0] = (h, r)
                comb1(*pend[0]); comb2(*pend[0]); comb3(*pend[0])
                S.barrier()
                if debug and "Oul" in debug:
                    for nm, tt_, bb_ in (("Oul", Oul, b_O[0]), ("num", num, b_num), ("o_t", o_t, b_o), ("ssb", ssb, b_ss)):
                        dd = S.dsem("dbg" + nm)
                        S.dma(SP, dbg[nm], tt_[:], [bb_], [], dd); out_sems.append(dd)
                if debug and "mixda" in debug:
                    dd = S.dsem("dbgB")
                    S.dma(SP, dbg["mixda"], mixT[:, 0:4, :], b_mix[0:4], [], dd); out_sems.append(dd)
                    S.barrier()
                S.flush()
            S.st = phab
            if debug and debug.get("_stop") == "B":
                S.barrier(); S.flush(out_sems)
                return nc
        S.st = st
        S.barrier()
        Wdn = S.sb("Wdn", [128, NCH, D], BF16)
        b_Wdn = S.buf("Wdn")
        with ExitStack() as phd:
            S.st = phd
            Z = S.sb("Z", [128, NT, 4, 256], BF16)
            stgD = [S.sb("stgD%d" % i, [128, D], F32) for i in range(2)]
            b_stgD = [S.buf(), S.buf()]; d_stgD = [S.dsem("stgD0"), S.dsem("stgD1")]
            WC = S.sb("WC", [128, 256], BF16)
            ci = S.sb("ci", [128, 128], I32)
            ci2 = S.sb("ci2", [128, 128], I32)
            pcol = S.sb("pcol", [128, 1], F32)
            kcol_i = S.sb("kcol_i", [128, NT], I32)
            kcol = S.sb("kcol", [128, NT], F32)
            jr_i = S.sb("jr_i", [128, 512], I32)
            Pi = [S.sb("Pi%d" % i, [128, 512], I32) for i in range(2)]
            Ci = [S.sb("Ci%d" % i, [128, 512], I32) for i in range(2)]
            tab = [S.sb("tab%d" % i, [128, 2, 512], BF16) for i in range(3)]
            Fsb = S.sb("Fsb", [128, 4, 512], BF16)
            b_Z, b_WC, b_k, b_jr, b_F = S.buf(), S.buf(), S.buf(), S.buf(), S.buf()
            b_Pi = [S.buf(), S.buf()]; b_Ci = [S.buf(), S.buf()]; b_tab = [S.buf(), S.buf(), S.buf()]
            SC128 = 2.0 * PI_IN / 128.0
            SC4096 = 2.0 * PI_IN / 4096.0
            S.op(POOL, lambda e: e.iota(ci[:], pattern=[[0, 128]], base=0, channel_multiplier=1), [], [b_WC])
            S.op(DVE, lambda e: e.tensor_copy(out=pcol[:], in_=ci[:, 0:1]), [b_WC], [b_WC])
            S.op(POOL, lambda e: e.iota(ci[:], pattern=[[1, 128]], base=0, channel_multiplier=0), [b_WC], [b_WC])
            S.op(DVE, lambda e: e.tensor_scalar(out=ci[:], in0=ci[:], scalar1=pcol[:, 0:1], scalar2=None, op0=ALU.mult), [b_WC], [b_WC])
            for half, add in ((0, 96), (1, 64)):
                S.op(DVE, lambda e, add=add: e.tensor_single_scalar(out=ci2[:], in_=ci[:], scalar=add, op=ALU.add), [b_WC], [b_WC])
                S.op(DVE, lambda e: e.tensor_single_scalar(out=ci2[:], in_=ci2[:], scalar=127, op=ALU.bitwise_and), [b_WC], [b_WC])
                S.op(ACT, lambda e, half=half: e.activation(out=WC[:, half * 128:(half + 1) * 128], in_=ci2[:], func=AF.Sin, scale=SC128, bias=nbias[:, 0:1]), [b_WC], [b_WC])
            S.op(POOL, lambda e: e.iota(kcol_i[:], pattern=[[128, NT]], base=0, channel_multiplier=1), [], [b_k])
            S.op(DVE, lambda e: e.tensor_copy(out=kcol[:], in_=kcol_i[:]), [b_k], [b_k])
            for t in range(NT):
                pz, bpz = (pa, b_pa2) if t % 2 == 0 else (pb, b_pb2)
                for g in range(4):
                    S.op(PE, lambda e, t=t, g=g, pz=pz: e.matmul(pz[:, g * 256:(g + 1) * 256], lhsT=mixT[:, 4 + g, t * 128:(t + 1) * 128], rhs=WC[:], start=(g % 2 == 0), stop=(g % 2 == 1)),
                         [b_mix[4 + g], b_WC], [bpz[g // 2]], inc=(g == 3))
                if t % 2 == 0:
                    S.op(ACT, lambda e, t=t, pz=pz: e.activation(out=Z[:, t, :, :].rearrange("p g n -> p (g n)"), in_=pz[:], func=AF.Copy), bpz, [b_Z])
                else:
                    S.op(DVE, lambda e, t=t, pz=pz: e.tensor_copy(out=Z[:, t, :, :].rearrange("p g n -> p (g n)"), in_=pz[:]), bpz, [b_Z])
            S.barrier()
            accD = [(pa, 0, b_pa2[0]), (pa, 512, b_pa2[1]), (pb, 0, b_pb2[0]), (pb, 512, b_pb2[1])]
            cnt = 0
            rotw = 0
            wdn_i = 0
            for jr in range(8):
                S.op(POOL, lambda e, jr=jr: e.iota(jr_i[:], pattern=[[1, 512]], base=jr * 512, channel_multiplier=0), [], [b_jr])
                for kt in range(NT):
                    s2, s3 = cnt % 2, cnt % 3
                    cnt += 1
                    if kt % 4 == 0 and wdn_i < NCH:
                        q_ = wdn_i % 2
                        S.dma(SP, stgD[q_][:], wdn_d[wdn_i * 128:(wdn_i + 1) * 128, :], [], [b_stgD[q_]], d_stgD[q_])
                        S.op(POOL, lambda e, q_=q_, wi=wdn_i: e.tensor_copy(out=Wdn[:, wi, :], in_=stgD[q_][:]), [b_stgD[q_]], [b_Wdn])
                        wdn_i += 1
                    S.op(DVE, lambda e, kt=kt, s2=s2: e.tensor_scalar(out=Pi[s2][:], in0=jr_i[:], scalar1=kcol[:, kt:kt + 1], scalar2=None, op0=ALU.mult), [b_jr, b_k], [b_Pi[s2]])
                    S.op(DVE, lambda e, s2=s2: e.tensor_single_scalar(out=Pi[s2][:], in_=Pi[s2][:], scalar=4095, op=ALU.bitwise_and), [b_Pi[s2]], [b_Pi[s2]])
                    S.op(DVE, lambda e, s2=s2: e.tensor_single_scalar(out=Ci[s2][:], in_=Pi[s2][:], scalar=3072, op=ALU.add), [b_Pi[s2]], [b_Ci[s2]])
                    S.op(DVE, lambda e, s2=s2: e.tensor_single_scalar(out=Ci[s2][:], in_=Ci[s2][:], scalar=4095, op=ALU.bitwise_and), [b_Ci[s2]], [b_Ci[s2]])
                    S.op(ACT, lambda e, s2=s2, s3=s3: e.activation(out=tab[s3][:, 0, :], in_=Ci[s2][:], func=AF.Sin, scale=SC4096, bias=nbias[:, 0:1]), [b_Ci[s2]], [b_tab[s3]])
                    S.op(ACT, lambda e, s2=s2, s3=s3: e.activation(out=tab[s3][:, 1, :], in_=Pi[s2][:], func=AF.Sin, scale=SC4096, bias=nbias[:, 0:1]), [b_Pi[s2]], [b_tab[s3]])
                    for g in range(4):
                        pp, off, bb = accD[g]
                        S.op(PE, lambda e, kt=kt, g=g, pp=pp, off=off, s3=s3: e.matmul(pp[:, off:off + 512], lhsT=Z[:, kt, g, 0:128], rhs=tab[s3][:, 0, :], start=(kt == 0), stop=False),
                             [b_Z, b_tab[s3]], [bb], inc=False)
                        S.op(PE, lambda e, kt=kt, g=g, pp=pp, off=off, s3=s3: e.matmul(pp[:, off:off + 512], lhsT=Z[:, kt, g, 128:256], rhs=tab[s3][:, 1, :], start=False, stop=(kt == NT - 1)),
                             [b_Z, b_tab[s3]], [bb], inc=(kt == NT - 1 or g == 3))
                for g in range(4):
                    pp, off, bb = accD[g]
                    S.op(DVE, lambda e, g=g, pp=pp, off=off: e.tensor_scalar(out=Fsb[:, g, :], in0=pp[:, off:off + 512], scalar1=FT_NORM, scalar2=None, op0=ALU.mult), [bb], [b_F])
                for g in range(4):
                    bank = rotw % 3; rotw += 1
                    S.op(PE, lambda e, g=g, bank=bank: e.matmul(pc[:, bank * 512:(bank + 1) * 512], lhsT=wft[:, g, :], rhs=Fsb[:, g, :], start=True, stop=True), [b_F, b_const], [b_pc3[bank]])
                    S.op(DVE, lambda e, g=g, bank=bank, jr=jr: e.tensor_copy(out=mixT[:, 4 + g, jr * 512:(jr + 1) * 512], in_=pc[:, bank * 512:(bank + 1) * 512]), [b_pc3[bank]], [b_mix[4 + g]])
            S.barrier()
            if debug and "mixft" in debug:
                dd = S.dsem("dbgD")
                S.dma(SP, dbg["mixft"], mixT[:, 4:8, :], b_mix[4:8], [], dd); out_sems.append(dd)
                S.barrier()
            S.flush()
        S.st = st
        if debug and debug.get("_stop") == "D":
            S.barrier(); S.flush(out_sems)
            return nc
        S.barrier()
        with ExitStack() as phf:
            S.st = phf
            NX1 = 5
            Wout = S.sb("Wout", [128, KC, D], BF16)
            gfin = S.sb("gfin", [128, D], F32)
            x1 = S.sb("x1", [128, NX1, D], F32)
            xn2 = [S.sb("xn2_%d" % i, [128, D], BF16) for i in range(2)]
            h2T = S.sb("h2T", [128, KC, 514], BF16)
            aT = S.sb("aT", [128, NCH, 512], BF16)
            wup = [S.sb("wup%d" % i, [128, 2, KC, 128], BF16) for i in range(3)]
            tgv = [S.sb("tgv%d" % i, [128, 2, 256], F32) for i in range(3)]
            junkF2 = S.sb("junkF2", [128, D], BF16)
            ssE = S.sb("ssE", [128, 4], F32)
            ssF = S.sb("ssF", [128, 8], F32)
            hr = S.sb("hr", [128, KC, 8], BF16)
            b_hr = S.buf()
            b_Wout, b_gf, b_h2T = (S.buf() for _ in range(3))
            b_xn2 = [S.buf(), S.buf()]
            b_ssF = [S.buf() for _ in range(8)]
            b_x1 = [S.buf() for _ in range(NX1)]
            b_aT = [S.buf(), S.buf()]
            b_wup = [S.buf(), S.buf(), S.buf()]; b_tgv = [S.buf(), S.buf(), S.buf()]; b_junk2 = S.buf(); b_ssE = [S.buf() for _ in range(4)]
            d_gf, d_xt, d_stg = S.dsem("gf"), S.dsem("xtF"), S.dsem("stgF")
            d_wup = [S.dsem("wup0"), S.dsem("wup1"), S.dsem("wup2")]
            d_y = [S.dsem("y%d" % i) for i in range(NX1)]
            out_sems.extend(d_y)
            S.dma(SP, gfin[:], gfin_d.partition_broadcast(128), [], [b_gf], d_gf)
            b_stg = [S.buf(), S.buf()]; d_stg2 = [S.dsem("stgF0"), S.dsem("stgF1")]
            wl = 0
            for kc in range(KC):
                s_ = wl % 2; wl += 1
                S.dma(SP if s_ == 0 else POOL, x1[:, s_, :], wout_d[kc * 128:(kc + 1) * 128, :], [], [b_stg[s_]], d_stg2[s_])
                S.op(DVE if s_ == 0 else ACT, (lambda e, kc=kc, s_=s_: e.tensor_copy(out=Wout[:, kc, :], in_=x1[:, s_, :])) if s_ == 0 else
                     (lambda e, kc=kc, s_=s_: e.activation(out=Wout[:, kc, :], in_=x1[:, s_, :], func=AF.Copy)), [b_stg[s_]], [b_Wout])
            for s_ in range(2):
                b_x1[s_].w = b_stg[s_].w; b_x1[s_].r = list(b_stg[s_].r)

            fcount = [0]

            def frontA(tok0, n, x1dst, b_dst, step=1):
                k = fcount[0]; fcount[0] += 1
                tsl = slice(tok0, tok0 + n * step, step)
                sl_, sc_ = k % 2, k % 8
                pp, bpp = (pa, b_pa2) if k % 2 == 0 else (pb, b_pb2)
                S.dma(SP, x1dst, x_d[tsl, :], [], [b_dst], d_xt)
                for half in range(2):
                    for kc in range(KC):
                        S.op(PE, lambda e, half=half, kc=kc: e.matmul(pp[0:n, half * 512:(half + 1) * 512], lhsT=mixT[:, kc, tsl], rhs=Wout[:, kc, half * 512:(half + 1) * 512], start=(kc == 0), stop=(kc == KC - 1)),
                             b_mix + [b_Wout], [bpp[half]], inc=(kc == KC - 1))
                S.op(DVE, lambda e: e.tensor_tensor(out=x1dst, in0=x1dst, in1=pp[0:n, :], op=ALU.add), [b_dst] + bpp, [b_dst])
                S.op(ACT, lambda e: e.activation(out=xn2[sl_][0:n, :], in_=x1dst, func=AF.Square, accum_out=ssF[0:n, sc_:sc_ + 1]), [b_dst], [b_xn2[sl_], b_ssF[sc_]])
                emit_rstd(ssF[0:n, sc_:sc_ + 1], ssF[0:n, sc_:sc_ + 1], D, EPS, [b_ssF[sc_]])
                S.op(DVE, lambda e: e.tensor_scalar(out=xn2[sl_][0:n, :], in0=x1dst, scalar1=ssF[0:n, sc_:sc_ + 1], scalar2=None, op0=ALU.mult), [b_dst, b_ssF[sc_]], [b_xn2[sl_]])
                return sl_

            def frontB(sl_, n, col, dstT=None, b_dT=None):
                dstT = h2T if dstT is None else dstT
                b_dT = b_h2T if b_dT is None else b_dT
                for kc in range(KC):
                    S.op(PE, lambda e, kc=kc: e.transpose(pt[:, kc * 128:kc * 128 + n], xn2[sl_][0:n, kc * 128:(kc + 1) * 128], ident[0:n, 0:n]), [b_xn2[sl_], b_const], [b_pt], inc=(kc == KC - 1))
                S.op(ACT, lambda e: e.activation(out=dstT[:, :, col:col + n], in_=pt[:].rearrange("p (k m) -> p k m", k=KC)[:, :, 0:n], func=AF.Copy), [b_pt], [b_dT])

            def x1slot(g):
                return x1[:, g % NX1, :], b_x1[g % NX1]

            S.op(DVE, lambda e: e.memset(h2T[:, :, 0:1], 0.0), [], [b_h2T])
            for j in range(4):
                xs, bx = x1slot(j)
                frontB(frontA(j * 128, 128, xs, bx), 128, 1 + j * 128)
            frontB(frontA(512, 7, x1[0:7, 4, :], b_x1[4], step=512), 7, 0, hr, b_hr)
            S.op(DVE, lambda e: e.tensor_copy(out=h2T[:, :, 513:514], in_=hr[:, :, 0:1]), [b_hr, b_h2T], [b_h2T])

            cw = [0]
            cs = [0]
            for b in range(8):
                t0 = b * 512
                prev = None
                units = [(i, half) for i in range(NCH) for half in range(2)]
                for unit in units + [None]:
                    cur = None
                    if unit is not None:
                        i, half = unit
                        if half == 0:
                            n_ = cw[0]; cw[0] += 1
                            ws = n_ % 3
                            if n_ == 0:
                                for q_ in range(2):
                                    S.dma(SP, wup[q_][:].rearrange("p a k c -> p (a k c)"), wup_s[q_, :, :], [], [b_wup[q_]], d_wup[q_])
                            if n_ + 2 < 8 * NCH:
                                q_ = (n_ + 2) % 3
                                S.dma(SP, wup[q_][:].rearrange("p a k c -> p (a k c)"), wup_s[(n_ + 2) % NCH, :, :], [], [b_wup[q_]], d_wup[q_])
                        c0 = half * 256
                        u = cs[0]; cs[0] += 1
                        s_, s3 = u % 2, u % 3
                        pp, bpp = (pa, b_pa2) if s_ == 0 else (pb, b_pb2)
                        for gv in range(2):
                            for kc in range(KC):
                                S.op(PE, lambda e, gv=gv, kc=kc, ws=ws, c0=c0, pp=pp: e.matmul(pp[:, gv * 512:gv * 512 + 258], lhsT=wup[ws][:, gv, kc, :], rhs=h2T[:, kc, c0:c0 + 258], start=(kc == 0), stop=(kc == KC - 1)),
                                     [b_wup[ws], b_h2T], [bpp[gv]], inc=(kc == KC - 1))
                        for gv in range(2):
                            ch = i + gv * NCH
                            S.op(ACT, lambda e, gv=gv, ch=ch, s3=s3, pp=pp: e.activation(out=tgv[s3][:, gv, :], in_=pp[:, gv * 512 + 1:gv * 512 + 257], func=AF.Identity, scale=wcv[:, 1, ch:ch + 1], bias=wcv[:, 3, ch:ch + 1]),
                                 [bpp[gv], b_const], [b_tgv[s3]])
                        for gv in range(2):
                            ch = i + gv * NCH
                            S.op(DVE, lambda e, gv=gv, ch=ch, s3=s3, pp=pp: e.scalar_tensor_tensor(out=tgv[s3][:, gv, :], in0=pp[:, gv * 512:gv * 512 + 256], scalar=wcv[:, 0, ch:ch + 1], in1=tgv[s3][:, gv, :], op0=ALU.mult, op1=ALU.add),
                                 [bpp[gv], b_const, b_tgv[s3]], [b_tgv[s3]])
                            S.op(DVE, lambda e, gv=gv, ch=ch, s3=s3, pp=pp: e.scalar_tensor_tensor(out=tgv[s3][:, gv, :], in0=pp[:, gv * 512 + 2:gv * 512 + 258], scalar=wcv[:, 2, ch:ch + 1], in1=tgv[s3][:, gv, :], op0=ALU.mult, op1=ALU.add),
                                 [bpp[gv], b_const, b_tgv[s3]], [b_tgv[s3]])
                        cur = (i, half, s3, s_)
                    if prev is not None:
                        pi_, ph_, p3, p2 = prev
                        S.op(ACT, lambda e, p3=p3: e.activation(out=tgv[p3][:, 0, :], in_=tgv[p3][:, 0, :], func=AF.Silu), [b_tgv[p3]], [b_tgv[p3]])
                        S.op(POOL, lambda e, p3=p3, pi_=pi_, ph_=ph_: e.tensor_tensor(out=aT[:, pi_, ph_ * 256:ph_ * 256 + 256], in0=tgv[p3][:, 0, :], in1=tgv[p3][:, 1, :], op=ALU.mult),
                             [b_tgv[p3]], [b_aT[ph_]])
                    prev = cur
                if b < 7:
                    S.op(DVE, lambda e: e.tensor_copy(out=h2T[:, :, 0:1], in_=h2T[:, :, 512:513]), [b_h2T], [b_h2T])
                    if b < 6:
                        S.op(DVE, lambda e, b=b: e.tensor_copy(out=h2T[:, :, 513:514], in_=hr[:, :, b + 1:b + 2]), [b_hr, b_h2T], [b_h2T])
                    else:
                        S.op(DVE, lambda e: e.memset(h2T[:, :, 513:514], 0.0), [b_h2T], [b_h2T])
                pendB = None
                for j in range(4):
                    g = 4 * b + j
                    xs, bx = x1slot(g)
                    for half in range(2):
                        for i in range(NCH):
                            S.op(PE, lambda e, j=j, half=half, i=i: e.matmul(pc[:, half * 512:(half + 1) * 512], lhsT=aT[:, i, j * 128:(j + 1) * 128], rhs=Wdn[:, i, half * 512:(half + 1) * 512], start=(i == 0), stop=(i == NCH - 1)),
                                 [b_aT[j // 2], b_Wdn], [b_pc3[half]], inc=(i == NCH - 1))
                    S.op(DVE, lambda e, xs=xs: e.tensor_tensor(out=xs, in0=xs, in1=pc[:, 0:1024], op=ALU.add), [b_pc3[0], b_pc3[1], bx], [bx])
                    if b < 7:
                        xs2, bx2 = x1slot(g + 4)
                        slA = frontA(t0 + 512 + j * 128, 128, xs2, bx2)
                    if pendB is not None:
                        frontB(*pendB)
                    pendB = (slA, 128, 1 + j * 128) if b < 7 else None
                    S.op(ACT, lambda e, xs=xs, j=j: e.activation(out=junkF2[:], in_=xs, func=AF.Square, accum_out=ssE[:, j:j + 1]), [bx], [b_junk2, b_ssE[j]])
                    emit_rstd(ssE[:, j:j + 1], ssE[:, j:j + 1], D, EPS, [b_ssE[j]])
                    S.op(DVE, lambda e, xs=xs, j=j: e.scalar_tensor_tensor(out=xs, in0=xs, scalar=ssE[:, j:j + 1], in1=gfin[:], op0=ALU.mult, op1=ALU.mult), [bx, b_ssE[j], b_gf], [bx])
                    S.dma(SP, y_d[t0 + j * 128:t0 + (j + 1) * 128, :], xs, [bx], [], d_y[g % NX1])
                if pendB is not None:
                    frontB(*pendB)
            S.barrier()
            S.flush(out_sems)
        S.st = st
    return nc


def kernel(**inputs):
    nc = build_program()
    x = np.asarray(inputs["x"], dtype=np.float32)
    B = x.shape[0]
    shared = {}
    for k, v in inputs.items():
        if k == "x":
            continue
        a = np.asarray(v, dtype=np.float32)
        shared[k] = np.ascontiguousarray(a[0] if k != "g_final" else a)
    in_maps = []
    for b in range(B):
        m = dict(shared)
        m["x"] = np.ascontiguousarray(x[b])
        in_maps.append(m)
    res = run_bass_kernel_spmd(nc, in_maps, core_ids=list(range(B)))
    return np.stack([np.asarray(r["y"], dtype=np.float32) for r in res.results], axis=0)
```

```python
import math
from contextlib import ExitStack
import numpy as np
import concourse.bass as bass
import concourse.mybir as mybir
from concourse.bass_utils import run_bass_kernel_spmd

F32 = mybir.dt.float32
BF16 = mybir.dt.bfloat16
F16 = mybir.dt.float16
I32 = mybir.dt.int32
AF = mybir.ActivationFunctionType
ALU = mybir.AluOpType
AX = mybir.AxisListType


class _Res:
    def __init__(self, name, sem, attr=None):
        self.name, self.sem, self.attr = name, sem, attr
        self.count = 0
        self.pos = 0
        self.flushed = 0
        self.prog = []
        self.seen = {}


class _Buf:
    def __init__(self, name):
        self.name = name
        self.w = None
        self.r = []


class Sched:
    def __init__(self, nc, stack):
        self.nc, self.st = nc, stack
        mk = lambda n, a: _Res(n, stack.enter_context(nc.semaphore("s_" + n)), a)
        self.PE, self.ACT, self.DVE = mk("pe", "tensor"), mk("act", "scalar"), mk("dve", "vector")
        self.POOL, self.SP = mk("pool", "gpsimd"), mk("sp", "sync")
        self.engines = [self.PE, self.ACT, self.DVE, self.POOL, self.SP]
        self.n_instr = 0

    def sb(self, name, shape, dt):
        return self.st.enter_context(self.nc.sbuf_tensor(name, shape, dt))

    def ps(self, name, shape, dt):
        return self.st.enter_context(self.nc.psum_tensor(name, shape, dt))

    def buf(self, name="b"):
        return _Buf(name)

    def dsem(self, name):
        return _Res(name, self.st.enter_context(self.nc.semaphore("d_" + name)))

    def _waits(self, E, reads, writes, extra=()):
        deps = {}
        def add(tok):
            if tok is None:
                return
            r, v = tok
            if r is E and E is self.PE:
                return
            if deps.get(r, 0) < v:
                deps[r] = v
        for b in reads:
            add(b.w)
        for b in writes:
            add(b.w)
            for t in b.r:
                add(t)
        for t in extra:
            add(t)
        for r, v in deps.items():
            if E.seen.get(r, 0) < v:
                E.seen[r] = v
                E.prog.append(("w", r, v))

    def _mark(self, tok, reads, writes):
        for b in writes:
            b.w = tok
            b.r = []
        for b in reads:
            if b not in writes:
                b.r.append(tok)
        self.n_instr += 1

    def op(self, E, fn, reads, writes, inc=True):
        self._waits(E, reads, writes)
        E.pos += 1
        tok = (E, E.pos)
        E.prog.append(("i", fn, E.pos))
        self._mark(tok, reads, writes)
        return tok

    def dma(self, E, out, in_, reads, writes, ds, **kw):
        prev = (ds, ds.count) if ds.count else None
        self._waits(E, reads, writes, extra=(prev,) if prev else ())
        ds.count += 16
        tok = (ds, ds.count)
        E.prog.append(("d", lambda e: e.dma_start(out=out, in_=in_, **kw), ds))
        self._mark(tok, reads, writes)
        return tok

    def barrier(self):
        for E in self.engines:
            for R in self.engines:
                if R is self.SP or (R is E and E is self.PE) or R.pos == 0:
                    continue
                if R.prog and R.prog[-1][0] == "d":
                    pass
                if E.seen.get(R, 0) < R.pos:
                    E.seen[R] = R.pos
                    E.prog.append(("w", R, R.pos))

    def flush(self, final_dsems=()):
        self.barrier()
        for ds in final_dsems:
            if ds.count:
                self.SP.prog.append(("w", ds, ds.count))
        nc = self.nc
        comp = [E for E in self.engines]
        needed = {E: set() for E in comp}
        for E in comp:
            for it in E.prog:
                if it[0] == "w" and it[1] in needed and it[2] > it[1].flushed:
                    needed[it[1]].add(it[2])
        rank = {}
        for E in comp:
            for k, p in enumerate(sorted(needed[E])):
                rank[(E, p)] = E.count + k + 1
        progs = {}
        for E in comp:
            out = []
            for it in E.prog:
                if it[0] == "w":
                    r, v = it[1], it[2]
                    if r in needed:
                        if v <= r.flushed:
                            continue
                        out.append(("w", r.sem, rank[(r, v)]))
                    else:
                        out.append(("w", r.sem, v))
                elif it[0] == "i":
                    out.append(("i", it[1], E.sem if (E, it[2]) in rank else None, 1))
                else:
                    out.append(("i", it[1], it[2].sem, 16))
            progs[E.attr] = out
        for E in comp:
            E.count += len(needed[E])
            E.flushed = E.pos
            E.prog = []

        def runner(prog):
            def _(eng):
                for it in prog:
                    if it[0] == "w":
                        eng.wait_ge(it[1], it[2])
                    else:
                        ins = it[1](eng)
                        if it[2] is not None:
                            ins.then_inc(it[2], it[3])
            return _
        with nc.Block() as block:
            block.tensor(runner(progs["tensor"]))
            block.scalar(runner(progs["scalar"]))
            block.vector(runner(progs["vector"]))
            block.gpsimd(runner(progs["gpsimd"]))
            block.sync(runner(progs["sync"]))

    def finish(self, dsems):
        self.flush(dsems)


S_LEN = 4096
D = 1024
NT = S_LEN // 128
KC = D // 128
NH = 4
DFF = 2816
NCH = DFF // 128
EPS = 1e-6
SUBLN_EPS = 1e-5
LAMBDA_INIT = 0.8 - 0.6 * math.exp(0.0)
SLOPES = [2.0 ** (-8.0 * (h + 1) / NH) for h in range(NH)]
PI_IN = 3.1415925
FT_NORM = 1.0 / math.sqrt(4096.0 * 128.0)


def build_program(debug=None):
    nc = bass.Bass("TRN2", target_bir_lowering=False)
    dr = lambda n, shp, dt=F32, kind="ExternalInput": nc.dram_tensor(n, shp, dt, kind=kind).ap()
    x_d = dr("x", [S_LEN, D])
    gmix_d = dr("g_mix", [D]); win_d = dr("w_in", [D, 2048])
    lq1_d, lk1_d, lq2_d, lk2_d = (dr(n, [64]) for n in ("lambda_q1", "lambda_k1", "lambda_q2", "lambda_k2"))
    gsub_d = dr("g_subln", [128]); wft_d = dr("w_ft", [4, 128, 128]); wout_d = dr("w_out", [D, D])
    gffn_d = dr("g_ffn", [D]); wup_d = dr("w_up", [D, 2 * DFF]); wconv_d = dr("w_conv", [3, 2 * DFF])
    bconv_d = dr("b_conv", [2 * DFF]); wdn_d = dr("w_down", [DFF, D]); gfin_d = dr("g_final", [D])
    y_d = dr("y", [S_LEN, D], F32, "ExternalOutput")
    wup_s = nc.dram_tensor("wup_s", [NCH, 128, 2 * KC * 128], BF16).ap()
    dbg = {}
    if debug:
        for n, (shp, dt) in debug.items():
            dbg[n] = dr("dbg_" + n, shp, dt, "ExternalOutput")

    with ExitStack() as st:
        S = Sched(nc, st)
        PE, ACT, DVE, POOL, SP = S.PE, S.ACT, S.DVE, S.POOL, S.SP
        out_sems = []

        mixT = S.sb("mixT", [128, 8, S_LEN], BF16)
        b_mix = [S.buf("mix%d" % i) for i in range(8)]
        ident = S.sb("ident", [128, 128], BF16)
        iota_i = S.sb("iota_i", [128, 128], I32)
        gsub = S.sb("gsub", [128, 128], F32)
        wcv = S.sb("wcv", [128, 4, 2 * NCH], F32)
        gcol = S.sb("gcol", [128, 2, KC], F32)
        lamt = S.sb("lamt", [128, 4, 64], F32)
        lams = S.sb("lams", [128, 8], F32)
        wft = S.sb("wft", [128, 4, 128], BF16)
        nbias = S.sb("nbias", [128, 1], F32)
        b_const = S.buf("const")
        pt = S.ps("pt", [128, 1024], BF16)
        pa = S.ps("pa", [128, 1024], F32)
        pb = S.ps("pb", [128, 1024], F32)
        pc = S.ps("pc", [128, 1536], F32)
        b_pt, b_pa, b_pb, b_pc = S.buf("pt"), S.buf("pa"), S.buf("pb"), S.buf("pc")
        b_pa2 = [S.buf("pa0"), S.buf("pa1")]
        b_pb2 = [S.buf("pb0"), S.buf("pb1")]
        b_pc3 = [S.buf("pc0"), S.buf("pc1"), S.buf("pc2")]
        dc = [S.dsem("c%d" % i) for i in range(4)]

        S.op(DVE, lambda e: e.memset(nbias[:], -PI_IN), [], [b_const])
        S.op(POOL, lambda e: e.iota(iota_i[:], pattern=[[1, 128]], base=0, channel_multiplier=-1), [], [b_const])
        S.op(DVE, lambda e: e.tensor_single_scalar(out=ident[:], in_=iota_i[:], scalar=0, op=ALU.is_equal), [b_const], [b_const])
        S.dma(SP, gsub[:], gsub_d.partition_broadcast(128), [], [b_const], dc[0])
        S.dma(SP, wcv[:, 0:3, :], wconv_d.rearrange("t (c p) -> p t c", p=128), [], [b_const], dc[2], allow_slow_non_contiguous=True)
        S.dma(SP, wcv[:, 3, :], bconv_d.rearrange("(c p) -> p c", p=128), [], [b_const], dc[3], allow_slow_non_contiguous=True)
        S.dma(SP, gcol[:, 0, :], gmix_d.rearrange("(c p) -> p c", p=128), [], [b_const], dc[0], allow_slow_non_contiguous=True)
        S.dma(SP, gcol[:, 1, :], gffn_d.rearrange("(c p) -> p c", p=128), [], [b_const], dc[1], allow_slow_non_contiguous=True)
        for i, ld in enumerate((lq1_d, lk1_d, lq2_d, lk2_d)):
            S.dma(SP, lamt[:, i, :], ld.partition_broadcast(128), [], [b_const], dc[2 + i % 2])
        S.op(DVE, lambda e: e.tensor_scalar(out=gsub[:], in0=gsub[:], scalar1=1.0 - LAMBDA_INIT, scalar2=None, op0=ALU.mult), [b_const], [b_const])
        S.op(DVE, lambda e: e.tensor_tensor(out=lamt[:, 0, :], in0=lamt[:, 0, :], in1=lamt[:, 1, :], op=ALU.mult), [b_const], [b_const])
        S.op(DVE, lambda e: e.tensor_tensor(out=lamt[:, 2, :], in0=lamt[:, 2, :], in1=lamt[:, 3, :], op=ALU.mult), [b_const], [b_const])
        S.op(DVE, lambda e: e.reduce_sum(out=lams[:, 0:1], in_=lamt[:, 0, :], axis=AX.X), [b_const], [b_const])
        S.op(DVE, lambda e: e.reduce_sum(out=lams[:, 1:2], in_=lamt[:, 2, :], axis=AX.X), [b_const], [b_const])
        S.op(ACT, lambda e: e.activation(out=lams[:, 2:4], in_=lams[:, 0:2], func=AF.Exp), [b_const], [b_const])
        S.op(DVE, lambda e: e.scalar_tensor_tensor(out=lams[:, 4:5], in0=lams[:, 3:4], scalar=-LAMBDA_INIT, in1=lams[:, 2:3], op0=ALU.add, op1=ALU.subtract), [b_const], [b_const])

        with ExitStack() as ph:
            S.st = ph
            stg = [S.sb("stg%d" % i, [128, 2 * DFF], F32) for i in range(2)]
            wbig = S.sb("wbig", [128, NCH, 2, KC, 128], BF16)
            b_stg = [S.buf(), S.buf()]; b_wbig = S.buf()
            dl = [S.dsem("wl0"), S.dsem("wl1")]; dsv = [S.dsem("ws0"), S.dsem("ws1")]
            S.dma(SP, stg[0][:, 0:512].rearrange("p (g d) -> p g d", g=4), wft_d.rearrange("g c d -> c g d"), [], [b_stg[0]], dl[0])
            S.op(DVE, lambda e: e.tensor_copy(out=wft[:].rearrange("p g d -> p (g d)"), in_=stg[0][:, 0:512]), [b_stg[0]], [b_const])
            win_sb = mixT[:, 0:4, :].rearrange("p a (b n) -> p (a b) n", n=2048)
            for kc in range(KC):
                s_ = kc % 2
                S.dma(SP, stg[s_][:, 0:2048], win_d[kc * 128:(kc + 1) * 128, :], [], [b_stg[s_]], dl[s_])
                if s_ == 0:
                    S.op(DVE, lambda e, kc=kc, s_=s_: e.tensor_scalar(out=win_sb[:, kc, :], in0=stg[s_][:, 0:2048], scalar1=gcol[:, 0, kc:kc + 1], scalar2=None, op0=ALU.mult),
                         [b_stg[s_], b_const], b_mix[0:4])
                else:
                    S.op(ACT, lambda e, kc=kc, s_=s_: e.activation(out=win_sb[:, kc, :], in_=stg[s_][:, 0:2048], func=AF.Copy, scale=gcol[:, 0, kc:kc + 1]),
                         [b_stg[s_], b_const], b_mix[0:4])
            for kc in range(KC):
                s_ = kc % 2
                S.dma(SP, stg[s_][:], wup_d[kc * 128:(kc + 1) * 128, :], [], [b_stg[s_]], dl[s_])
                src = lambda gv, s_=s_: stg[s_][:, gv * DFF:(gv + 1) * DFF].rearrange("p (i ch) -> p i ch", ch=128)
                S.op(DVE, lambda e, kc=kc, s_=s_: e.tensor_scalar(out=wbig[:, :, 0, kc, :], in0=stg[s_][:, 0:DFF].rearrange("p (i ch) -> p i ch", ch=128), scalar1=gcol[:, 1, kc:kc + 1], scalar2=None, op0=ALU.mult),
                     [b_stg[s_], b_const], [b_wbig])
                S.op(ACT, lambda e, kc=kc, s_=s_: e.activation(out=wbig[:, :, 1, kc, :], in_=stg[s_][:, DFF:2 * DFF].rearrange("p (i ch) -> p i ch", ch=128), func=AF.Copy, scale=gcol[:, 1, kc:kc + 1]),
                     [b_stg[s_], b_const], [b_wbig])
            for q in range(2):
                lo, hi = q * (NCH // 2), (q + 1) * (NCH // 2)
                S.dma(SP, wup_s[lo:hi, :, :].rearrange("i p n -> p i n"), wbig[:, lo:hi, :, :, :].rearrange("p i a k c -> p i (a k c)"), [b_wbig], [], dsv[q])
            wups_done = [(dsv[0], dsv[0].count), (dsv[1], dsv[1].count)]
            S.barrier()
            for E in S.engines:
                for r, v in wups_done:
                    E.seen[r] = v
                    E.prog.append(("w", r, v))
            S.flush()
        S.st = st

        def emit_rstd(dst, ss, dim, eps, bufs):
            S.op(DVE, lambda e: e.tensor_scalar(out=dst, in0=ss, scalar1=1.0 / dim, scalar2=eps, op0=ALU.mult, op1=ALU.add), bufs, bufs)
            S.op(ACT, lambda e: e.activation(out=dst, in_=dst, func=AF.Ln), bufs, bufs)
            S.op(ACT, lambda e: e.activation(out=dst, in_=dst, func=AF.Exp, scale=-0.5), bufs, bufs)

        with ExitStack() as phab:
            S.st = phab
            QT = S.sb("QT", [128, NH, S_LEN], BF16)
            KT = S.sb("KT", [128, NH, S_LEN], BF16)
            VA = S.sb("VA", [128, NT, NH, 129], BF16)
            UT = mixT[:, 4:8, :]
            b_qk = S.buf("qk"); b_v = S.buf("v"); b_ut = b_mix[4:8]
            S.op(POOL, lambda e: e.memset(VA[:, :, :, 128:129], 1.0), [], [b_v])
            with ExitStack() as pha:
                S.st = pha
                xt = [S.sb("xt%d" % i, [128, D], F32) for i in range(2)]
                xn = [S.sb("xn%d" % i, [128, D], BF16) for i in range(2)]
                junk = S.sb("junkA", [128, D], BF16)
                ssq = S.sb("ssqA", [128, 2], F32)
                hT = [S.sb("hT%d" % i, [128, KC, 512], BF16) for i in range(2)]
                b_xt = [S.buf(), S.buf()]; b_xn = [S.buf(), S.buf()]; b_hT = [S.buf(), S.buf()]
                b_junk = S.buf(); b_ss = [S.buf(), S.buf()]
                dx = [S.dsem("xa0"), S.dsem("xa1")]
                win_sb = mixT[:, 0:4, :].rearrange("p a (b n) -> p (a b) n", n=2048)
                b_win = b_mix[0:4]
                rot = [0]
                def big_group(sp, oi):
                    hs = sp % 2
                    kind, hh = oi // 4, oi % 4
                    col0 = (0, 512, 1536)[kind] + hh * 128
                    bank = rot[0] % 3; rot[0] += 1
                    for kc in range(KC):
                        S.op(PE, lambda e, kc=kc: e.matmul(pc[:, bank * 512:(bank + 1) * 512], lhsT=win_sb[:, kc, col0:col0 + 128], rhs=hT[hs][:, kc, :], start=(kc == 0), stop=(kc == KC - 1)),
                             [b_hT[hs]] + b_win, [b_pc3[bank]])
                    dst = (QT, KT, None)[kind]
                    if kind < 2:
                        if oi % 2 == 0:
                            S.op(ACT, lambda e: e.activation(out=dst[:, hh, sp * 512:(sp + 1) * 512], in_=pc[:, bank * 512:(bank + 1) * 512], func=AF.Copy),
                                 [b_pc3[bank]], [b_qk])
                        else:
                            S.op(DVE, lambda e: e.tensor_copy(out=dst[:, hh, sp * 512:(sp + 1) * 512], in_=pc[:, bank * 512:(bank + 1) * 512]),
                                 [b_pc3[bank]], [b_qk])
                    else:
                        S.op(DVE, lambda e: e.tensor_copy(out=mixT[:, 4 + hh, sp * 512:(sp + 1) * 512], in_=pc[:, bank * 512:(bank + 1) * 512]),
                             [b_pc3[bank]], [b_mix[4 + hh]])

                def tile_chain(sp, tt):
                    hs = sp % 2
                    t = sp * 4 + tt
                    s_ = t % 2
                    S.dma(SP, xt[s_][:], x_d[t * 128:(t + 1) * 128, :], [], [b_xt[s_]], dx[s_])
                    S.op(ACT, lambda e: e.activation(out=junk[:], in_=xt[s_][:], func=AF.Square, accum_out=ssq[:, s_:s_ + 1]),
                         [b_xt[s_]], [b_junk, b_ss[s_]])
                    emit_rstd(ssq[:, s_:s_ + 1], ssq[:, s_:s_ + 1], D, EPS, [b_ss[s_]])
                    S.op(DVE, lambda e: e.tensor_scalar(out=xn[s_][:], in0=xt[s_][:], scalar1=ssq[:, s_:s_ + 1], scalar2=None, op0=ALU.mult),
                         [b_xt[s_], b_ss[s_]], [b_xn[s_]])
                    for kc in range(KC):
                        S.op(PE, lambda e, kc=kc: e.transpose(pt[:, kc * 128:(kc + 1) * 128], xn[s_][:, kc * 128:(kc + 1) * 128], ident[:]),
                             [b_xn[s_], b_const], [b_pt])
                    S.op(ACT, lambda e: e.activation(out=hT[hs][:, :, tt * 128:(tt + 1) * 128], in_=pt[:].rearrange("p (k n) -> p k n", k=KC), func=AF.Copy),
                         [b_pt], [b_hT[hs]])

                def v_tile(sp, tt):
                    hs = sp % 2
                    t = sp * 4 + tt
                    pv, bpv = (pa, b_pa2) if t % 2 == 0 else (pb, b_pb2)
                    for kc in range(KC):
                        S.op(PE, lambda e, kc=kc: e.matmul(pv[:, 0:512], lhsT=hT[hs][:, kc, tt * 128:(tt + 1) * 128], rhs=win_sb[:, kc, 1024:1536], start=(kc == 0), stop=(kc == KC - 1)),
                             [b_hT[hs]] + b_win, [bpv[0]])
                    S.op(DVE, lambda e: e.tensor_copy(out=VA[:, t, :, 0:128], in_=pv[:, 0:512].rearrange("p (h v) -> p h v", h=NH)),
                         [bpv[0]], [b_v])

                for sp in range(NT // 4 + 1):
                    for tt in range(4):
                        if sp < NT // 4:
                            tile_chain(sp, tt)
                        if sp > 0:
                            for oi in range(tt * 3, tt * 3 + 3):
                                big_group(sp - 1, oi)
                        if sp < NT // 4:
                            v_tile(sp, tt)
                S.barrier()
                if debug and "QT" in debug:
                    dd = S.dsem("dbgA")
                    S.dma(SP, dbg["QT"], QT[:], [b_qk], [], dd); out_sems.append(dd)
                    dd = S.dsem("dbgA2")
                    S.dma(SP, dbg["KT"], KT[:], [b_qk], [], dd); out_sems.append(dd)
                    dd = S.dsem("dbgA3")
                    S.dma(SP, dbg["VA"], VA[:], [b_v], [], dd); out_sems.append(dd)
                    dd = S.dsem("dbgA4")
                    S.dma(SP, dbg["UT"], mixT[:, 4:8, :], b_mix[4:8], [], dd); out_sems.append(dd)
                    S.barrier()
                S.flush()
            S.st = phab
            if debug and debug.get("_stop") == "A":
                S.barrier(); S.flush(out_sems)
                return nc
            with ExitStack() as phb:
                S.st = phb
                Rcol = S.sb("Rcol", [128, 60], I32)
                colb = S.sb("colb", [128, NH, 60], F32)
                R2 = S.sb("R2", [128, 4], I32)
                R2f = S.sb("R2f", [128, 4], F32)
                fpm = S.sb("fpm", [128, NH, 2, 4], F32)
                Qr = S.sb("Qr", [128, 512], F32)
                Ti = S.sb("Ti", [128, 512], I32)
                Tf = S.sb("Tf", [128, 512], F32)
                Bt = S.sb("Bt", [128, 4, 512], F16)
                ident16 = S.sb("ident16", [128, 128], F16)
                Eb = S.sb("Eb", [128, 3, 2, 512], BF16)
                Oul = S.sb("Oul", [128, 2, 3, 387], F32)
                num = S.sb("num", [128, 8, 129], F32)
                rden = S.sb("rden", [128, 8], F32)
                Pn = S.sb("Pn", [128, 8, 128], F32)
                o_t = S.sb("o_t", [128, 4, 128], F32)
                junkB = S.sb("junkB", [128, 128], F32)
                ssb = S.sb("ssb", [128, 4], F32)
                on_t = S.sb("on_t", [128, 4, 128], BF16)
                b_tab = S.buf("tab"); b_Bt = S.buf("Bt"); b_T = S.buf("T")
                b_E = [[S.buf(), S.buf()] for _ in range(3)]
                b_O = [S.buf("Oup"), S.buf("Olo")]
                b_num, b_P, b_o, b_ss, b_on = S.buf(), S.buf(), S.buf(), S.buf(), S.buf()
                zcol_i = 28

                S.op(POOL, lambda e: e.iota(Rcol[:], pattern=[[128, 60]], base=-28 * 128 - 256, channel_multiplier=1), [], [b_tab])
                S.op(POOL, lambda e: e.iota(R2[:], pattern=[[128, 4]], base=-256, channel_multiplier=1), [], [b_tab])
                S.op(POOL, lambda e: e.iota(Ti[:], pattern=[[1, 512]], base=-256, channel_multiplier=0), [], [b_T])
                S.op(DVE, lambda e: e.tensor_copy(out=Qr[:], in_=Ti[:]), [b_T], [b_tab, b_T])
                S.op(DVE, lambda e: e.tensor_copy(out=R2f[:], in_=R2[:]), [b_tab], [b_tab])
                S.op(DVE, lambda e: e.tensor_copy(out=ident16[:], in_=ident[:]), [b_const], [b_tab])
                for h in range(NH):
                    sl = SLOPES[h]
                    S.op(DVE, lambda e, h=h, sl=sl: e.tensor_scalar(out=colb[:, h, 0:28], in0=Rcol[:, 0:28], scalar1=sl, scalar2=None, op0=ALU.mult), [b_tab], [b_tab])
                    S.op(DVE, lambda e, h=h, sl=sl: e.tensor_scalar(out=colb[:, h, 28:60], in0=Rcol[:, 28:60], scalar1=-sl, scalar2=None, op0=ALU.mult), [b_tab], [b_tab])
                    S.op(DVE, lambda e, h=h: e.memset(colb[:, h, 28:32], 0.0), [b_tab], [b_tab])
                    S.op(ACT, lambda e, h=h, sl=sl: e.activation(out=fpm[:, h, 0, :], in_=R2f[:], func=AF.Exp, scale=sl), [b_tab], [b_tab])
                    S.op(ACT, lambda e, h=h, sl=sl: e.activation(out=fpm[:, h, 1, :], in_=R2f[:], func=AF.Exp, scale=-sl), [b_tab], [b_tab])

                def acc_ap(c, j, w=129):
                    idx = c * 4 + j
                    o0 = (idx // 3) * 512 + (idx % 3) * 129
                    return pc[:, o0:o0 + w], b_pc3[idx // 3]

                def reg(t, c, j, lo=0, hi=129):
                    idx = c * 4 + j
                    return Oul[:, t, idx // 3, (idx % 3) * 129 + lo:(idx % 3) * 129 + hi]

                def comb1(h, r):
                    for c in range(2):
                        for j in range(4):
                            idx = c * 4 + j
                            S.op(DVE, lambda e, c=c, j=j, idx=idx: e.tensor_scalar(out=num[:, idx, :], in0=reg(0, c, j), scalar1=fpm[:, h, 0, j:j + 1], scalar2=None, op0=ALU.mult),
                                 [b_O[0], b_tab], [b_num])
                            if r > 0:
                                S.op(DVE, lambda e, c=c, j=j, idx=idx: e.scalar_tensor_tensor(out=num[:, idx, :], in0=reg(1, c, j), scalar=fpm[:, h, 1, j:j + 1], in1=num[:, idx, :], op0=ALU.mult, op1=ALU.add),
                                     [b_O[1], b_tab, b_num], [b_num])
                    S.op(DVE, lambda e: e.reciprocal(out=rden[:], in_=num[:, :, 128]), [b_num], [b_num])
                    for idx in range(8):
                        S.op(DVE, lambda e, idx=idx: e.tensor_scalar(out=Pn[:, idx, :], in0=num[:, idx, 0:128], scalar1=rden[:, idx:idx + 1], scalar2=None, op0=ALU.mult),
                             [b_num], [b_P])
                    S.op(DVE, lambda e: e.scalar_tensor_tensor(out=o_t[:], in0=Pn[:, 4:8, :], scalar=lams[:, 4:5], in1=Pn[:, 0:4, :], op0=ALU.mult, op1=ALU.add),
                         [b_P, b_const], [b_o])
                    for j in range(4):
                        S.op(DVE, lambda e, j=j: e.scalar_tensor_tensor(out=junkB[:], in0=o_t[:, j, :], scalar=1.0, in1=o_t[:, j, :], op0=ALU.mult, op1=ALU.mult, accum_out=ssb[:, j:j + 1]),
                             [b_o], [b_ss])
                    S.op(DVE, lambda e: e.tensor_scalar(out=ssb[:], in0=ssb[:], scalar1=1.0 / 128, scalar2=SUBLN_EPS, op0=ALU.mult, op1=ALU.add), [b_ss], [b_ss])

                def comb2(h, r):
                    S.op(ACT, lambda e: e.activation(out=ssb[:], in_=ssb[:], func=AF.Ln), [b_ss], [b_ss])
                    S.op(ACT, lambda e: e.activation(out=ssb[:], in_=ssb[:], func=AF.Exp, scale=-0.5), [b_ss], [b_ss])

                def comb3(h, r):
                    for j in range(4):
                        S.op(DVE, lambda e, j=j: e.scalar_tensor_tensor(out=on_t[:, j, :], in0=o_t[:, j, :], scalar=ssb[:, j:j + 1], in1=gsub[:], op0=ALU.mult, op1=ALU.mult),
                             [b_o, b_ss, b_const], [b_on])
                    for j in range(4):
                        S.op(PE, lambda e, j=j: e.transpose(pt[:, j * 128:(j + 1) * 128], on_t[:, j, :], ident[:]), [b_on, b_const], [b_pt], inc=(j == 3))
                    S.op(DVE, lambda e: e.tensor_copy(out=mixT[:, h, r * 512:(r + 1) * 512], in_=pt[:, 0:512]), [b_pt], [b_mix[h]])

                pend = [None]
                def gen_tables(h):
                    sl = SLOPES[h]
                    for a in range(4):
                        S.op(POOL, lambda e, a=a: e.iota(Ti[:], pattern=[[-1, 512]], base=128 * a, channel_multiplier=1), [], [b_T])
                        S.op(DVE, lambda e: e.tensor_copy(out=Tf[:], in_=Ti[:]), [b_T], [b_T])
                        S.op(DVE, lambda e: e.scalar_tensor_tensor(out=Tf[:], in0=Tf[:], scalar=-1.0, in1=Tf[:], op0=ALU.mult, op1=ALU.max), [b_T], [b_T])
                        S.op(DVE, lambda e: e.tensor_tensor(out=Tf[:], in0=Tf[:], in1=Qr[:], op=ALU.add), [b_T, b_tab], [b_T])
                        S.op(DVE, lambda e: e.tensor_scalar(out=Tf[:], in0=Tf[:], scalar1=-8.0 * sl, scalar2=None, op0=ALU.mult), [b_T], [b_T])
                        S.op(DVE, lambda e, a=a: e.tensor_copy(out=Bt[:, a, :], in_=Tf[:]), [b_T], [b_Bt])

                def qk(h, r, kt, sc):
                    a = kt - 4 * r
                    inr = 0 <= a <= 3
                    ps, bps = (pa, b_pa2) if sc == 0 else (pb, b_pb2)
                    for c in range(2):
                        S.op(PE, lambda e, c=c: e.matmul(ps[:, c * 512:(c + 1) * 512], lhsT=KT[c * 64:(c + 1) * 64, h, kt * 128:(kt + 1) * 128],
                                                         rhs=QT[c * 64:(c + 1) * 64, h, r * 512:(r + 1) * 512], start=True, stop=not inr),
                             [b_qk], [bps[c]])
                    if inr:
                        for c in range(2):
                            S.op(PE, lambda e, c=c: e.matmul(ps[:, c * 512:(c + 1) * 512], lhsT=ident16[:], rhs=Bt[:, a, :], start=False, stop=True),
                                 [b_Bt, b_tab], [bps[c]])

                def act_(h, r, kt, sc, es):
                    a = kt - 4 * r
                    ps, bps = (pa, b_pa2) if sc == 0 else (pb, b_pb2)
                    S.op(ACT, lambda e: e.activation(out=Eb[:, es, :, :].rearrange("p c n -> p (c n)"), in_=ps[:, 0:1024], func=AF.Exp,
                                                     bias=colb[:, h, a + 28:a + 29], scale=0.125),
                         [bps[0], bps[1], b_tab], [b_E[es][0], b_E[es][1]])

                def av_(h, r, kt, es):
                    grp_first = (kt == 0) or (kt == 4 * r)
                    grp_last = (kt == 4 * r - 1) or (kt == NT - 1)
                    for c in range(2):
                        for j in range(4):
                            ap, bb = acc_ap(c, j)
                            S.op(PE, lambda e, c=c, j=j, ap=ap: e.matmul(ap, lhsT=Eb[:, es, c, j * 128:(j + 1) * 128], rhs=VA[:, kt, h, :], start=(grp_first and (c * 4 + j) % 3 == 0), stop=(grp_last and (c * 4 + j) in (2, 5, 7))),
                                 [b_E[es][c], b_v], [bb])
                    if grp_last:
                        t = 0 if kt == NT - 1 else 1
                        for bk in range(3):
                            w_ = 387 if bk < 2 else 258
                            S.op(DVE, lambda e, bk=bk, w_=w_: e.tensor_copy(out=Oul[:, t, bk, 0:w_], in_=pc[:, bk * 512:bk * 512 + w_]),
                                 [b_pc3[bk]], [b_O[t]])

                steps = [(h, r, kt) for h in range(NH) for r in (range(8) if h % 2 == 0 else range(7, -1, -1)) for kt in range(NT)]
                last_inr = {h: ((7, 31) if h % 2 == 0 else (0, 3)) for h in range(NH)}
                gen_tables(0)
                qk(0, 0, 0, 0)
                qk(0, 0, 1, 1)
                for si, (h, r, kt) in enumerate(steps):
                    sc, es = si % 2, si % 3
                    act_(h, r, kt, sc, es)
                    if si + 2 < len(steps):
                        h2_, r2_, kt2_ = steps[si + 2]
                        qk(h2_, r2_, kt2_, sc)
                        if (r2_, kt2_) == last_inr[h2_] and h2_ + 1 < NH:
                            gen_tables(h2_ + 1)
                    av_(h, r, kt, es)
                    if pend[0] is not None:
                        if kt == 1:
                            comb1(*pend[0])
                        elif kt == 22:
                            comb2(*pend[0])
                        elif kt == 26:
                            comb3(*pend[0])
                            pend[0] = None
                    if kt == NT - 1:
                        pend[0] = (h, r)
                comb1(*pend[0]); comb2(*pend[0]); comb3(*pend[0])
                S.barrier()
                if debug and "Oul" in debug:
                    for nm, tt_, bb_ in (("Oul", Oul, b_O[0]), ("num", num, b_num), ("o_t", o_t, b_o), ("ssb", ssb, b_ss)):
                        dd = S.dsem("dbg" + nm)
                        S.dma(SP, dbg[nm], tt_[:], [bb_], [], dd); out_sems.append(dd)
                if debug and "mixda" in debug:
                    dd = S.dsem("dbgB")
                    S.dma(SP, dbg["mixda"], mixT[:, 0:4, :], b_mix[0:4], [], dd); out_sems.append(dd)
                    S.barrier()
                S.flush()
            S.st = phab
            if debug and debug.get("_stop") == "B":
                S.barrier(); S.flush(out_sems)
                return nc
        S.st = st
        S.barrier()
        Wdn = S.sb("Wdn", [128, NCH, D], BF16)
        b_Wdn = S.buf("Wdn")
        with ExitStack() as phd:
            S.st = phd
            Z = S.sb("Z", [128, NT, 4, 256], BF16)
            stgD = [S.sb("stgD%d" % i, [128, D], F32) for i in range(2)]
            b_stgD = [S.buf(), S.buf()]; d_stgD = [S.dsem("stgD0"), S.dsem("stgD1")]
            WC = S.sb("WC", [128, 256], BF16)
            ci = S.sb("ci", [128, 128], I32)
            ci2 = S.sb("ci2", [128, 128], I32)
            pcol = S.sb("pcol", [128, 1], F32)
            kcol_i = S.sb("kcol_i", [128, NT], I32)
            kcol = S.sb("kcol", [128, NT], F32)
            jr_i = S.sb("jr_i", [128, 512], I32)
            Pi = [S.sb("Pi%d" % i, [128, 512], I32) for i in range(2)]
            Ci = [S.sb("Ci%d" % i, [128, 512], I32) for i in range(2)]
            tab = [S.sb("tab%d" % i, [128, 2, 512], BF16) for i in range(3)]
            Fsb = S.sb("Fsb", [128, 4, 512], BF16)
            b_Z, b_WC, b_k, b_jr, b_F = S.buf(), S.buf(), S.buf(), S.buf(), S.buf()
            b_Pi = [S.buf(), S.buf()]; b_Ci = [S.buf(), S.buf()]; b_tab = [S.buf(), S.buf(), S.buf()]
            SC128 = 2.0 * PI_IN / 128.0
            SC4096 = 2.0 * PI_IN / 4096.0
            S.op(POOL, lambda e: e.iota(ci[:], pattern=[[0, 128]], base=0, channel_multiplier=1), [], [b_WC])
            S.op(DVE, lambda e: e.tensor_copy(out=pcol[:], in_=ci[:, 0:1]), [b_WC], [b_WC])
            S.op(POOL, lambda e: e.iota(ci[:], pattern=[[1, 128]], base=0, channel_multiplier=0), [b_WC], [b_WC])
            S.op(DVE, lambda e: e.tensor_scalar(out=ci[:], in0=ci[:], scalar1=pcol[:, 0:1], scalar2=None, op0=ALU.mult), [b_WC], [b_WC])
            for half, add in ((0, 96), (1, 64)):
                S.op(DVE, lambda e, add=add: e.tensor_single_scalar(out=ci2[:], in_=ci[:], scalar=add, op=ALU.add), [b_WC], [b_WC])
                S.op(DVE, lambda e: e.tensor_single_scalar(out=ci2[:], in_=ci2[:], scalar=127, op=ALU.bitwise_and), [b_WC], [b_WC])
                S.op(ACT, lambda e, half=half: e.activation(out=WC[:, half * 128:(half + 1) * 128], in_=ci2[:], func=AF.Sin, scale=SC128, bias=nbias[:, 0:1]), [b_WC], [b_WC])
            S.op(POOL, lambda e: e.iota(kcol_i[:], pattern=[[128, NT]], base=0, channel_multiplier=1), [], [b_k])
            S.op(DVE, lambda e: e.tensor_copy(out=kcol[:], in_=kcol_i[:]), [b_k], [b_k])
            for t in range(NT):
                pz, bpz = (pa, b_pa2) if t % 2 == 0 else (pb, b_pb2)
                for g in range(4):
                    S.op(PE, lambda e, t=t, g=g, pz=pz: e.matmul(pz[:, g * 256:(g + 1) * 256], lhsT=mixT[:, 4 + g, t * 128:(t + 1) * 128], rhs=WC[:], start=(g % 2 == 0), stop=(g % 2 == 1)),
                         [b_mix[4 + g], b_WC], [bpz[g // 2]], inc=(g == 3))
                if t % 2 == 0:
                    S.op(ACT, lambda e, t=t, pz=pz: e.activation(out=Z[:, t, :, :].rearrange("p g n -> p (g n)"), in_=pz[:], func=AF.Copy), bpz, [b_Z])
                else:
                    S.op(DVE, lambda e, t=t, pz=pz: e.tensor_copy(out=Z[:, t, :, :].rearrange("p g n -> p (g n)"), in_=pz[:]), bpz, [b_Z])
            S.barrier()
            accD = [(pa, 0, b_pa2[0]), (pa, 512, b_pa2[1]), (pb, 0, b_pb2[0]), (pb, 512, b_pb2[1])]
            cnt = 0
            rotw = 0
            wdn_i = 0
            for jr in range(8):
                S.op(POOL, lambda e, jr=jr: e.iota(jr_i[:], pattern=[[1, 512]], base=jr * 512, channel_multiplier=0), [], [b_jr])
                for kt in range(NT):
                    s2, s3 = cnt % 2, cnt % 3
                    cnt += 1
                    if kt % 4 == 0 and wdn_i < NCH:
                        q_ = wdn_i % 2
                        S.dma(SP, stgD[q_][:], wdn_d[wdn_i * 128:(wdn_i + 1) * 128, :], [], [b_stgD[q_]], d_stgD[q_])
                        S.op(POOL, lambda e, q_=q_, wi=wdn_i: e.tensor_copy(out=Wdn[:, wi, :], in_=stgD[q_][:]), [b_stgD[q_]], [b_Wdn])
                        wdn_i += 1
                    S.op(DVE, lambda e, kt=kt, s2=s2: e.tensor_scalar(out=Pi[s2][:], in0=jr_i[:], scalar1=kcol[:, kt:kt + 1], scalar2=None, op0=ALU.mult), [b_jr, b_k], [b_Pi[s2]])
                    S.op(DVE, lambda e, kt=kt, s2=s2: e.tensor_scalar(out=Ci[s2][:], in0=jr_i[:], scalar1=kcol[:, kt:kt + 1], scalar2=3072.0, op0=ALU.mult, op1=ALU.add), [b_jr, b_k], [b_Ci[s2]])
                    S.op(DVE, lambda e, s2=s2: e.tensor_single_scalar(out=Pi[s2][:], in_=Pi[s2][:], scalar=4095, op=ALU.bitwise_and), [b_Pi[s2]], [b_Pi[s2]])
                    S.op(DVE, lambda e, s2=s2: e.tensor_single_scalar(out=Ci[s2][:], in_=Ci[s2][:], scalar=4095, op=ALU.bitwise_and), [b_Ci[s2]], [b_Ci[s2]])
                    S.op(ACT, lambda e, s2=s2, s3=s3: e.activation(out=tab[s3][:, 0, :], in_=Ci[s2][:], func=AF.Sin, scale=SC4096, bias=nbias[:, 0:1]), [b_Ci[s2]], [b_tab[s3]])
                    S.op(ACT, lambda e, s2=s2, s3=s3: e.activation(out=tab[s3][:, 1, :], in_=Pi[s2][:], func=AF.Sin, scale=SC4096, bias=nbias[:, 0:1]), [b_Pi[s2]], [b_tab[s3]])
                    for g in range(4):
                        pp, off, bb = accD[g]
                        S.op(PE, lambda e, kt=kt, g=g, pp=pp, off=off, s3=s3: e.matmul(pp[:, off:off + 512], lhsT=Z[:, kt, g, 0:128], rhs=tab[s3][:, 0, :], start=(kt == 0), stop=False),
                             [b_Z, b_tab[s3]], [bb], inc=False)
                        S.op(PE, lambda e, kt=kt, g=g, pp=pp, off=off, s3=s3: e.matmul(pp[:, off:off + 512], lhsT=Z[:, kt, g, 128:256], rhs=tab[s3][:, 1, :], start=False, stop=(kt == NT - 1)),
                             [b_Z, b_tab[s3]], [bb], inc=(kt == NT - 1 or g == 3))
                for g in range(4):
                    pp, off, bb = accD[g]
                    S.op(DVE, lambda e, g=g, pp=pp, off=off: e.tensor_scalar(out=Fsb[:, g, :], in0=pp[:, off:off + 512], scalar1=FT_NORM, scalar2=None, op0=ALU.mult), [bb], [b_F])
                for g in range(4):
                    bank = rotw % 3; rotw += 1
                    S.op(PE, lambda e, g=g, bank=bank: e.matmul(pc[:, bank * 512:(bank + 1) * 512], lhsT=wft[:, g, :], rhs=Fsb[:, g, :], start=True, stop=True), [b_F, b_const], [b_pc3[bank]])
                    S.op(DVE, lambda e, g=g, bank=bank, jr=jr: e.tensor_copy(out=mixT[:, 4 + g, jr * 512:(jr + 1) * 512], in_=pc[:, bank * 512:(bank + 1) * 512]), [b_pc3[bank]], [b_mix[4 + g]])
            S.barrier()
            if debug and "mixft" in debug:
                dd = S.dsem("dbgD")
                S.dma(SP, dbg["mixft"], mixT[:, 4:8, :], b_mix[4:8], [], dd); out_sems.append(dd)
                S.barrier()
            S.flush()
        S.st = st
        if debug and debug.get("_stop") == "D":
            S.barrier(); S.flush(out_sems)
            return nc
        S.barrier()
        with ExitStack() as phf:
            S.st = phf
            NX1 = 5
            Wout = S.sb("Wout", [128, KC, D], BF16)
            gfin = S.sb("gfin", [128, D], F32)
            x1 = S.sb("x1", [128, NX1, D], F32)
            xn2 = [S.sb("xn2_%d" % i, [128, D], BF16) for i in range(2)]
            h2T = S.sb("h2T", [128, KC, 514], BF16)
            aT = S.sb("aT", [128, NCH, 512], BF16)
            wup = [S.sb("wup%d" % i, [128, 2, KC, 128], BF16) for i in range(3)]
            tgv = [S.sb("tgv%d" % i, [128, 2, 256], F32) for i in range(3)]
            junkF2 = S.sb("junkF2", [128, D], BF16)
            ssE = S.sb("ssE", [128, 4], F32)
            ssF = S.sb("ssF", [128, 8], F32)
            hr = S.sb("hr", [128, KC, 8], BF16)
            b_hr = S.buf()
            b_Wout, b_gf, b_h2T = (S.buf() for _ in range(3))
            b_xn2 = [S.buf(), S.buf()]
            b_ssF = [S.buf() for _ in range(8)]
            b_x1 = [S.buf() for _ in range(NX1)]
            b_aT = [S.buf(), S.buf()]
            b_wup = [S.buf(), S.buf(), S.buf()]; b_tgv = [S.buf(), S.buf(), S.buf()]; b_junk2 = S.buf(); b_ssE = [S.buf() for _ in range(4)]
            d_gf, d_xt, d_stg = S.dsem("gf"), S.dsem("xtF"), S.dsem("stgF")
            d_wup = [S.dsem("wup0"), S.dsem("wup1"), S.dsem("wup2")]
            d_y = [S.dsem("y%d" % i) for i in range(NX1)]
            out_sems.extend(d_y)
            S.dma(SP, gfin[:], gfin_d.partition_broadcast(128), [], [b_gf], d_gf)
            b_stg = [S.buf(), S.buf()]; d_stg2 = [S.dsem("stgF0"), S.dsem("stgF1")]
            wl = 0
            for kc in range(KC):
                s_ = wl % 2; wl += 1
                S.dma(SP if s_ == 0 else POOL, x1[:, s_, :], wout_d[kc * 128:(kc + 1) * 128, :], [], [b_stg[s_]], d_stg2[s_])
                S.op(DVE if s_ == 0 else ACT, (lambda e, kc=kc, s_=s_: e.tensor_copy(out=Wout[:, kc, :], in_=x1[:, s_, :])) if s_ == 0 else
                     (lambda e, kc=kc, s_=s_: e.activation(out=Wout[:, kc, :], in_=x1[:, s_, :], func=AF.Copy)), [b_stg[s_]], [b_Wout])
            for s_ in range(2):
                b_x1[s_].w = b_stg[s_].w; b_x1[s_].r = list(b_stg[s_].r)

            fcount = [0]

            def frontA(tok0, n, x1dst, b_dst, step=1):
                k = fcount[0]; fcount[0] += 1
                tsl = slice(tok0, tok0 + n * step, step)
                sl_, sc_ = k % 2, k % 8
                pp, bpp = (pa, b_pa2) if k % 2 == 0 else (pb, b_pb2)
                S.dma(SP, x1dst, x_d[tsl, :], [], [b_dst], d_xt)
                for half in range(2):
                    for kc in range(KC):
                        S.op(PE, lambda e, half=half, kc=kc: e.matmul(pp[0:n, half * 512:(half + 1) * 512], lhsT=mixT[:, kc, tsl], rhs=Wout[:, kc, half * 512:(half + 1) * 512], start=(kc == 0), stop=(kc == KC - 1)),
                             b_mix + [b_Wout], [bpp[half]], inc=(kc == KC - 1))
                S.op(DVE, lambda e: e.tensor_tensor(out=x1dst, in0=x1dst, in1=pp[0:n, :], op=ALU.add), [b_dst] + bpp, [b_dst])
                S.op(ACT, lambda e: e.activation(out=xn2[sl_][0:n, :], in_=x1dst, func=AF.Square, accum_out=ssF[0:n, sc_:sc_ + 1]), [b_dst], [b_xn2[sl_], b_ssF[sc_]])
                emit_rstd(ssF[0:n, sc_:sc_ + 1], ssF[0:n, sc_:sc_ + 1], D, EPS, [b_ssF[sc_]])
                S.op(DVE, lambda e: e.tensor_scalar(out=xn2[sl_][0:n, :], in0=x1dst, scalar1=ssF[0:n, sc_:sc_ + 1], scalar2=None, op0=ALU.mult), [b_dst, b_ssF[sc_]], [b_xn2[sl_]])
                return sl_

            def frontB(sl_, n, col, dstT=None, b_dT=None):
                dstT = h2T if dstT is None else dstT
                b_dT = b_h2T if b_dT is None else b_dT
                for kc in range(KC):
                    S.op(PE, lambda e, kc=kc: e.transpose(pt[:, kc * 128:kc * 128 + n], xn2[sl_][0:n, kc * 128:(kc + 1) * 128], ident[0:n, 0:n]), [b_xn2[sl_], b_const], [b_pt], inc=(kc == KC - 1))
                S.op(ACT, lambda e: e.activation(out=dstT[:, :, col:col + n], in_=pt[:].rearrange("p (k m) -> p k m", k=KC)[:, :, 0:n], func=AF.Copy), [b_pt], [b_dT])

            def x1slot(g):
                return x1[:, g % NX1, :], b_x1[g % NX1]

            S.op(DVE, lambda e: e.memset(h2T[:, :, 0:1], 0.0), [], [b_h2T])
            for j in range(4):
                xs, bx = x1slot(j)
                frontB(frontA(j * 128, 128, xs, bx), 128, 1 + j * 128)
            frontB(frontA(512, 7, x1[0:7, 4, :], b_x1[4], step=512), 7, 0, hr, b_hr)
            S.op(DVE, lambda e: e.tensor_copy(out=h2T[:, :, 513:514], in_=hr[:, :, 0:1]), [b_hr, b_h2T], [b_h2T])

            cw = [0]
            cs = [0]
            for b in range(8):
                t0 = b * 512
                prev = None
                units = [(i, half) for i in range(NCH) for half in range(2)]
                for unit in units + [None]:
                    cur = None
                    if unit is not None:
                        i, half = unit
                        if half == 0:
                            n_ = cw[0]; cw[0] += 1
                            ws = n_ % 3
                            if n_ == 0:
                                for q_ in range(2):
                                    S.dma(SP, wup[q_][:].rearrange("p a k c -> p (a k c)"), wup_s[q_, :, :], [], [b_wup[q_]], d_wup[q_])
                            if n_ + 2 < 8 * NCH:
                                q_ = (n_ + 2) % 3
                                S.dma(SP, wup[q_][:].rearrange("p a k c -> p (a k c)"), wup_s[(n_ + 2) % NCH, :, :], [], [b_wup[q_]], d_wup[q_])
                        c0 = half * 256
                        u = cs[0]; cs[0] += 1
                        s_, s3 = u % 2, u % 3
                        pp, bpp = (pa, b_pa2) if s_ == 0 else (pb, b_pb2)
                        for gv in range(2):
                            for kc in range(KC):
                                S.op(PE, lambda e, gv=gv, kc=kc, ws=ws, c0=c0, pp=pp: e.matmul(pp[:, gv * 512:gv * 512 + 258], lhsT=wup[ws][:, gv, kc, :], rhs=h2T[:, kc, c0:c0 + 258], start=(kc == 0), stop=(kc == KC - 1)),
                                     [b_wup[ws], b_h2T], [bpp[gv]], inc=(kc == KC - 1))
                        for gv in range(2):
                            ch = i + gv * NCH
                            S.op(ACT, lambda e, gv=gv, ch=ch, s3=s3, pp=pp: e.activation(out=tgv[s3][:, gv, :], in_=pp[:, gv * 512 + 1:gv * 512 + 257], func=AF.Identity, scale=wcv[:, 1, ch:ch + 1], bias=wcv[:, 3, ch:ch + 1]),
                                 [bpp[gv], b_const], [b_tgv[s3]])
                        for gv in range(2):
                            ch = i + gv * NCH
                            S.op(DVE, lambda e, gv=gv, ch=ch, s3=s3, pp=pp: e.scalar_tensor_tensor(out=tgv[s3][:, gv, :], in0=pp[:, gv * 512:gv * 512 + 256], scalar=wcv[:, 0, ch:ch + 1], in1=tgv[s3][:, gv, :], op0=ALU.mult, op1=ALU.add),
                                 [bpp[gv], b_const, b_tgv[s3]], [b_tgv[s3]])
                            S.op(DVE, lambda e, gv=gv, ch=ch, s3=s3, pp=pp: e.scalar_tensor_tensor(out=tgv[s3][:, gv, :], in0=pp[:, gv * 512 + 2:gv * 512 + 258], scalar=wcv[:, 2, ch:ch + 1], in1=tgv[s3][:, gv, :], op0=ALU.mult, op1=ALU.add),
                                 [bpp[gv], b_const, b_tgv[s3]], [b_tgv[s3]])
                        cur = (i, half, s3, s_)
                    if prev is not None:
                        pi_, ph_, p3, p2 = prev
                        S.op(ACT, lambda e, p3=p3: e.activation(out=tgv[p3][:, 0, :], in_=tgv[p3][:, 0, :], func=AF.Silu), [b_tgv[p3]], [b_tgv[p3]])
                        S.op(POOL, lambda e, p3=p3, pi_=pi_, ph_=ph_: e.tensor_tensor(out=aT[:, pi_, ph_ * 256:ph_ * 256 + 256], in0=tgv[p3][:, 0, :], in1=tgv[p3][:, 1, :], op=ALU.mult),
                             [b_tgv[p3]], [b_aT[ph_]])
                    prev = cur
                if b < 7:
                    S.op(DVE, lambda e: e.tensor_copy(out=h2T[:, :, 0:1], in_=h2T[:, :, 512:513]), [b_h2T], [b_h2T])
                    if b < 6:
                        S.op(DVE, lambda e, b=b: e.tensor_copy(out=h2T[:, :, 513:514], in_=hr[:, :, b + 1:b + 2]), [b_hr, b_h2T], [b_h2T])
                    else:
                        S.op(DVE, lambda e: e.memset(h2T[:, :, 513:514], 0.0), [b_h2T], [b_h2T])
                pendB = None
                for j in range(4):
                    g = 4 * b + j
                    xs, bx = x1slot(g)
                    for half in range(2):
                        for i in range(NCH):
                            S.op(PE, lambda e, j=j, half=half, i=i: e.matmul(pc[:, half * 512:(half + 1) * 512], lhsT=aT[:, i, j * 128:(j + 1) * 128], rhs=Wdn[:, i, half * 512:(half + 1) * 512], start=(i == 0), stop=(i == NCH - 1)),
                                 [b_aT[j // 2], b_Wdn], [b_pc3[half]], inc=(i == NCH - 1))
                    S.op(DVE, lambda e, xs=xs: e.tensor_tensor(out=xs, in0=xs, in1=pc[:, 0:1024], op=ALU.add), [b_pc3[0], b_pc3[1], bx], [bx])
                    if b < 7:
                        xs2, bx2 = x1slot(g + 4)
                        slA = frontA(t0 + 512 + j * 128, 128, xs2, bx2)
                    if pendB is not None:
                        frontB(*pendB)
                    pendB = (slA, 128, 1 + j * 128) if b < 7 else None
                    S.op(ACT, lambda e, xs=xs, j=j: e.activation(out=junkF2[:], in_=xs, func=AF.Square, accum_out=ssE[:, j:j + 1]), [bx], [b_junk2, b_ssE[j]])
                    emit_rstd(ssE[:, j:j + 1], ssE[:, j:j + 1], D, EPS, [b_ssE[j]])
                    S.op(DVE, lambda e, xs=xs, j=j: e.scalar_tensor_tensor(out=xs, in0=xs, scalar=ssE[:, j:j + 1], in1=gfin[:], op0=ALU.mult, op1=ALU.mult), [bx, b_ssE[j], b_gf], [bx])
                    S.dma(SP, y_d[t0 + j * 128:t0 + (j + 1) * 128, :], xs, [bx], [], d_y[g % NX1])
                if pendB is not None:
                    frontB(*pendB)
            S.barrier()
            S.flush(out_sems)
        S.st = st
    return nc


def kernel(**inputs):
    nc = build_program()
    x = np.asarray(inputs["x"], dtype=np.float32)
    B = x.shape[0]
    shared = {}
    for k, v in inputs.items():
        if k == "x":
            continue
        a = np.asarray(v, dtype=np.float32)
        shared[k] = np.ascontiguousarray(a[0] if k != "g_final" else a)
    in_maps = []
    for b in range(B):
        m = dict(shared)
        m["x"] = np.ascontiguousarray(x[b])
        in_maps.append(m)
    res = run_bass_kernel_spmd(nc, in_maps, core_ids=list(range(B)))
    return np.stack([np.asarray(r["y"], dtype=np.float32) for r in res.results], axis=0)
```

```python
import math
from contextlib import ExitStack
import numpy as np
import concourse.bass as bass
import concourse.mybir as mybir
from concourse.bass_utils import run_bass_kernel_spmd

F32 = mybir.dt.float32
BF16 = mybir.dt.bfloat16
F16 = mybir.dt.float16
I32 = mybir.dt.int32
AF = mybir.ActivationFunctionType
ALU = mybir.AluOpType
AX = mybir.AxisListType


class _Res:
    def __init__(self, name, sem, attr=None):
        self.name, self.sem, self.attr = name, sem, attr
        self.count = 0
        self.pos = 0
        self.flushed = 0
        self.prog = []
        self.seen = {}


class _Buf:
    def __init__(self, name):
        self.name = name
        self.w = None
        self.r = []


class Sched:
    def __init__(self, nc, stack):
        self.nc, self.st = nc, stack
        mk = lambda n, a: _Res(n, stack.enter_context(nc.semaphore("s_" + n)), a)
        self.PE, self.ACT, self.DVE = mk("pe", "tensor"), mk("act", "scalar"), mk("dve", "vector")
        self.POOL, self.SP = mk("pool", "gpsimd"), mk("sp", "sync")
        self.engines = [self.PE, self.ACT, self.DVE, self.POOL, self.SP]
        self.n_instr = 0

    def sb(self, name, shape, dt):
        return self.st.enter_context(self.nc.sbuf_tensor(name, shape, dt))

    def ps(self, name, shape, dt):
        return self.st.enter_context(self.nc.psum_tensor(name, shape, dt))

    def buf(self, name="b"):
        return _Buf(name)

    def dsem(self, name):
        return _Res(name, self.st.enter_context(self.nc.semaphore("d_" + name)))

    def _waits(self, E, reads, writes, extra=()):
        deps = {}
        def add(tok):
            if tok is None:
                return
            r, v = tok
            if r is E and E is self.PE:
                return
            if deps.get(r, 0) < v:
                deps[r] = v
        for b in reads:
            add(b.w)
        for b in writes:
            add(b.w)
            for t in b.r:
                add(t)
        for t in extra:
            add(t)
        for r, v in deps.items():
            if E.seen.get(r, 0) < v:
                E.seen[r] = v
                E.prog.append(("w", r, v))

    def _mark(self, tok, reads, writes):
        for b in writes:
            b.w = tok
            b.r = []
        for b in reads:
            if b not in writes:
                b.r.append(tok)
        self.n_instr += 1

    def op(self, E, fn, reads, writes, inc=True):
        self._waits(E, reads, writes)
        E.pos += 1
        tok = (E, E.pos)
        E.prog.append(("i", fn, E.pos))
        self._mark(tok, reads, writes)
        return tok

    def dma(self, E, out, in_, reads, writes, ds, **kw):
        prev = (ds, ds.count) if ds.count else None
        self._waits(E, reads, writes, extra=(prev,) if prev else ())
        ds.count += 16
        tok = (ds, ds.count)
        E.prog.append(("d", lambda e: e.dma_start(out=out, in_=in_, **kw), ds))
        self._mark(tok, reads, writes)
        return tok

    def barrier(self):
        for E in self.engines:
            for R in self.engines:
                if R is self.SP or (R is E and E is self.PE) or R.pos == 0:
                    continue
                if R.prog and R.prog[-1][0] == "d":
                    pass
                if E.seen.get(R, 0) < R.pos:
                    E.seen[R] = R.pos
                    E.prog.append(("w", R, R.pos))

    def flush(self, final_dsems=()):
        self.barrier()
        for ds in final_dsems:
            if ds.count:
                self.SP.prog.append(("w", ds, ds.count))
        nc = self.nc
        comp = [E for E in self.engines]
        needed = {E: set() for E in comp}
        for E in comp:
            for it in E.prog:
                if it[0] == "w" and it[1] in needed and it[2] > it[1].flushed:
                    needed[it[1]].add(it[2])
        rank = {}
        for E in comp:
            for k, p in enumerate(sorted(needed[E])):
                rank[(E, p)] = E.count + k + 1
        progs = {}
        for E in comp:
            out = []
            for it in E.prog:
                if it[0] == "w":
                    r, v = it[1], it[2]
                    if r in needed:
                        if v <= r.flushed:
                            continue
                        out.append(("w", r.sem, rank[(r, v)]))
                    else:
                        out.append(("w", r.sem, v))
                elif it[0] == "i":
                    out.append(("i", it[1], E.sem if (E, it[2]) in rank else None, 1))
                else:
                    out.append(("i", it[1], it[2].sem, 16))
            progs[E.attr] = out
        for E in comp:
            E.count += len(needed[E])
            E.flushed = E.pos
            E.prog = []

        def runner(prog):
            def _(eng):
                for it in prog:
                    if it[0] == "w":
                        eng.wait_ge(it[1], it[2])
                    else:
                        ins = it[1](eng)
                        if it[2] is not None:
                            ins.then_inc(it[2], it[3])
            return _
        with nc.Block() as block:
            block.tensor(runner(progs["tensor"]))
            block.scalar(runner(progs["scalar"]))
            block.vector(runner(progs["vector"]))
            block.gpsimd(runner(progs["gpsimd"]))
            block.sync(runner(progs["sync"]))

    def finish(self, dsems):
        self.flush(dsems)


S_LEN = 4096
D = 1024
NT = S_LEN // 128
KC = D // 128
NH = 4
DFF = 2816
NCH = DFF // 128
EPS = 1e-6
SUBLN_EPS = 1e-5
LAMBDA_INIT = 0.8 - 0.6 * math.exp(0.0)
SLOPES = [2.0 ** (-8.0 * (h + 1) / NH) for h in range(NH)]
PI_IN = 3.1415925
FT_NORM = 1.0 / math.sqrt(4096.0 * 128.0)


def build_program(debug=None):
    nc = bass.Bass("TRN2", target_bir_lowering=False)
    dr = lambda n, shp, dt=F32, kind="ExternalInput": nc.dram_tensor(n, shp, dt, kind=kind).ap()
    x_d = dr("x", [S_LEN, D])
    gmix_d = dr("g_mix", [D]); win_d = dr("w_in", [D, 2048])
    lq1_d, lk1_d, lq2_d, lk2_d = (dr(n, [64]) for n in ("lambda_q1", "lambda_k1", "lambda_q2", "lambda_k2"))
    gsub_d = dr("g_subln", [128]); wft_d = dr("w_ft", [4, 128, 128]); wout_d = dr("w_out", [D, D])
    gffn_d = dr("g_ffn", [D]); wup_d = dr("w_up", [D, 2 * DFF]); wconv_d = dr("w_conv", [3, 2 * DFF])
    bconv_d = dr("b_conv", [2 * DFF]); wdn_d = dr("w_down", [DFF, D]); gfin_d = dr("g_final", [D])
    y_d = dr("y", [S_LEN, D], F32, "ExternalOutput")
    wup_s = nc.dram_tensor("wup_s", [NCH, 128, 2 * KC * 128], BF16).ap()
    dbg = {}
    if debug:
        for n, (shp, dt) in debug.items():
            dbg[n] = dr("dbg_" + n, shp, dt, "ExternalOutput")

    with ExitStack() as st:
        S = Sched(nc, st)
        PE, ACT, DVE, POOL, SP = S.PE, S.ACT, S.DVE, S.POOL, S.SP
        out_sems = []

        mixT = S.sb("mixT", [128, 8, S_LEN], BF16)
        b_mix = [S.buf("mix%d" % i) for i in range(8)]
        ident = S.sb("ident", [128, 128], BF16)
        iota_i = S.sb("iota_i", [128, 128], I32)
        gsub = S.sb("gsub", [128, 128], F32)
        wcv = S.sb("wcv", [128, 4, 2 * NCH], F32)
        gcol = S.sb("gcol", [128, 2, KC], F32)
        lamt = S.sb("lamt", [128, 4, 64], F32)
        lams = S.sb("lams", [128, 8], F32)
        wft = S.sb("wft", [128, 4, 128], BF16)
        nbias = S.sb("nbias", [128, 1], F32)
        b_const = S.buf("const")
        pt = S.ps("pt", [128, 1024], BF16)
        pa = S.ps("pa", [128, 1024], F32)
        pb = S.ps("pb", [128, 1024], F32)
        pc = S.ps("pc", [128, 1536], F32)
        b_pt, b_pa, b_pb, b_pc = S.buf("pt"), S.buf("pa"), S.buf("pb"), S.buf("pc")
        b_pa2 = [S.buf("pa0"), S.buf("pa1")]
        b_pb2 = [S.buf("pb0"), S.buf("pb1")]
        b_pc3 = [S.buf("pc0"), S.buf("pc1"), S.buf("pc2")]
        dc = [S.dsem("c%d" % i) for i in range(4)]

        S.op(DVE, lambda e: e.memset(nbias[:], -PI_IN), [], [b_const])
        S.op(POOL, lambda e: e.iota(iota_i[:], pattern=[[1, 128]], base=0, channel_multiplier=-1), [], [b_const])
        S.op(DVE, lambda e: e.tensor_single_scalar(out=ident[:], in_=iota_i[:], scalar=0, op=ALU.is_equal), [b_const], [b_const])
        S.dma(SP, gsub[:], gsub_d.partition_broadcast(128), [], [b_const], dc[0])
        S.dma(SP, wcv[:, 0:3, :], wconv_d.rearrange("t (c p) -> p t c", p=128), [], [b_const], dc[2], allow_slow_non_contiguous=True)
        S.dma(SP, wcv[:, 3, :], bconv_d.rearrange("(c p) -> p c", p=128), [], [b_const], dc[3], allow_slow_non_contiguous=True)
        S.dma(SP, gcol[:, 0, :], gmix_d.rearrange("(c p) -> p c", p=128), [], [b_const], dc[0], allow_slow_non_contiguous=True)
        S.dma(SP, gcol[:, 1, :], gffn_d.rearrange("(c p) -> p c", p=128), [], [b_const], dc[1], allow_slow_non_contiguous=True)
        for i, ld in enumerate((lq1_d, lk1_d, lq2_d, lk2_d)):
            S.dma(SP, lamt[:, i, :], ld.partition_broadcast(128), [], [b_const], dc[2 + i % 2])
        S.op(DVE, lambda e: e.tensor_scalar(out=gsub[:], in0=gsub[:], scalar1=1.0 - LAMBDA_INIT, scalar2=None, op0=ALU.mult), [b_const], [b_const])
        S.op(DVE, lambda e: e.tensor_tensor(out=lamt[:, 0, :], in0=lamt[:, 0, :], in1=lamt[:, 1, :], op=ALU.mult), [b_const], [b_const])
        S.op(DVE, lambda e: e.tensor_tensor(out=lamt[:, 2, :], in0=lamt[:, 2, :], in1=lamt[:, 3, :], op=ALU.mult), [b_const], [b_const])
        S.op(DVE, lambda e: e.reduce_sum(out=lams[:, 0:1], in_=lamt[:, 0, :], axis=AX.X), [b_const], [b_const])
        S.op(DVE, lambda e: e.reduce_sum(out=lams[:, 1:2], in_=lamt[:, 2, :], axis=AX.X), [b_const], [b_const])
        S.op(ACT, lambda e: e.activation(out=lams[:, 2:4], in_=lams[:, 0:2], func=AF.Exp), [b_const], [b_const])
        S.op(DVE, lambda e: e.scalar_tensor_tensor(out=lams[:, 4:5], in0=lams[:, 3:4], scalar=-LAMBDA_INIT, in1=lams[:, 2:3], op0=ALU.add, op1=ALU.subtract), [b_const], [b_const])

        with ExitStack() as ph:
            S.st = ph
            stg = [S.sb("stg%d" % i, [128, 2 * DFF], F32) for i in range(2)]
            wbig = S.sb("wbig", [128, NCH, 2, KC, 128], BF16)
            b_stg = [S.buf(), S.buf()]; b_wbig = S.buf()
            dl = [S.dsem("wl0"), S.dsem("wl1")]; dsv = [S.dsem("ws0"), S.dsem("ws1")]
            S.dma(SP, stg[0][:, 0:512].rearrange("p (g d) -> p g d", g=4), wft_d.rearrange("g c d -> c g d"), [], [b_stg[0]], dl[0])
            S.op(DVE, lambda e: e.tensor_copy(out=wft[:].rearrange("p g d -> p (g d)"), in_=stg[0][:, 0:512]), [b_stg[0]], [b_const])
            win_sb = mixT[:, 0:4, :].rearrange("p a (b n) -> p (a b) n", n=2048)
            for kc in range(KC):
                s_ = kc % 2
                S.dma(SP, stg[s_][:, 0:2048], win_d[kc * 128:(kc + 1) * 128, :], [], [b_stg[s_]], dl[s_])
                if s_ == 0:
                    S.op(DVE, lambda e, kc=kc, s_=s_: e.tensor_scalar(out=win_sb[:, kc, :], in0=stg[s_][:, 0:2048], scalar1=gcol[:, 0, kc:kc + 1], scalar2=None, op0=ALU.mult),
                         [b_stg[s_], b_const], b_mix[0:4])
                else:
                    S.op(ACT, lambda e, kc=kc, s_=s_: e.activation(out=win_sb[:, kc, :], in_=stg[s_][:, 0:2048], func=AF.Copy, scale=gcol[:, 0, kc:kc + 1]),
                         [b_stg[s_], b_const], b_mix[0:4])
            for kc in range(KC):
                s_ = kc % 2
                S.dma(SP, stg[s_][:], wup_d[kc * 128:(kc + 1) * 128, :], [], [b_stg[s_]], dl[s_])
                src = lambda gv, s_=s_: stg[s_][:, gv * DFF:(gv + 1) * DFF].rearrange("p (i ch) -> p i ch", ch=128)
                S.op(DVE, lambda e, kc=kc, s_=s_: e.tensor_scalar(out=wbig[:, :, 0, kc, :], in0=stg[s_][:, 0:DFF].rearrange("p (i ch) -> p i ch", ch=128), scalar1=gcol[:, 1, kc:kc + 1], scalar2=None, op0=ALU.mult),
                     [b_stg[s_], b_const], [b_wbig])
                S.op(ACT, lambda e, kc=kc, s_=s_: e.activation(out=wbig[:, :, 1, kc, :], in_=stg[s_][:, DFF:2 * DFF].rearrange("p (i ch) -> p i ch", ch=128), func=AF.Copy, scale=gcol[:, 1, kc:kc + 1]),
                     [b_stg[s_], b_const], [b_wbig])
            for q in range(2):
                lo, hi = q * (NCH // 2), (q + 1) * (NCH // 2)
                S.dma(SP, wup_s[lo:hi, :, :].rearrange("i p n -> p i n"), wbig[:, lo:hi, :, :, :].rearrange("p i a k c -> p i (a k c)"), [b_wbig], [], dsv[q])
            wups_done = [(dsv[0], dsv[0].count), (dsv[1], dsv[1].count)]
            S.barrier()
            for E in S.engines:
                for r, v in wups_done:
                    E.seen[r] = v
                    E.prog.append(("w", r, v))
            S.flush()
        S.st = st

        def emit_rstd(dst, ss, dim, eps, bufs):
            S.op(DVE, lambda e: e.tensor_scalar(out=dst, in0=ss, scalar1=1.0 / dim, scalar2=eps, op0=ALU.mult, op1=ALU.add), bufs, bufs)
            S.op(ACT, lambda e: e.activation(out=dst, in_=dst, func=AF.Ln), bufs, bufs)
            S.op(ACT, lambda e: e.activation(out=dst, in_=dst, func=AF.Exp, scale=-0.5), bufs, bufs)

        with ExitStack() as phab:
            S.st = phab
            QT = S.sb("QT", [128, NH, S_LEN], BF16)
            KT = S.sb("KT", [128, NH, S_LEN], BF16)
            VA = S.sb("VA", [128, NT, NH, 129], BF16)
            UT = mixT[:, 4:8, :]
            b_qk = S.buf("qk"); b_v = S.buf("v"); b_ut = b_mix[4:8]
            b_qkw = [[S.buf() for _ in range(NH)] for _ in range(2)]
            b_vw = [S.buf() for _ in range(NT)]
            S.op(POOL, lambda e: e.memset(VA[:, :, :, 128:129], 1.0), [], [b_v])
            with ExitStack() as pha:
                S.st = pha
                xt = [S.sb("xt%d" % i, [128, D], F32) for i in range(2)]
                xn = [S.sb("xn%d" % i, [128, D], BF16) for i in range(2)]
                junk = S.sb("junkA", [128, D], BF16)
                ssq = S.sb("ssqA", [128, 2], F32)
                hT = [S.sb("hT%d" % i, [128, KC, 512], BF16) for i in range(2)]
                b_xt = [S.buf(), S.buf()]; b_xn = [S.buf(), S.buf()]; b_hT = [[S.buf() for _ in range(4)] for _ in range(2)]
                b_junk = S.buf(); b_ss = [S.buf(), S.buf()]
                dx = [S.dsem("xa0"), S.dsem("xa1")]
                win_sb = mixT[:, 0:4, :].rearrange("p a (b n) -> p (a b) n", n=2048)
                b_win = b_mix[0:4]
                rot = [0]
                def big_group(sp, oi):
                    hs = sp % 2
                    kind, hh = oi // 4, oi % 4
                    col0 = (0, 512, 1536)[kind] + hh * 128
                    bank = rot[0] % 3; rot[0] += 1
                    for kc in range(KC):
                        S.op(PE, lambda e, kc=kc: e.matmul(pc[:, bank * 512:(bank + 1) * 512], lhsT=win_sb[:, kc, col0:col0 + 128], rhs=hT[hs][:, kc, :], start=(kc == 0), stop=(kc == KC - 1)),
                             b_hT[hs] + b_win, [b_pc3[bank]])
                    dst = (QT, KT, None)[kind]
                    if kind < 2:
                        if oi % 2 == 0:
                            S.op(ACT, lambda e: e.activation(out=dst[:, hh, sp * 512:(sp + 1) * 512], in_=pc[:, bank * 512:(bank + 1) * 512], func=AF.Copy),
                                 [b_pc3[bank]], [b_qkw[kind][hh]])
                        else:
                            S.op(DVE, lambda e: e.tensor_copy(out=dst[:, hh, sp * 512:(sp + 1) * 512], in_=pc[:, bank * 512:(bank + 1) * 512]),
                                 [b_pc3[bank]], [b_qkw[kind][hh]])
                    else:
                        S.op(DVE, lambda e: e.tensor_copy(out=mixT[:, 4 + hh, sp * 512:(sp + 1) * 512], in_=pc[:, bank * 512:(bank + 1) * 512]),
                             [b_pc3[bank]], [b_mix[4 + hh]])

                def tile_chain(sp, tt):
                    hs = sp % 2
                    t = sp * 4 + tt
                    s_ = t % 2
                    S.dma(SP, xt[s_][:], x_d[t * 128:(t + 1) * 128, :], [], [b_xt[s_]], dx[s_])
                    S.op(ACT, lambda e: e.activation(out=junk[:], in_=xt[s_][:], func=AF.Square, accum_out=ssq[:, s_:s_ + 1]),
                         [b_xt[s_]], [b_junk, b_ss[s_]])
                    emit_rstd(ssq[:, s_:s_ + 1], ssq[:, s_:s_ + 1], D, EPS, [b_ss[s_]])
                    S.op(DVE, lambda e: e.tensor_scalar(out=xn[s_][:], in0=xt[s_][:], scalar1=ssq[:, s_:s_ + 1], scalar2=None, op0=ALU.mult),
                         [b_xt[s_], b_ss[s_]], [b_xn[s_]])
                    for kc in range(KC):
                        S.op(PE, lambda e, kc=kc: e.transpose(pt[:, kc * 128:(kc + 1) * 128], xn[s_][:, kc * 128:(kc + 1) * 128], ident[:]),
                             [b_xn[s_], b_const], [b_pt])
                    S.op(ACT, lambda e: e.activation(out=hT[hs][:, :, tt * 128:(tt + 1) * 128], in_=pt[:].rearrange("p (k n) -> p k n", k=KC), func=AF.Copy),
                         [b_pt], [b_hT[hs][tt]])

                def v_tile(sp, tt):
                    hs = sp % 2
                    t = sp * 4 + tt
                    pv, bpv = (pa, b_pa2) if t % 2 == 0 else (pb, b_pb2)
                    for kc in range(KC):
                        S.op(PE, lambda e, kc=kc: e.matmul(pv[:, 0:512], lhsT=hT[hs][:, kc, tt * 128:(tt + 1) * 128], rhs=win_sb[:, kc, 1024:1536], start=(kc == 0), stop=(kc == KC - 1)),
                             [b_hT[hs][tt]] + b_win, [bpv[0]])
                    S.op(DVE, lambda e: e.tensor_copy(out=VA[:, t, :, 0:128], in_=pv[:, 0:512].rearrange("p (h v) -> p h v", h=NH)),
                         [bpv[0]], [b_vw[t]])

                for sp in range(NT // 4 + 1):
                    for tt in range(4):
                        if sp < NT // 4:
                            tile_chain(sp, tt)
                        if sp > 0:
                            for oi in range(tt * 3, tt * 3 + 3):
                                big_group(sp - 1, oi)
                        if sp < NT // 4:
                            v_tile(sp, tt)
                S.barrier()
                if debug and "QT" in debug:
                    dd = S.dsem("dbgA")
                    S.dma(SP, dbg["QT"], QT[:], [b_qk], [], dd); out_sems.append(dd)
                    dd = S.dsem("dbgA2")
                    S.dma(SP, dbg["KT"], KT[:], [b_qk], [], dd); out_sems.append(dd)
                    dd = S.dsem("dbgA3")
                    S.dma(SP, dbg["VA"], VA[:], [b_v], [], dd); out_sems.append(dd)
                    dd = S.dsem("dbgA4")
                    S.dma(SP, dbg["UT"], mixT[:, 4:8, :], b_mix[4:8], [], dd); out_sems.append(dd)
                    S.barrier()
                S.flush()
            S.st = phab
            if debug and debug.get("_stop") == "A":
                S.barrier(); S.flush(out_sems)
                return nc
            with ExitStack() as phb:
                S.st = phb
                Rcol = S.sb("Rcol", [128, 60], I32)
                colb = S.sb("colb", [128, NH, 60], F32)
                R2 = S.sb("R2", [128, 4], I32)
                R2f = S.sb("R2f", [128, 4], F32)
                fpm = S.sb("fpm", [128, NH, 2, 4], F32)
                Qr = S.sb("Qr", [128, 512], F32)
                Ti = S.sb("Ti", [128, 512], I32)
                Tf = S.sb("Tf", [128, 512], F32)
                Bt = S.sb("Bt", [128, 4, 512], F16)
                ident16 = S.sb("ident16", [128, 128], F16)
                Eb = S.sb("Eb", [128, 3, 2, 512], BF16)
                Oul = S.sb("Oul", [128, 2, 3, 387], F32)
                num = S.sb("num", [128, 8, 129], F32)
                rden = S.sb("rden", [128, 8], F32)
                Pn = S.sb("Pn", [128, 8, 128], F32)
                o_t = S.sb("o_t", [128, 4, 128], F32)
                junkB = S.sb("junkB", [128, 128], F32)
                ssb = S.sb("ssb", [128, 4], F32)
                on_t = S.sb("on_t", [128, 4, 128], BF16)
                b_tab = S.buf("tab"); b_Bt = S.buf("Bt"); b_T = S.buf("T")
                b_E = [[S.buf(), S.buf()] for _ in range(3)]
                b_O = [S.buf("Oup"), S.buf("Olo")]
                b_num, b_P, b_o, b_ss, b_on = S.buf(), S.buf(), S.buf(), S.buf(), S.buf()
                zcol_i = 28

                S.op(POOL, lambda e: e.iota(Rcol[:], pattern=[[128, 60]], base=-28 * 128 - 256, channel_multiplier=1), [], [b_tab])
                S.op(POOL, lambda e: e.iota(R2[:], pattern=[[128, 4]], base=-256, channel_multiplier=1), [], [b_tab])
                S.op(POOL, lambda e: e.iota(Ti[:], pattern=[[1, 512]], base=-256, channel_multiplier=0), [], [b_T])
                S.op(DVE, lambda e: e.tensor_copy(out=Qr[:], in_=Ti[:]), [b_T], [b_tab, b_T])
                S.op(DVE, lambda e: e.tensor_copy(out=R2f[:], in_=R2[:]), [b_tab], [b_tab])
                S.op(DVE, lambda e: e.tensor_copy(out=ident16[:], in_=ident[:]), [b_const], [b_tab])
                for h in range(NH):
                    sl = SLOPES[h]
                    S.op(DVE, lambda e, h=h, sl=sl: e.tensor_scalar(out=colb[:, h, 0:28], in0=Rcol[:, 0:28], scalar1=sl, scalar2=None, op0=ALU.mult), [b_tab], [b_tab])
                    S.op(DVE, lambda e, h=h, sl=sl: e.tensor_scalar(out=colb[:, h, 28:60], in0=Rcol[:, 28:60], scalar1=-sl, scalar2=None, op0=ALU.mult), [b_tab], [b_tab])
                    S.op(DVE, lambda e, h=h: e.memset(colb[:, h, 28:32], 0.0), [b_tab], [b_tab])
                    S.op(ACT, lambda e, h=h, sl=sl: e.activation(out=fpm[:, h, 0, :], in_=R2f[:], func=AF.Exp, scale=sl), [b_tab], [b_tab])
                    S.op(ACT, lambda e, h=h, sl=sl: e.activation(out=fpm[:, h, 1, :], in_=R2f[:], func=AF.Exp, scale=-sl), [b_tab], [b_tab])

                def acc_ap(c, j, w=129):
                    idx = c * 4 + j
                    o0 = (idx // 3) * 512 + (idx % 3) * 129
                    return pc[:, o0:o0 + w], b_pc3[idx // 3]

                def reg(t, c, j, lo=0, hi=129):
                    idx = c * 4 + j
                    return Oul[:, t, idx // 3, (idx % 3) * 129 + lo:(idx % 3) * 129 + hi]

                def comb1(h, r):
                    for c in range(2):
                        for j in range(4):
                            idx = c * 4 + j
                            S.op(DVE, lambda e, c=c, j=j, idx=idx: e.tensor_scalar(out=num[:, idx, :], in0=reg(0, c, j), scalar1=fpm[:, h, 0, j:j + 1], scalar2=None, op0=ALU.mult),
                                 [b_O[0], b_tab], [b_num])
                            if r > 0:
                                S.op(DVE, lambda e, c=c, j=j, idx=idx: e.scalar_tensor_tensor(out=num[:, idx, :], in0=reg(1, c, j), scalar=fpm[:, h, 1, j:j + 1], in1=num[:, idx, :], op0=ALU.mult, op1=ALU.add),
                                     [b_O[1], b_tab, b_num], [b_num])
                    S.op(DVE, lambda e: e.reciprocal(out=rden[:], in_=num[:, :, 128]), [b_num], [b_num])
                    for idx in range(8):
                        S.op(DVE, lambda e, idx=idx: e.tensor_scalar(out=Pn[:, idx, :], in0=num[:, idx, 0:128], scalar1=rden[:, idx:idx + 1], scalar2=None, op0=ALU.mult),
                             [b_num], [b_P])
                    S.op(DVE, lambda e: e.scalar_tensor_tensor(out=o_t[:], in0=Pn[:, 4:8, :], scalar=lams[:, 4:5], in1=Pn[:, 0:4, :], op0=ALU.mult, op1=ALU.add),
                         [b_P, b_const], [b_o])
                    for j in range(4):
                        S.op(DVE, lambda e, j=j: e.scalar_tensor_tensor(out=junkB[:], in0=o_t[:, j, :], scalar=1.0, in1=o_t[:, j, :], op0=ALU.mult, op1=ALU.mult, accum_out=ssb[:, j:j + 1]),
                             [b_o], [b_ss])
                    S.op(DVE, lambda e: e.tensor_scalar(out=ssb[:], in0=ssb[:], scalar1=1.0 / 128, scalar2=SUBLN_EPS, op0=ALU.mult, op1=ALU.add), [b_ss], [b_ss])

                def comb2(h, r):
                    S.op(ACT, lambda e: e.activation(out=ssb[:], in_=ssb[:], func=AF.Ln), [b_ss], [b_ss])
                    S.op(ACT, lambda e: e.activation(out=ssb[:], in_=ssb[:], func=AF.Exp, scale=-0.5), [b_ss], [b_ss])

                def comb3(h, r):
                    for j in range(4):
                        S.op(DVE, lambda e, j=j: e.scalar_tensor_tensor(out=on_t[:, j, :], in0=o_t[:, j, :], scalar=ssb[:, j:j + 1], in1=gsub[:], op0=ALU.mult, op1=ALU.mult),
                             [b_o, b_ss, b_const], [b_on])
                    for j in range(4):
                        S.op(PE, lambda e, j=j: e.transpose(pt[:, j * 128:(j + 1) * 128], on_t[:, j, :], ident[:]), [b_on, b_const], [b_pt], inc=(j == 3))
                    S.op(DVE, lambda e: e.tensor_copy(out=mixT[:, h, r * 512:(r + 1) * 512], in_=pt[:, 0:512]), [b_pt], [b_mix[h]])

                pend = [None]
                def gen_tables(h):
                    sl = SLOPES[h]
                    for a in range(4):
                        S.op(POOL, lambda e, a=a: e.iota(Ti[:], pattern=[[-1, 512]], base=128 * a, channel_multiplier=1), [], [b_T])
                        S.op(DVE, lambda e: e.tensor_copy(out=Tf[:], in_=Ti[:]), [b_T], [b_T])
                        S.op(DVE, lambda e: e.scalar_tensor_tensor(out=Tf[:], in0=Tf[:], scalar=-1.0, in1=Tf[:], op0=ALU.mult, op1=ALU.max), [b_T], [b_T])
                        S.op(DVE, lambda e: e.tensor_tensor(out=Tf[:], in0=Tf[:], in1=Qr[:], op=ALU.add), [b_T, b_tab], [b_T])
                        S.op(DVE, lambda e: e.tensor_scalar(out=Tf[:], in0=Tf[:], scalar1=-8.0 * sl, scalar2=None, op0=ALU.mult), [b_T], [b_T])
                        S.op(DVE, lambda e, a=a: e.tensor_copy(out=Bt[:, a, :], in_=Tf[:]), [b_T], [b_Bt])

                def qk(h, r, kt, sc):
                    a = kt - 4 * r
                    inr = 0 <= a <= 3
                    ps, bps = (pa, b_pa2) if sc == 0 else (pb, b_pb2)
                    for c in range(2):
                        S.op(PE, lambda e, c=c: e.matmul(ps[:, c * 512:(c + 1) * 512], lhsT=KT[c * 64:(c + 1) * 64, h, kt * 128:(kt + 1) * 128],
                                                         rhs=QT[c * 64:(c + 1) * 64, h, r * 512:(r + 1) * 512], start=True, stop=not inr),
                             [b_qk], [bps[c]])
                    if inr:
                        for c in range(2):
                            S.op(PE, lambda e, c=c: e.matmul(ps[:, c * 512:(c + 1) * 512], lhsT=ident16[:], rhs=Bt[:, a, :], start=False, stop=True),
                                 [b_Bt, b_tab], [bps[c]])

                def act_(h, r, kt, sc, es):
                    a = kt - 4 * r
                    ps, bps = (pa, b_pa2) if sc == 0 else (pb, b_pb2)
                    S.op(ACT, lambda e: e.activation(out=Eb[:, es, :, :].rearrange("p c n -> p (c n)"), in_=ps[:, 0:1024], func=AF.Exp,
                                                     bias=colb[:, h, a + 28:a + 29], scale=0.125),
                         [bps[0], bps[1], b_tab], [b_E[es][0], b_E[es][1]])

                def av_(h, r, kt, es):
                    grp_first = (kt == 0) or (kt == 4 * r)
                    grp_last = (kt == 4 * r - 1) or (kt == NT - 1)
                    for c in range(2):
                        for j in range(4):
                            ap, bb = acc_ap(c, j)
                            S.op(PE, lambda e, c=c, j=j, ap=ap: e.matmul(ap, lhsT=Eb[:, es, c, j * 128:(j + 1) * 128], rhs=VA[:, kt, h, :], start=(grp_first and (c * 4 + j) % 3 == 0), stop=(grp_last and (c * 4 + j) in (2, 5, 7))),
                                 [b_E[es][c], b_v], [bb])
                    if grp_last:
                        t = 0 if kt == NT - 1 else 1
                        for bk in range(3):
                            w_ = 387 if bk < 2 else 258
                            S.op(DVE, lambda e, bk=bk, w_=w_: e.tensor_copy(out=Oul[:, t, bk, 0:w_], in_=pc[:, bk * 512:bk * 512 + w_]),
                                 [b_pc3[bk]], [b_O[t]])

                steps = [(h, r, kt) for h in range(NH) for r in (range(8) if h % 2 == 0 else range(7, -1, -1)) for kt in range(NT)]
                last_inr = {h: ((7, 31) if h % 2 == 0 else (0, 3)) for h in range(NH)}
                gen_tables(0)
                qk(0, 0, 0, 0)
                qk(0, 0, 1, 1)
                for si, (h, r, kt) in enumerate(steps):
                    sc, es = si % 2, si % 3
                    act_(h, r, kt, sc, es)
                    if si + 2 < len(steps):
                        h2_, r2_, kt2_ = steps[si + 2]
                        qk(h2_, r2_, kt2_, sc)
                        if (r2_, kt2_) == last_inr[h2_] and h2_ + 1 < NH:
                            gen_tables(h2_ + 1)
                    av_(h, r, kt, es)
                    if pend[0] is not None:
                        if kt == 1:
                            comb1(*pend[0])
                        elif kt == 22:
                            comb2(*pend[0])
                        elif kt == 26:
                            comb3(*pend[0])
                            pend[0] = None
                    if kt == NT - 1:
                        pend[0] = (h, r)
                comb1(*pend[0]); comb2(*pend[0]); comb3(*pend[0])
                S.barrier()
                if debug and "Oul" in debug:
                    for nm, tt_, bb_ in (("Oul", Oul, b_O[0]), ("num", num, b_num), ("o_t", o_t, b_o), ("ssb", ssb, b_ss)):
                        dd = S.dsem("dbg" + nm)
                        S.dma(SP, dbg[nm], tt_[:], [bb_], [], dd); out_sems.append(dd)
                if debug and "mixda" in debug:
                    dd = S.dsem("dbgB")
                    S.dma(SP, dbg["mixda"], mixT[:, 0:4, :], b_mix[0:4], [], dd); out_sems.append(dd)
                    S.barrier()
                S.flush()
            S.st = phab
            if debug and debug.get("_stop") == "B":
                S.barrier(); S.flush(out_sems)
                return nc
        S.st = st
        S.barrier()
        Wdn = S.sb("Wdn", [128, NCH, D], BF16)
        b_Wdn = S.buf("Wdn")
        with ExitStack() as phd:
            S.st = phd
            Z = S.sb("Z", [128, NT, 4, 256], BF16)
            stgD = [S.sb("stgD%d" % i, [128, D], F32) for i in range(2)]
            b_stgD = [S.buf(), S.buf()]; d_stgD = [S.dsem("stgD0"), S.dsem("stgD1")]
            WC = S.sb("WC", [128, 256], BF16)
            ci = S.sb("ci", [128, 128], I32)
            ci2 = S.sb("ci2", [128, 128], I32)
            pcol = S.sb("pcol", [128, 1], F32)
            kcol_i = S.sb("kcol_i", [128, NT], I32)
            kcol = S.sb("kcol", [128, NT], F32)
            jr_i = S.sb("jr_i", [128, 512], I32)
            Pi = [S.sb("Pi%d" % i, [128, 512], I32) for i in range(2)]
            Ci = [S.sb("Ci%d" % i, [128, 512], I32) for i in range(2)]
            tab = [S.sb("tab%d" % i, [128, 2, 512], BF16) for i in range(3)]
            Fsb = S.sb("Fsb", [128, 4, 512], BF16)
            b_Z, b_WC, b_k, b_jr, b_F = S.buf(), S.buf(), S.buf(), S.buf(), S.buf()
            b_Pi = [S.buf(), S.buf()]; b_Ci = [S.buf(), S.buf()]; b_tab = [S.buf(), S.buf(), S.buf()]
            SC128 = 2.0 * PI_IN / 128.0
            SC4096 = 2.0 * PI_IN / 4096.0
            S.op(POOL, lambda e: e.iota(ci[:], pattern=[[0, 128]], base=0, channel_multiplier=1), [], [b_WC])
            S.op(DVE, lambda e: e.tensor_copy(out=pcol[:], in_=ci[:, 0:1]), [b_WC], [b_WC])
            S.op(POOL, lambda e: e.iota(ci[:], pattern=[[1, 128]], base=0, channel_multiplier=0), [b_WC], [b_WC])
            S.op(DVE, lambda e: e.tensor_scalar(out=ci[:], in0=ci[:], scalar1=pcol[:, 0:1], scalar2=None, op0=ALU.mult), [b_WC], [b_WC])
            for half, add in ((0, 96), (1, 64)):
                S.op(DVE, lambda e, add=add: e.tensor_single_scalar(out=ci2[:], in_=ci[:], scalar=add, op=ALU.add), [b_WC], [b_WC])
                S.op(DVE, lambda e: e.tensor_single_scalar(out=ci2[:], in_=ci2[:], scalar=127, op=ALU.bitwise_and), [b_WC], [b_WC])
                S.op(ACT, lambda e, half=half: e.activation(out=WC[:, half * 128:(half + 1) * 128], in_=ci2[:], func=AF.Sin, scale=SC128, bias=nbias[:, 0:1]), [b_WC], [b_WC])
            S.op(POOL, lambda e: e.iota(kcol_i[:], pattern=[[128, NT]], base=0, channel_multiplier=1), [], [b_k])
            S.op(DVE, lambda e: e.tensor_copy(out=kcol[:], in_=kcol_i[:]), [b_k], [b_k])
            for t in range(NT):
                pz, bpz = (pa, b_pa2) if t % 2 == 0 else (pb, b_pb2)
                for g in range(4):
                    S.op(PE, lambda e, t=t, g=g, pz=pz: e.matmul(pz[:, g * 256:(g + 1) * 256], lhsT=mixT[:, 4 + g, t * 128:(t + 1) * 128], rhs=WC[:], start=(g % 2 == 0), stop=(g % 2 == 1)),
                         [b_mix[4 + g], b_WC], [bpz[g // 2]], inc=(g == 3))
                if t % 2 == 0:
                    S.op(ACT, lambda e, t=t, pz=pz: e.activation(out=Z[:, t, :, :].rearrange("p g n -> p (g n)"), in_=pz[:], func=AF.Copy), bpz, [b_Z])
                else:
                    S.op(DVE, lambda e, t=t, pz=pz: e.tensor_copy(out=Z[:, t, :, :].rearrange("p g n -> p (g n)"), in_=pz[:]), bpz, [b_Z])
            S.barrier()
            accD = [(pa, 0, b_pa2[0]), (pa, 512, b_pa2[1]), (pb, 0, b_pb2[0]), (pb, 512, b_pb2[1])]
            cnt = 0
            rotw = 0
            wdn_i = 0
            for jr in range(8):
                S.op(POOL, lambda e, jr=jr: e.iota(jr_i[:], pattern=[[1, 512]], base=jr * 512, channel_multiplier=0), [], [b_jr])
                for kt in range(NT):
                    s2, s3 = cnt % 2, cnt % 3
                    cnt += 1
                    if kt % 4 == 0 and wdn_i < NCH:
                        q_ = wdn_i % 2
                        S.dma(SP, stgD[q_][:], wdn_d[wdn_i * 128:(wdn_i + 1) * 128, :], [], [b_stgD[q_]], d_stgD[q_])
                        S.op(POOL, lambda e, q_=q_, wi=wdn_i: e.tensor_copy(out=Wdn[:, wi, :], in_=stgD[q_][:]), [b_stgD[q_]], [b_Wdn])
                        wdn_i += 1
                    S.op(DVE, lambda e, kt=kt, s2=s2: e.tensor_scalar(out=Pi[s2][:], in0=jr_i[:], scalar1=kcol[:, kt:kt + 1], scalar2=None, op0=ALU.mult), [b_jr, b_k], [b_Pi[s2]])
                    S.op(DVE, lambda e, kt=kt, s2=s2: e.tensor_scalar(out=Ci[s2][:], in0=jr_i[:], scalar1=kcol[:, kt:kt + 1], scalar2=3072.0, op0=ALU.mult, op1=ALU.add), [b_jr, b_k], [b_Ci[s2]])
                    S.op(DVE, lambda e, s2=s2: e.tensor_single_scalar(out=Pi[s2][:], in_=Pi[s2][:], scalar=4095, op=ALU.bitwise_and), [b_Pi[s2]], [b_Pi[s2]])
                    S.op(DVE, lambda e, s2=s2: e.tensor_single_scalar(out=Ci[s2][:], in_=Ci[s2][:], scalar=4095, op=ALU.bitwise_and), [b_Ci[s2]], [b_Ci[s2]])
                    S.op(ACT, lambda e, s2=s2, s3=s3: e.activation(out=tab[s3][:, 0, :], in_=Ci[s2][:], func=AF.Sin, scale=SC4096, bias=nbias[:, 0:1]), [b_Ci[s2]], [b_tab[s3]])
                    S.op(ACT, lambda e, s2=s2, s3=s3: e.activation(out=tab[s3][:, 1, :], in_=Pi[s2][:], func=AF.Sin, scale=SC4096, bias=nbias[:, 0:1]), [b_Pi[s2]], [b_tab[s3]])
                    for g in range(4):
                        pp, off, bb = accD[g]
                        S.op(PE, lambda e, kt=kt, g=g, pp=pp, off=off, s3=s3: e.matmul(pp[:, off:off + 512], lhsT=Z[:, kt, g, 0:128], rhs=tab[s3][:, 0, :], start=(kt == 0), stop=False),
                             [b_Z, b_tab[s3]], [bb], inc=False)
                        S.op(PE, lambda e, kt=kt, g=g, pp=pp, off=off, s3=s3: e.matmul(pp[:, off:off + 512], lhsT=Z[:, kt, g, 128:256], rhs=tab[s3][:, 1, :], start=False, stop=(kt == NT - 1)),
                             [b_Z, b_tab[s3]], [bb], inc=(kt == NT - 1 or g == 3))
                for g in range(4):
                    pp, off, bb = accD[g]
                    S.op(DVE, lambda e, g=g, pp=pp, off=off: e.tensor_scalar(out=Fsb[:, g, :], in0=pp[:, off:off + 512], scalar1=FT_NORM, scalar2=None, op0=ALU.mult), [bb], [b_F])
                for g in range(4):
                    bank = rotw % 3; rotw += 1
                    S.op(PE, lambda e, g=g, bank=bank: e.matmul(pc[:, bank * 512:(bank + 1) * 512], lhsT=wft[:, g, :], rhs=Fsb[:, g, :], start=True, stop=True), [b_F, b_const], [b_pc3[bank]])
                    S.op(DVE, lambda e, g=g, bank=bank, jr=jr: e.tensor_copy(out=mixT[:, 4 + g, jr * 512:(jr + 1) * 512], in_=pc[:, bank * 512:(bank + 1) * 512]), [b_pc3[bank]], [b_mix[4 + g]])
            S.barrier()
            if debug and "mixft" in debug:
                dd = S.dsem("dbgD")
                S.dma(SP, dbg["mixft"], mixT[:, 4:8, :], b_mix[4:8], [], dd); out_sems.append(dd)
                S.barrier()
            S.flush()
        S.st = st
        if debug and debug.get("_stop") == "D":
            S.barrier(); S.flush(out_sems)
            return nc
        S.barrier()
        with ExitStack() as phf:
            S.st = phf
            NX1 = 5
            Wout = S.sb("Wout", [128, KC, D], BF16)
            gfin = S.sb("gfin", [128, D], F32)
            x1 = S.sb("x1", [128, NX1, D], F32)
            xn2 = [S.sb("xn2_%d" % i, [128, D], BF16) for i in range(2)]
            h2T = S.sb("h2T", [128, KC, 514], BF16)
            aT = S.sb("aT", [128, NCH, 512], BF16)
            wup = [S.sb("wup%d" % i, [128, 2, KC, 128], BF16) for i in range(3)]
            tgv = [S.sb("tgv%d" % i, [128, 2, 256], F32) for i in range(3)]
            junkF2 = S.sb("junkF2", [128, D], BF16)
            ssE = S.sb("ssE", [128, 4], F32)
            ssF = S.sb("ssF", [128, 8], F32)
            hr = S.sb("hr", [128, KC, 8], BF16)
            b_hr = S.buf()
            b_Wout, b_gf, b_h2T = (S.buf() for _ in range(3))
            b_xn2 = [S.buf(), S.buf()]
            b_ssF = [S.buf() for _ in range(8)]
            b_x1 = [S.buf() for _ in range(NX1)]
            b_aT = [[S.buf(), S.buf()] for _ in range(NCH)]
            b_wup = [S.buf(), S.buf(), S.buf()]; b_tgv = [[S.buf(), S.buf()] for _ in range(3)]; b_junk2 = S.buf(); b_ssE = [S.buf() for _ in range(4)]
            d_gf, d_xt, d_stg = S.dsem("gf"), S.dsem("xtF"), S.dsem("stgF")
            d_wup = [S.dsem("wup0"), S.dsem("wup1"), S.dsem("wup2")]
            d_y = [S.dsem("y%d" % i) for i in range(NX1)]
            out_sems.extend(d_y)
            S.dma(SP, gfin[:], gfin_d.partition_broadcast(128), [], [b_gf], d_gf)
            b_stg = [S.buf(), S.buf()]; d_stg2 = [S.dsem("stgF0"), S.dsem("stgF1")]
            wl = 0
            for kc in range(KC):
                s_ = wl % 2; wl += 1
                S.dma(SP if s_ == 0 else POOL, x1[:, s_, :], wout_d[kc * 128:(kc + 1) * 128, :], [], [b_stg[s_]], d_stg2[s_])
                S.op(DVE if s_ == 0 else ACT, (lambda e, kc=kc, s_=s_: e.tensor_copy(out=Wout[:, kc, :], in_=x1[:, s_, :])) if s_ == 0 else
                     (lambda e, kc=kc, s_=s_: e.activation(out=Wout[:, kc, :], in_=x1[:, s_, :], func=AF.Copy)), [b_stg[s_]], [b_Wout])
            for s_ in range(2):
                b_x1[s_].w = b_stg[s_].w; b_x1[s_].r = list(b_stg[s_].r)

            fcount = [0]

            def frontA(tok0, n, x1dst, b_dst, step=1):
                k = fcount[0]; fcount[0] += 1
                tsl = slice(tok0, tok0 + n * step, step)
                sl_, sc_ = k % 2, k % 8
                pp, bpp = (pa, b_pa2) if k % 2 == 0 else (pb, b_pb2)
                S.dma(SP, x1dst, x_d[tsl, :], [], [b_dst], d_xt)
                for half in range(2):
                    for kc in range(KC):
                        S.op(PE, lambda e, half=half, kc=kc: e.matmul(pp[0:n, half * 512:(half + 1) * 512], lhsT=mixT[:, kc, tsl], rhs=Wout[:, kc, half * 512:(half + 1) * 512], start=(kc == 0), stop=(kc == KC - 1)),
                             b_mix + [b_Wout], [bpp[half]], inc=(kc == KC - 1))
                S.op(DVE, lambda e: e.tensor_tensor(out=x1dst, in0=x1dst, in1=pp[0:n, :], op=ALU.add), [b_dst] + bpp, [b_dst])
                S.op(ACT, lambda e: e.activation(out=xn2[sl_][0:n, :], in_=x1dst, func=AF.Square, accum_out=ssF[0:n, sc_:sc_ + 1]), [b_dst], [b_xn2[sl_], b_ssF[sc_]])
                emit_rstd(ssF[0:n, sc_:sc_ + 1], ssF[0:n, sc_:sc_ + 1], D, EPS, [b_ssF[sc_]])
                S.op(DVE, lambda e: e.tensor_scalar(out=xn2[sl_][0:n, :], in0=x1dst, scalar1=ssF[0:n, sc_:sc_ + 1], scalar2=None, op0=ALU.mult), [b_dst, b_ssF[sc_]], [b_xn2[sl_]])
                return sl_

            def frontB(sl_, n, col, dstT=None, b_dT=None):
                dstT = h2T if dstT is None else dstT
                b_dT = b_h2T if b_dT is None else b_dT
                for kc in range(KC):
                    S.op(PE, lambda e, kc=kc: e.transpose(pt[:, kc * 128:kc * 128 + n], xn2[sl_][0:n, kc * 128:(kc + 1) * 128], ident[0:n, 0:n]), [b_xn2[sl_], b_const], [b_pt], inc=(kc == KC - 1))
                S.op(ACT, lambda e: e.activation(out=dstT[:, :, col:col + n], in_=pt[:].rearrange("p (k m) -> p k m", k=KC)[:, :, 0:n], func=AF.Copy), [b_pt], [b_dT])

            def x1slot(g):
                return x1[:, g % NX1, :], b_x1[g % NX1]

            S.op(DVE, lambda e: e.memset(h2T[:, :, 0:1], 0.0), [], [b_h2T])
            for j in range(4):
                xs, bx = x1slot(j)
                frontB(frontA(j * 128, 128, xs, bx), 128, 1 + j * 128)
            frontB(frontA(512, 7, x1[0:7, 4, :], b_x1[4], step=512), 7, 0, hr, b_hr)
            S.op(DVE, lambda e: e.tensor_copy(out=h2T[:, :, 513:514], in_=hr[:, :, 0:1]), [b_hr, b_h2T], [b_h2T])

            cw = [0]
            cs = [0]
            for b in range(8):
                t0 = b * 512
                prev = None
                units = [(i, half) for i in range(NCH) for half in range(2)]
                for unit in units + [None]:
                    cur = None
                    if unit is not None:
                        i, half = unit
                        if half == 0:
                            n_ = cw[0]; cw[0] += 1
                            ws = n_ % 3
                            if n_ == 0:
                                for q_ in range(2):
                                    S.dma(SP, wup[q_][:].rearrange("p a k c -> p (a k c)"), wup_s[q_, :, :], [], [b_wup[q_]], d_wup[q_])
                            if n_ + 2 < 8 * NCH:
                                q_ = (n_ + 2) % 3
                                S.dma(SP, wup[q_][:].rearrange("p a k c -> p (a k c)"), wup_s[(n_ + 2) % NCH, :, :], [], [b_wup[q_]], d_wup[q_])
                        c0 = half * 256
                        u = cs[0]; cs[0] += 1
                        s_, s3 = u % 2, u % 3
                        pp, bpp = (pa, b_pa2) if s_ == 0 else (pb, b_pb2)
                        for gv in range(2):
                            for kc in range(KC):
                                S.op(PE, lambda e, gv=gv, kc=kc, ws=ws, c0=c0, pp=pp: e.matmul(pp[:, gv * 512:gv * 512 + 258], lhsT=wup[ws][:, gv, kc, :], rhs=h2T[:, kc, c0:c0 + 258], start=(kc == 0), stop=(kc == KC - 1)),
                                     [b_wup[ws], b_h2T], [bpp[gv]], inc=(kc == KC - 1))
                        for gv in range(2):
                            ch = i + gv * NCH
                            S.op(ACT, lambda e, gv=gv, ch=ch, s3=s3, pp=pp: e.activation(out=tgv[s3][:, gv, :], in_=pp[:, gv * 512 + 1:gv * 512 + 257], func=AF.Identity, scale=wcv[:, 1, ch:ch + 1], bias=wcv[:, 3, ch:ch + 1]),
                                 [bpp[gv], b_const], [b_tgv[s3][gv]])
                        for tap, off in ((0, 0), (2, 2)):
                            for gv in range(2):
                                ch = i + gv * NCH
                                S.op(DVE, lambda e, gv=gv, ch=ch, s3=s3, pp=pp, tap=tap, off=off: e.scalar_tensor_tensor(out=tgv[s3][:, gv, :], in0=pp[:, gv * 512 + off:gv * 512 + off + 256], scalar=wcv[:, tap, ch:ch + 1], in1=tgv[s3][:, gv, :], op0=ALU.mult, op1=ALU.add),
                                     [bpp[gv], b_const, b_tgv[s3][gv]], [b_tgv[s3][gv]])
                        cur = (i, half, s3, s_)
                    if prev is not None:
                        pi_, ph_, p3, p2 = prev
                        S.op(ACT, lambda e, p3=p3: e.activation(out=tgv[p3][:, 0, :], in_=tgv[p3][:, 0, :], func=AF.Silu), [b_tgv[p3][0]], [b_tgv[p3][0]])
                        S.op(POOL, lambda e, p3=p3, pi_=pi_, ph_=ph_: e.tensor_tensor(out=aT[:, pi_, ph_ * 256:ph_ * 256 + 256], in0=tgv[p3][:, 0, :], in1=tgv[p3][:, 1, :], op=ALU.mult),
                             [b_tgv[p3][0], b_tgv[p3][1]], [b_aT[pi_][ph_]])
                    prev = cur
                if b < 7:
                    S.op(DVE, lambda e: e.tensor_copy(out=h2T[:, :, 0:1], in_=h2T[:, :, 512:513]), [b_h2T], [b_h2T])
                    if b < 6:
                        S.op(DVE, lambda e, b=b: e.tensor_copy(out=h2T[:, :, 513:514], in_=hr[:, :, b + 1:b + 2]), [b_hr, b_h2T], [b_h2T])
                    else:
                        S.op(DVE, lambda e: e.memset(h2T[:, :, 513:514], 0.0), [b_h2T], [b_h2T])
                pendB = None
                for j in range(4):
                    g = 4 * b + j
                    xs, bx = x1slot(g)
                    for half in range(2):
                        for i in range(NCH):
                            S.op(PE, lambda e, j=j, half=half, i=i: e.matmul(pc[:, half * 512:(half + 1) * 512], lhsT=aT[:, i, j * 128:(j + 1) * 128], rhs=Wdn[:, i, half * 512:(half + 1) * 512], start=(i == 0), stop=(i == NCH - 1)),
                                 [b_aT[i][j // 2], b_Wdn], [b_pc3[half]], inc=(i == NCH - 1))
                    S.op(DVE, lambda e, xs=xs: e.tensor_tensor(out=xs, in0=xs, in1=pc[:, 0:1024], op=ALU.add), [b_pc3[0], b_pc3[1], bx], [bx])
                    if b < 7:
                        xs2, bx2 = x1slot(g + 4)
                        slA = frontA(t0 + 512 + j * 128, 128, xs2, bx2)
                    if pendB is not None:
                        frontB(*pendB)
                    pendB = (slA, 128, 1 + j * 128) if b < 7 else None
                    S.op(ACT, lambda e, xs=xs, j=j: e.activation(out=junkF2[:], in_=xs, func=AF.Square, accum_out=ssE[:, j:j + 1]), [bx], [b_junk2, b_ssE[j]])
                    emit_rstd(ssE[:, j:j + 1], ssE[:, j:j + 1], D, EPS, [b_ssE[j]])
                    S.op(DVE, lambda e, xs=xs, j=j: e.scalar_tensor_tensor(out=xs, in0=xs, scalar=ssE[:, j:j + 1], in1=gfin[:], op0=ALU.mult, op1=ALU.mult), [bx, b_ssE[j], b_gf], [bx])
                    S.dma(SP, y_d[t0 + j * 128:t0 + (j + 1) * 128, :], xs, [bx], [], d_y[g % NX1])
                if pendB is not None:
                    frontB(*pendB)
            S.barrier()
            S.flush(out_sems)
        S.st = st
    return nc


def kernel(**inputs):
    nc = build_program()
    x = np.asarray(inputs["x"], dtype=np.float32)
    B = x.shape[0]
    shared = {}
    for k, v in inputs.items():
        if k == "x":
            continue
        a = np.asarray(v, dtype=np.float32)
        shared[k] = np.ascontiguousarray(a[0] if k != "g_final" else a)
    in_maps = []
    for b in range(B):
        m = dict(shared)
        m["x"] = np.ascontiguousarray(x[b])
        in_maps.append(m)
    res = run_bass_kernel_spmd(nc, in_maps, core_ids=list(range(B)))
    return np.stack([np.asarray(r["y"], dtype=np.float32) for r in res.results], axis=0)
```

```python
import math
from contextlib import ExitStack
import numpy as np
import concourse.bass as bass
import concourse.mybir as mybir
from concourse.bass_utils import run_bass_kernel_spmd

F32 = mybir.dt.float32
BF16 = mybir.dt.bfloat16
F16 = mybir.dt.float16
I32 = mybir.dt.int32
AF = mybir.ActivationFunctionType
ALU = mybir.AluOpType
AX = mybir.AxisListType


class _Res:
    def __init__(self, name, sem, attr=None):
        self.name, self.sem, self.attr = name, sem, attr
        self.count = 0
        self.pos = 0
        self.flushed = 0
        self.prog = []
        self.seen = {}


class _Buf:
    def __init__(self, name):
        self.name = name
        self.w = None
        self.r = []


class Sched:
    def __init__(self, nc, stack):
        self.nc, self.st = nc, stack
        mk = lambda n, a: _Res(n, stack.enter_context(nc.semaphore("s_" + n)), a)
        self.PE, self.ACT, self.DVE = mk("pe", "tensor"), mk("act", "scalar"), mk("dve", "vector")
        self.POOL, self.SP = mk("pool", "gpsimd"), mk("sp", "sync")
        self.engines = [self.PE, self.ACT, self.DVE, self.POOL, self.SP]
        self.n_instr = 0

    def sb(self, name, shape, dt):
        return self.st.enter_context(self.nc.sbuf_tensor(name, shape, dt))

    def ps(self, name, shape, dt):
        return self.st.enter_context(self.nc.psum_tensor(name, shape, dt))

    def buf(self, name="b"):
        return _Buf(name)

    def dsem(self, name):
        return _Res(name, self.st.enter_context(self.nc.semaphore("d_" + name)))

    def _waits(self, E, reads, writes, extra=()):
        deps = {}
        def add(tok):
            if tok is None:
                return
            r, v = tok
            if r is E and E is self.PE:
                return
            if deps.get(r, 0) < v:
                deps[r] = v
        for b in reads:
            add(b.w)
        for b in writes:
            add(b.w)
            for t in b.r:
                add(t)
        for t in extra:
            add(t)
        for r, v in deps.items():
            if E.seen.get(r, 0) < v:
                E.seen[r] = v
                E.prog.append(("w", r, v))

    def _mark(self, tok, reads, writes):
        for b in writes:
            b.w = tok
            b.r = []
        for b in reads:
            if b not in writes:
                b.r.append(tok)
        self.n_instr += 1

    def op(self, E, fn, reads, writes, inc=True):
        self._waits(E, reads, writes)
        E.pos += 1
        tok = (E, E.pos)
        E.prog.append(("i", fn, E.pos))
        self._mark(tok, reads, writes)
        return tok

    def dma(self, E, out, in_, reads, writes, ds, **kw):
        prev = (ds, ds.count) if ds.count else None
        self._waits(E, reads, writes, extra=(prev,) if prev else ())
        ds.count += 16
        tok = (ds, ds.count)
        E.prog.append(("d", lambda e: e.dma_start(out=out, in_=in_, **kw), ds))
        self._mark(tok, reads, writes)
        return tok

    def barrier(self):
        for E in self.engines:
            for R in self.engines:
                if R is self.SP or (R is E and E is self.PE) or R.pos == 0:
                    continue
                if R.prog and R.prog[-1][0] == "d":
                    pass
                if E.seen.get(R, 0) < R.pos:
                    E.seen[R] = R.pos
                    E.prog.append(("w", R, R.pos))

    def flush(self, final_dsems=()):
        self.barrier()
        for ds in final_dsems:
            if ds.count:
                self.SP.prog.append(("w", ds, ds.count))
        nc = self.nc
        comp = [E for E in self.engines]
        needed = {E: set() for E in comp}
        for E in comp:
            for it in E.prog:
                if it[0] == "w" and it[1] in needed and it[2] > it[1].flushed:
                    needed[it[1]].add(it[2])
        rank = {}
        for E in comp:
            for k, p in enumerate(sorted(needed[E])):
                rank[(E, p)] = E.count + k + 1
        progs = {}
        for E in comp:
            out = []
            for it in E.prog:
                if it[0] == "w":
                    r, v = it[1], it[2]
                    if r in needed:
                        if v <= r.flushed:
                            continue
                        out.append(("w", r.sem, rank[(r, v)]))
                    else:
                        out.append(("w", r.sem, v))
                elif it[0] == "i":
                    out.append(("i", it[1], E.sem if (E, it[2]) in rank else None, 1))
                else:
                    out.append(("i", it[1], it[2].sem, 16))
            progs[E.attr] = out
        for E in comp:
            E.count += len(needed[E])
            E.flushed = E.pos
            E.prog = []

        def runner(prog):
            def _(eng):
                for it in prog:
                    if it[0] == "w":
                        eng.wait_ge(it[1], it[2])
                    else:
                        ins = it[1](eng)
                        if it[2] is not None:
                            ins.then_inc(it[2], it[3])
            return _
        with nc.Block() as block:
            block.tensor(runner(progs["tensor"]))
            block.scalar(runner(progs["scalar"]))
            block.vector(runner(progs["vector"]))
            block.gpsimd(runner(progs["gpsimd"]))
            block.sync(runner(progs["sync"]))

    def finish(self, dsems):
        self.flush(dsems)


S_LEN = 4096
D = 1024
NT = S_LEN // 128
KC = D // 128
NH = 4
DFF = 2816
NCH = DFF // 128
EPS = 1e-6
SUBLN_EPS = 1e-5
LAMBDA_INIT = 0.8 - 0.6 * math.exp(0.0)
SLOPES = [2.0 ** (-8.0 * (h + 1) / NH) for h in range(NH)]
PI_IN = 3.1415925
FT_NORM = 1.0 / math.sqrt(4096.0 * 128.0)


def build_program(debug=None):
    nc = bass.Bass("TRN2", target_bir_lowering=False)
    dr = lambda n, shp, dt=F32, kind="ExternalInput": nc.dram_tensor(n, shp, dt, kind=kind).ap()
    x_d = dr("x", [S_LEN, D])
    gmix_d = dr("g_mix", [D]); win_d = dr("w_in", [D, 2048])
    lq1_d, lk1_d, lq2_d, lk2_d = (dr(n, [64]) for n in ("lambda_q1", "lambda_k1", "lambda_q2", "lambda_k2"))
    gsub_d = dr("g_subln", [128]); wft_d = dr("w_ft", [4, 128, 128]); wout_d = dr("w_out", [D, D])
    gffn_d = dr("g_ffn", [D]); wup_d = dr("w_up", [D, 2 * DFF]); wconv_d = dr("w_conv", [3, 2 * DFF])
    bconv_d = dr("b_conv", [2 * DFF]); wdn_d = dr("w_down", [DFF, D]); gfin_d = dr("g_final", [D])
    y_d = dr("y", [S_LEN, D], F32, "ExternalOutput")
    wup_s = nc.dram_tensor("wup_s", [NCH, 128, 2 * KC * 128], BF16).ap()
    dbg = {}
    if debug:
        for n, (shp, dt) in debug.items():
            dbg[n] = dr("dbg_" + n, shp, dt, "ExternalOutput")

    with ExitStack() as st:
        S = Sched(nc, st)
        PE, ACT, DVE, POOL, SP = S.PE, S.ACT, S.DVE, S.POOL, S.SP
        out_sems = []

        mixT = S.sb("mixT", [128, 8, S_LEN], BF16)
        b_mix = [S.buf("mix%d" % i) for i in range(8)]
        ident = S.sb("ident", [128, 128], BF16)
        iota_i = S.sb("iota_i", [128, 128], I32)
        gsub = S.sb("gsub", [128, 128], F32)
        wcv = S.sb("wcv", [128, 4, 2 * NCH], F32)
        gcol = S.sb("gcol", [128, 2, KC], F32)
        lamt = S.sb("lamt", [128, 4, 64], F32)
        lams = S.sb("lams", [128, 8], F32)
        wft = S.sb("wft", [128, 4, 128], BF16)
        nbias = S.sb("nbias", [128, 1], F32)
        b_const = S.buf("const")
        pt = S.ps("pt", [128, 1024], BF16)
        pa = S.ps("pa", [128, 1024], F32)
        pb = S.ps("pb", [128, 1024], F32)
        pc = S.ps("pc", [128, 1536], F32)
        b_pt, b_pa, b_pb, b_pc = S.buf("pt"), S.buf("pa"), S.buf("pb"), S.buf("pc")
        b_pa2 = [S.buf("pa0"), S.buf("pa1")]
        b_pb2 = [S.buf("pb0"), S.buf("pb1")]
        b_pc3 = [S.buf("pc0"), S.buf("pc1"), S.buf("pc2")]
        dc = [S.dsem("c%d" % i) for i in range(4)]

        S.op(DVE, lambda e: e.memset(nbias[:], -PI_IN), [], [b_const])
        S.op(POOL, lambda e: e.iota(iota_i[:], pattern=[[1, 128]], base=0, channel_multiplier=-1), [], [b_const])
        S.op(DVE, lambda e: e.tensor_single_scalar(out=ident[:], in_=iota_i[:], scalar=0, op=ALU.is_equal), [b_const], [b_const])
        S.dma(SP, gsub[:], gsub_d.partition_broadcast(128), [], [b_const], dc[0])
        S.dma(SP, wcv[:, 0:3, :], wconv_d.rearrange("t (c p) -> p t c", p=128), [], [b_const], dc[2], allow_slow_non_contiguous=True)
        S.dma(SP, wcv[:, 3, :], bconv_d.rearrange("(c p) -> p c", p=128), [], [b_const], dc[3], allow_slow_non_contiguous=True)
        S.dma(SP, gcol[:, 0, :], gmix_d.rearrange("(c p) -> p c", p=128), [], [b_const], dc[0], allow_slow_non_contiguous=True)
        S.dma(SP, gcol[:, 1, :], gffn_d.rearrange("(c p) -> p c", p=128), [], [b_const], dc[1], allow_slow_non_contiguous=True)
        for i, ld in enumerate((lq1_d, lk1_d, lq2_d, lk2_d)):
            S.dma(SP, lamt[:, i, :], ld.partition_broadcast(128), [], [b_const], dc[2 + i % 2])
        S.op(DVE, lambda e: e.tensor_scalar(out=gsub[:], in0=gsub[:], scalar1=1.0 - LAMBDA_INIT, scalar2=None, op0=ALU.mult), [b_const], [b_const])
        S.op(DVE, lambda e: e.tensor_tensor(out=lamt[:, 0, :], in0=lamt[:, 0, :], in1=lamt[:, 1, :], op=ALU.mult), [b_const], [b_const])
        S.op(DVE, lambda e: e.tensor_tensor(out=lamt[:, 2, :], in0=lamt[:, 2, :], in1=lamt[:, 3, :], op=ALU.mult), [b_const], [b_const])
        S.op(DVE, lambda e: e.reduce_sum(out=lams[:, 0:1], in_=lamt[:, 0, :], axis=AX.X), [b_const], [b_const])
        S.op(DVE, lambda e: e.reduce_sum(out=lams[:, 1:2], in_=lamt[:, 2, :], axis=AX.X), [b_const], [b_const])
        S.op(ACT, lambda e: e.activation(out=lams[:, 2:4], in_=lams[:, 0:2], func=AF.Exp), [b_const], [b_const])
        S.op(DVE, lambda e: e.scalar_tensor_tensor(out=lams[:, 4:5], in0=lams[:, 3:4], scalar=-LAMBDA_INIT, in1=lams[:, 2:3], op0=ALU.add, op1=ALU.subtract), [b_const], [b_const])

        with ExitStack() as ph:
            S.st = ph
            stg = [S.sb("stg%d" % i, [128, 2 * DFF], F32) for i in range(2)]
            wbig = S.sb("wbig", [128, NCH, 2, KC, 128], BF16)
            b_stg = [S.buf(), S.buf()]; b_wbig = S.buf()
            dl = [S.dsem("wl0"), S.dsem("wl1")]; dsv = [S.dsem("ws0"), S.dsem("ws1")]
            S.dma(SP, stg[0][:, 0:512].rearrange("p (g d) -> p g d", g=4), wft_d.rearrange("g c d -> c g d"), [], [b_stg[0]], dl[0])
            S.op(DVE, lambda e: e.tensor_copy(out=wft[:].rearrange("p g d -> p (g d)"), in_=stg[0][:, 0:512]), [b_stg[0]], [b_const])
            win_sb = mixT[:, 0:4, :].rearrange("p a (b n) -> p (a b) n", n=2048)
            for kc in range(KC):
                s_ = kc % 2
                S.dma(SP, stg[s_][:, 0:2048], win_d[kc * 128:(kc + 1) * 128, :], [], [b_stg[s_]], dl[s_])
                if s_ == 0:
                    S.op(DVE, lambda e, kc=kc, s_=s_: e.tensor_scalar(out=win_sb[:, kc, :], in0=stg[s_][:, 0:2048], scalar1=gcol[:, 0, kc:kc + 1], scalar2=None, op0=ALU.mult),
                         [b_stg[s_], b_const], b_mix[0:4])
                else:
                    S.op(ACT, lambda e, kc=kc, s_=s_: e.activation(out=win_sb[:, kc, :], in_=stg[s_][:, 0:2048], func=AF.Copy, scale=gcol[:, 0, kc:kc + 1]),
                         [b_stg[s_], b_const], b_mix[0:4])
            for kc in range(KC):
                s_ = kc % 2
                S.dma(SP, stg[s_][:], wup_d[kc * 128:(kc + 1) * 128, :], [], [b_stg[s_]], dl[s_])
                src = lambda gv, s_=s_: stg[s_][:, gv * DFF:(gv + 1) * DFF].rearrange("p (i ch) -> p i ch", ch=128)
                S.op(DVE, lambda e, kc=kc, s_=s_: e.tensor_scalar(out=wbig[:, :, 0, kc, :], in0=stg[s_][:, 0:DFF].rearrange("p (i ch) -> p i ch", ch=128), scalar1=gcol[:, 1, kc:kc + 1], scalar2=None, op0=ALU.mult),
                     [b_stg[s_], b_const], [b_wbig])
                S.op(ACT, lambda e, kc=kc, s_=s_: e.activation(out=wbig[:, :, 1, kc, :], in_=stg[s_][:, DFF:2 * DFF].rearrange("p (i ch) -> p i ch", ch=128), func=AF.Copy, scale=gcol[:, 1, kc:kc + 1]),
                     [b_stg[s_], b_const], [b_wbig])
            for q in range(2):
                lo, hi = q * (NCH // 2), (q + 1) * (NCH // 2)
                S.dma(SP, wup_s[lo:hi, :, :].rearrange("i p n -> p i n"), wbig[:, lo:hi, :, :, :].rearrange("p i a k c -> p i (a k c)"), [b_wbig], [], dsv[q])
            wups_done = [(dsv[0], dsv[0].count), (dsv[1], dsv[1].count)]
            S.barrier()
            for E in S.engines:
                for r, v in wups_done:
                    E.seen[r] = v
                    E.prog.append(("w", r, v))
            S.flush()
        S.st = st

        def emit_rstd(dst, ss, dim, eps, bufs):
            S.op(DVE, lambda e: e.tensor_scalar(out=dst, in0=ss, scalar1=1.0 / dim, scalar2=eps, op0=ALU.mult, op1=ALU.add), bufs, bufs)
            S.op(ACT, lambda e: e.activation(out=dst, in_=dst, func=AF.Ln), bufs, bufs)
            S.op(ACT, lambda e: e.activation(out=dst, in_=dst, func=AF.Exp, scale=-0.5), bufs, bufs)

        with ExitStack() as phab:
            S.st = phab
            QT = S.sb("QT", [128, NH, S_LEN], BF16)
            KT = S.sb("KT", [128, NH, S_LEN], BF16)
            VA = S.sb("VA", [128, NT, NH, 129], BF16)
            UT = mixT[:, 4:8, :]
            b_qk = S.buf("qk"); b_v = S.buf("v"); b_ut = b_mix[4:8]
            b_qkw = [[S.buf() for _ in range(NH)] for _ in range(2)]
            b_vw = [S.buf() for _ in range(NT)]
            S.op(POOL, lambda e: e.memset(VA[:, :, :, 128:129], 1.0), [], [b_v])
            with ExitStack() as pha:
                S.st = pha
                xt = [S.sb("xt%d" % i, [128, D], F32) for i in range(2)]
                xn = [S.sb("xn%d" % i, [128, D], BF16) for i in range(2)]
                junk = S.sb("junkA", [128, D], BF16)
                ssq = S.sb("ssqA", [128, 2], F32)
                hT = [S.sb("hT%d" % i, [128, KC, 512], BF16) for i in range(2)]
                b_xt = [S.buf(), S.buf()]; b_xn = [S.buf(), S.buf()]; b_hT = [[S.buf() for _ in range(4)] for _ in range(2)]
                b_junk = S.buf(); b_ss = [S.buf(), S.buf()]
                dx = [S.dsem("xa0"), S.dsem("xa1")]
                win_sb = mixT[:, 0:4, :].rearrange("p a (b n) -> p (a b) n", n=2048)
                b_win = b_mix[0:4]
                rot = [0]
                def big_group(sp, oi):
                    hs = sp % 2
                    kind, hh = oi // 4, oi % 4
                    col0 = (0, 512, 1536)[kind] + hh * 128
                    bank = rot[0] % 3; rot[0] += 1
                    for kc in range(KC):
                        S.op(PE, lambda e, kc=kc: e.matmul(pc[:, bank * 512:(bank + 1) * 512], lhsT=win_sb[:, kc, col0:col0 + 128], rhs=hT[hs][:, kc, :], start=(kc == 0), stop=(kc == KC - 1)),
                             b_hT[hs] + b_win, [b_pc3[bank]])
                    dst = (QT, KT, None)[kind]
                    if kind < 2:
                        if oi % 2 == 0:
                            S.op(ACT, lambda e: e.activation(out=dst[:, hh, sp * 512:(sp + 1) * 512], in_=pc[:, bank * 512:(bank + 1) * 512], func=AF.Copy),
                                 [b_pc3[bank]], [b_qkw[kind][hh]])
                        else:
                            S.op(DVE, lambda e: e.tensor_copy(out=dst[:, hh, sp * 512:(sp + 1) * 512], in_=pc[:, bank * 512:(bank + 1) * 512]),
                                 [b_pc3[bank]], [b_qkw[kind][hh]])
                    else:
                        S.op(DVE, lambda e: e.tensor_copy(out=mixT[:, 4 + hh, sp * 512:(sp + 1) * 512], in_=pc[:, bank * 512:(bank + 1) * 512]),
                             [b_pc3[bank]], [b_mix[4 + hh]])

                def tile_p1(sp, tt):
                    t = sp * 4 + tt
                    s_ = t % 2
                    S.dma(SP, xt[s_][:], x_d[t * 128:(t + 1) * 128, :], [], [b_xt[s_]], dx[s_])
                    S.op(ACT, lambda e: e.activation(out=junk[:], in_=xt[s_][:], func=AF.Square, accum_out=ssq[:, s_:s_ + 1]),
                         [b_xt[s_]], [b_junk, b_ss[s_]])
                    emit_rstd(ssq[:, s_:s_ + 1], ssq[:, s_:s_ + 1], D, EPS, [b_ss[s_]])
                    S.op(DVE, lambda e: e.tensor_scalar(out=xn[s_][:], in0=xt[s_][:], scalar1=ssq[:, s_:s_ + 1], scalar2=None, op0=ALU.mult),
                         [b_xt[s_], b_ss[s_]], [b_xn[s_]])

                def tile_p2(sp, tt):
                    hs = sp % 2
                    t = sp * 4 + tt
                    s_ = t % 2
                    for kc in range(KC):
                        S.op(PE, lambda e, kc=kc: e.transpose(pt[:, kc * 128:(kc + 1) * 128], xn[s_][:, kc * 128:(kc + 1) * 128], ident[:]),
                             [b_xn[s_], b_const], [b_pt])
                    S.op(ACT, lambda e: e.activation(out=hT[hs][:, :, tt * 128:(tt + 1) * 128], in_=pt[:].rearrange("p (k n) -> p k n", k=KC), func=AF.Copy),
                         [b_pt], [b_hT[hs][tt]])

                def v_tile(sp, tt):
                    hs = sp % 2
                    t = sp * 4 + tt
                    pv, bpv = (pa, b_pa2) if t % 2 == 0 else (pb, b_pb2)
                    for kc in range(KC):
                        S.op(PE, lambda e, kc=kc: e.matmul(pv[:, 0:512], lhsT=hT[hs][:, kc, tt * 128:(tt + 1) * 128], rhs=win_sb[:, kc, 1024:1536], start=(kc == 0), stop=(kc == KC - 1)),
                             [b_hT[hs][tt]] + b_win, [bpv[0]])
                    S.op(DVE, lambda e: e.tensor_copy(out=VA[:, t, :, 0:128], in_=pv[:, 0:512].rearrange("p (h v) -> p h v", h=NH)),
                         [bpv[0]], [b_vw[t]])

                tile_p1(0, 0)
                for sp in range(NT // 4 + 1):
                    for tt in range(4):
                        if sp < NT // 4:
                            nt_ = sp * 4 + tt + 1
                            if nt_ < NT:
                                tile_p1(nt_ // 4, nt_ % 4)
                            tile_p2(sp, tt)
                        if sp > 0:
                            for oi in range(tt * 3, tt * 3 + 3):
                                big_group(sp - 1, oi)
                        if sp < NT // 4:
                            v_tile(sp, tt)
                S.barrier()
                if debug and "QT" in debug:
                    dd = S.dsem("dbgA")
                    S.dma(SP, dbg["QT"], QT[:], [b_qk], [], dd); out_sems.append(dd)
                    dd = S.dsem("dbgA2")
                    S.dma(SP, dbg["KT"], KT[:], [b_qk], [], dd); out_sems.append(dd)
                    dd = S.dsem("dbgA3")
                    S.dma(SP, dbg["VA"], VA[:], [b_v], [], dd); out_sems.append(dd)
                    dd = S.dsem("dbgA4")
                    S.dma(SP, dbg["UT"], mixT[:, 4:8, :], b_mix[4:8], [], dd); out_sems.append(dd)
                    S.barrier()
                S.flush()
            S.st = phab
            if debug and debug.get("_stop") == "A":
                S.barrier(); S.flush(out_sems)
                return nc
            with ExitStack() as phb:
                S.st = phb
                Rcol = S.sb("Rcol", [128, 60], I32)
                colb = S.sb("colb", [128, NH, 60], F32)
                R2 = S.sb("R2", [128, 4], I32)
                R2f = S.sb("R2f", [128, 4], F32)
                fpm = S.sb("fpm", [128, NH, 2, 4], F32)
                Qr = S.sb("Qr", [128, 512], F32)
                Ti = S.sb("Ti", [128, 512], I32)
                Tf = S.sb("Tf", [128, 512], F32)
                Bt = S.sb("Bt", [128, 4, 512], F16)
                ident16 = S.sb("ident16", [128, 128], F16)
                Eb = S.sb("Eb", [128, 3, 2, 512], BF16)
                Oul = S.sb("Oul", [128, 2, 3, 387], F32)
                num = S.sb("num", [128, 8, 129], F32)
                rden = S.sb("rden", [128, 8], F32)
                Pn = S.sb("Pn", [128, 8, 128], F32)
                o_t = S.sb("o_t", [128, 4, 128], F32)
                junkB = S.sb("junkB", [128, 128], F32)
                ssb = S.sb("ssb", [128, 4], F32)
                on_t = S.sb("on_t", [128, 4, 128], BF16)
                b_tab = S.buf("tab"); b_Bt = S.buf("Bt"); b_T = S.buf("T")
                b_E = [[S.buf(), S.buf()] for _ in range(3)]
                b_O = [S.buf("Oup"), S.buf("Olo")]
                b_num, b_P, b_o, b_ss, b_on = S.buf(), S.buf(), S.buf(), S.buf(), S.buf()
                zcol_i = 28

                S.op(POOL, lambda e: e.iota(Rcol[:], pattern=[[128, 60]], base=-28 * 128 - 256, channel_multiplier=1), [], [b_tab])
                S.op(POOL, lambda e: e.iota(R2[:], pattern=[[128, 4]], base=-256, channel_multiplier=1), [], [b_tab])
                S.op(POOL, lambda e: e.iota(Ti[:], pattern=[[1, 512]], base=-256, channel_multiplier=0), [], [b_T])
                S.op(DVE, lambda e: e.tensor_copy(out=Qr[:], in_=Ti[:]), [b_T], [b_tab, b_T])
                S.op(DVE, lambda e: e.tensor_copy(out=R2f[:], in_=R2[:]), [b_tab], [b_tab])
                S.op(DVE, lambda e: e.tensor_copy(out=ident16[:], in_=ident[:]), [b_const], [b_tab])
                for h in range(NH):
                    sl = SLOPES[h]
                    S.op(DVE, lambda e, h=h, sl=sl: e.tensor_scalar(out=colb[:, h, 0:28], in0=Rcol[:, 0:28], scalar1=sl, scalar2=None, op0=ALU.mult), [b_tab], [b_tab])
                    S.op(DVE, lambda e, h=h, sl=sl: e.tensor_scalar(out=colb[:, h, 28:60], in0=Rcol[:, 28:60], scalar1=-sl, scalar2=None, op0=ALU.mult), [b_tab], [b_tab])
                    S.op(DVE, lambda e, h=h: e.memset(colb[:, h, 28:32], 0.0), [b_tab], [b_tab])
                    S.op(ACT, lambda e, h=h, sl=sl: e.activation(out=fpm[:, h, 0, :], in_=R2f[:], func=AF.Exp, scale=sl), [b_tab], [b_tab])
                    S.op(ACT, lambda e, h=h, sl=sl: e.activation(out=fpm[:, h, 1, :], in_=R2f[:], func=AF.Exp, scale=-sl), [b_tab], [b_tab])

                def acc_ap(c, j, w=129):
                    idx = c * 4 + j
                    o0 = (idx // 3) * 512 + (idx % 3) * 129
                    return pc[:, o0:o0 + w], b_pc3[idx // 3]

                def reg(t, c, j, lo=0, hi=129):
                    idx = c * 4 + j
                    return Oul[:, t, idx // 3, (idx % 3) * 129 + lo:(idx % 3) * 129 + hi]

                def comb1(h, r):
                    for c in range(2):
                        for j in range(4):
                            idx = c * 4 + j
                            S.op(DVE, lambda e, c=c, j=j, idx=idx: e.tensor_scalar(out=num[:, idx, :], in0=reg(0, c, j), scalar1=fpm[:, h, 0, j:j + 1], scalar2=None, op0=ALU.mult),
                                 [b_O[0], b_tab], [b_num])
                            if r > 0:
                                S.op(DVE, lambda e, c=c, j=j, idx=idx: e.scalar_tensor_tensor(out=num[:, idx, :], in0=reg(1, c, j), scalar=fpm[:, h, 1, j:j + 1], in1=num[:, idx, :], op0=ALU.mult, op1=ALU.add),
                                     [b_O[1], b_tab, b_num], [b_num])
                    S.op(DVE, lambda e: e.reciprocal(out=rden[:], in_=num[:, :, 128]), [b_num], [b_num])
                    for idx in range(8):
                        S.op(DVE, lambda e, idx=idx: e.tensor_scalar(out=Pn[:, idx, :], in0=num[:, idx, 0:128], scalar1=rden[:, idx:idx + 1], scalar2=None, op0=ALU.mult),
                             [b_num], [b_P])
                    S.op(DVE, lambda e: e.scalar_tensor_tensor(out=o_t[:], in0=Pn[:, 4:8, :], scalar=lams[:, 4:5], in1=Pn[:, 0:4, :], op0=ALU.mult, op1=ALU.add),
                         [b_P, b_const], [b_o])
                    for j in range(4):
                        S.op(DVE, lambda e, j=j: e.scalar_tensor_tensor(out=junkB[:], in0=o_t[:, j, :], scalar=1.0, in1=o_t[:, j, :], op0=ALU.mult, op1=ALU.mult, accum_out=ssb[:, j:j + 1]),
                             [b_o], [b_ss])
                    S.op(DVE, lambda e: e.tensor_scalar(out=ssb[:], in0=ssb[:], scalar1=1.0 / 128, scalar2=SUBLN_EPS, op0=ALU.mult, op1=ALU.add), [b_ss], [b_ss])

                def comb2(h, r):
                    S.op(ACT, lambda e: e.activation(out=ssb[:], in_=ssb[:], func=AF.Ln), [b_ss], [b_ss])
                    S.op(ACT, lambda e: e.activation(out=ssb[:], in_=ssb[:], func=AF.Exp, scale=-0.5), [b_ss], [b_ss])

                def comb3(h, r):
                    for j in range(4):
                        S.op(DVE, lambda e, j=j: e.scalar_tensor_tensor(out=on_t[:, j, :], in0=o_t[:, j, :], scalar=ssb[:, j:j + 1], in1=gsub[:], op0=ALU.mult, op1=ALU.mult),
                             [b_o, b_ss, b_const], [b_on])
                    for j in range(4):
                        S.op(PE, lambda e, j=j: e.transpose(pt[:, j * 128:(j + 1) * 128], on_t[:, j, :], ident[:]), [b_on, b_const], [b_pt], inc=(j == 3))
                    S.op(DVE, lambda e: e.tensor_copy(out=mixT[:, h, r * 512:(r + 1) * 512], in_=pt[:, 0:512]), [b_pt], [b_mix[h]])

                pend = [None]
                def gen_tables(h):
                    sl = SLOPES[h]
                    for a in range(4):
                        S.op(POOL, lambda e, a=a: e.iota(Ti[:], pattern=[[-1, 512]], base=128 * a, channel_multiplier=1), [], [b_T])
                        S.op(DVE, lambda e: e.tensor_copy(out=Tf[:], in_=Ti[:]), [b_T], [b_T])
                        S.op(DVE, lambda e: e.scalar_tensor_tensor(out=Tf[:], in0=Tf[:], scalar=-1.0, in1=Tf[:], op0=ALU.mult, op1=ALU.max), [b_T], [b_T])
                        S.op(DVE, lambda e: e.tensor_tensor(out=Tf[:], in0=Tf[:], in1=Qr[:], op=ALU.add), [b_T, b_tab], [b_T])
                        S.op(DVE, lambda e: e.tensor_scalar(out=Tf[:], in0=Tf[:], scalar1=-8.0 * sl, scalar2=None, op0=ALU.mult), [b_T], [b_T])
                        S.op(DVE, lambda e, a=a: e.tensor_copy(out=Bt[:, a, :], in_=Tf[:]), [b_T], [b_Bt])

                def qk(h, r, kt, sc):
                    a = kt - 4 * r
                    inr = 0 <= a <= 3
                    ps, bps = (pa, b_pa2) if sc == 0 else (pb, b_pb2)
                    for c in range(2):
                        S.op(PE, lambda e, c=c: e.matmul(ps[:, c * 512:(c + 1) * 512], lhsT=KT[c * 64:(c + 1) * 64, h, kt * 128:(kt + 1) * 128],
                                                         rhs=QT[c * 64:(c + 1) * 64, h, r * 512:(r + 1) * 512], start=True, stop=not inr),
                             [b_qk], [bps[c]])
                    if inr:
                        for c in range(2):
                            S.op(PE, lambda e, c=c: e.matmul(ps[:, c * 512:(c + 1) * 512], lhsT=ident16[:], rhs=Bt[:, a, :], start=False, stop=True),
                                 [b_Bt, b_tab], [bps[c]])

                def act_(h, r, kt, sc, es):
                    a = kt - 4 * r
                    ps, bps = (pa, b_pa2) if sc == 0 else (pb, b_pb2)
                    S.op(ACT, lambda e: e.activation(out=Eb[:, es, :, :].rearrange("p c n -> p (c n)"), in_=ps[:, 0:1024], func=AF.Exp,
                                                     bias=colb[:, h, a + 28:a + 29], scale=0.125),
                         [bps[0], bps[1], b_tab], [b_E[es][0], b_E[es][1]])

                def av_(h, r, kt, es):
                    grp_first = (kt == 0) or (kt == 4 * r)
                    grp_last = (kt == 4 * r - 1) or (kt == NT - 1)
                    for c in range(2):
                        for j in range(4):
                            ap, bb = acc_ap(c, j)
                            S.op(PE, lambda e, c=c, j=j, ap=ap: e.matmul(ap, lhsT=Eb[:, es, c, j * 128:(j + 1) * 128], rhs=VA[:, kt, h, :], start=(grp_first and (c * 4 + j) % 3 == 0), stop=(grp_last and (c * 4 + j) in (2, 5, 7))),
                                 [b_E[es][c], b_v], [bb])
                    if grp_last:
                        t = 0 if kt == NT - 1 else 1
                        for bk in range(3):
                            w_ = 387 if bk < 2 else 258
                            S.op(DVE, lambda e, bk=bk, w_=w_: e.tensor_copy(out=Oul[:, t, bk, 0:w_], in_=pc[:, bk * 512:bk * 512 + w_]),
                                 [b_pc3[bk]], [b_O[t]])

                steps = [(h, r, kt) for h in range(NH) for r in (range(8) if h % 2 == 0 else range(7, -1, -1)) for kt in range(NT)]
                last_inr = {h: ((7, 31) if h % 2 == 0 else (0, 3)) for h in range(NH)}
                gen_tables(0)
                qk(0, 0, 0, 0)
                qk(0, 0, 1, 1)
                for si, (h, r, kt) in enumerate(steps):
                    sc, es = si % 2, si % 3
                    act_(h, r, kt, sc, es)
                    if si + 2 < len(steps):
                        h2_, r2_, kt2_ = steps[si + 2]
                        qk(h2_, r2_, kt2_, sc)
                        if (r2_, kt2_) == last_inr[h2_] and h2_ + 1 < NH:
                            gen_tables(h2_ + 1)
                    av_(h, r, kt, es)
                    if pend[0] is not None:
                        if kt == 1:
                            comb1(*pend[0])
                        elif kt == 22:
                            comb2(*pend[0])
                        elif kt == 26:
                            comb3(*pend[0])
                            pend[0] = None
                    if kt == NT - 1:
                        pend[0] = (h, r)
                comb1(*pend[0]); comb2(*pend[0]); comb3(*pend[0])
                S.barrier()
                if debug and "Oul" in debug:
                    for nm, tt_, bb_ in (("Oul", Oul, b_O[0]), ("num", num, b_num), ("o_t", o_t, b_o), ("ssb", ssb, b_ss)):
                        dd = S.dsem("dbg" + nm)
                        S.dma(SP, dbg[nm], tt_[:], [bb_], [], dd); out_sems.append(dd)
                if debug and "mixda" in debug:
                    dd = S.dsem("dbgB")
                    S.dma(SP, dbg["mixda"], mixT[:, 0:4, :], b_mix[0:4], [], dd); out_sems.append(dd)
                    S.barrier()
                S.flush()
            S.st = phab
            if debug and debug.get("_stop") == "B":
                S.barrier(); S.flush(out_sems)
                return nc
        S.st = st
        S.barrier()
        Wdn = S.sb("Wdn", [128, NCH, D], BF16)
        b_Wdn = S.buf("Wdn")
        with ExitStack() as phd:
            S.st = phd
            Z = S.sb("Z", [128, NT, 4, 256], BF16)
            stgD = [S.sb("stgD%d" % i, [128, D], F32) for i in range(2)]
            b_stgD = [S.buf(), S.buf()]; d_stgD = [S.dsem("stgD0"), S.dsem("stgD1")]
            WC = S.sb("WC", [128, 256], BF16)
            ci = S.sb("ci", [128, 128], I32)
            ci2 = S.sb("ci2", [128, 128], I32)
            pcol = S.sb("pcol", [128, 1], F32)
            kcol_i = S.sb("kcol_i", [128, NT], I32)
            kcol = S.sb("kcol", [128, NT], F32)
            jr_i = S.sb("jr_i", [128, 512], I32)
            Pi = [S.sb("Pi%d" % i, [128, 512], I32) for i in range(2)]
            Ci = [S.sb("Ci%d" % i, [128, 512], I32) for i in range(2)]
            tab = [S.sb("tab%d" % i, [128, 2, 512], BF16) for i in range(3)]
            Fsb = S.sb("Fsb", [128, 4, 512], BF16)
            b_Z, b_WC, b_k, b_jr, b_F = S.buf(), S.buf(), S.buf(), S.buf(), S.buf()
            b_Pi = [S.buf(), S.buf()]; b_Ci = [S.buf(), S.buf()]; b_tab = [S.buf(), S.buf(), S.buf()]
            SC128 = 2.0 * PI_IN / 128.0
            SC4096 = 2.0 * PI_IN / 4096.0
            S.op(POOL, lambda e: e.iota(ci[:], pattern=[[0, 128]], base=0, channel_multiplier=1), [], [b_WC])
            S.op(DVE, lambda e: e.tensor_copy(out=pcol[:], in_=ci[:, 0:1]), [b_WC], [b_WC])
            S.op(POOL, lambda e: e.iota(ci[:], pattern=[[1, 128]], base=0, channel_multiplier=0), [b_WC], [b_WC])
            S.op(DVE, lambda e: e.tensor_scalar(out=ci[:], in0=ci[:], scalar1=pcol[:, 0:1], scalar2=None, op0=ALU.mult), [b_WC], [b_WC])
            for half, add in ((0, 96), (1, 64)):
                S.op(DVE, lambda e, add=add: e.tensor_single_scalar(out=ci2[:], in_=ci[:], scalar=add, op=ALU.add), [b_WC], [b_WC])
                S.op(DVE, lambda e: e.tensor_single_scalar(out=ci2[:], in_=ci2[:], scalar=127, op=ALU.bitwise_and), [b_WC], [b_WC])
                S.op(ACT, lambda e, half=half: e.activation(out=WC[:, half * 128:(half + 1) * 128], in_=ci2[:], func=AF.Sin, scale=SC128, bias=nbias[:, 0:1]), [b_WC], [b_WC])
            S.op(POOL, lambda e: e.iota(kcol_i[:], pattern=[[128, NT]], base=0, channel_multiplier=1), [], [b_k])
            S.op(DVE, lambda e: e.tensor_copy(out=kcol[:], in_=kcol_i[:]), [b_k], [b_k])
            for t in range(NT):
                pz, bpz = (pa, b_pa2) if t % 2 == 0 else (pb, b_pb2)
                for g in range(4):
                    S.op(PE, lambda e, t=t, g=g, pz=pz: e.matmul(pz[:, g * 256:(g + 1) * 256], lhsT=mixT[:, 4 + g, t * 128:(t + 1) * 128], rhs=WC[:], start=(g % 2 == 0), stop=(g % 2 == 1)),
                         [b_mix[4 + g], b_WC], [bpz[g // 2]], inc=(g == 3))
                if t % 2 == 0:
                    S.op(ACT, lambda e, t=t, pz=pz: e.activation(out=Z[:, t, :, :].rearrange("p g n -> p (g n)"), in_=pz[:], func=AF.Copy), bpz, [b_Z])
                else:
                    S.op(DVE, lambda e, t=t, pz=pz: e.tensor_copy(out=Z[:, t, :, :].rearrange("p g n -> p (g n)"), in_=pz[:]), bpz, [b_Z])
            S.barrier()
            accD = [(pa, 0, b_pa2[0]), (pa, 512, b_pa2[1]), (pb, 0, b_pb2[0]), (pb, 512, b_pb2[1])]
            cnt = 0
            rotw = 0
            wdn_i = 0
            for jr in range(8):
                S.op(POOL, lambda e, jr=jr: e.iota(jr_i[:], pattern=[[1, 512]], base=jr * 512, channel_multiplier=0), [], [b_jr])
                for kt in range(NT):
                    s2, s3 = cnt % 2, cnt % 3
                    cnt += 1
                    if kt % 4 == 0 and wdn_i < NCH:
                        q_ = wdn_i % 2
                        S.dma(SP, stgD[q_][:], wdn_d[wdn_i * 128:(wdn_i + 1) * 128, :], [], [b_stgD[q_]], d_stgD[q_])
                        S.op(POOL, lambda e, q_=q_, wi=wdn_i: e.tensor_copy(out=Wdn[:, wi, :], in_=stgD[q_][:]), [b_stgD[q_]], [b_Wdn])
                        wdn_i += 1
                    S.op(DVE, lambda e, kt=kt, s2=s2: e.tensor_scalar(out=Pi[s2][:], in0=jr_i[:], scalar1=kcol[:, kt:kt + 1], scalar2=None, op0=ALU.mult), [b_jr, b_k], [b_Pi[s2]])
                    S.op(DVE, lambda e, kt=kt, s2=s2: e.tensor_scalar(out=Ci[s2][:], in0=jr_i[:], scalar1=kcol[:, kt:kt + 1], scalar2=3072.0, op0=ALU.mult, op1=ALU.add), [b_jr, b_k], [b_Ci[s2]])
                    S.op(DVE, lambda e, s2=s2: e.tensor_single_scalar(out=Pi[s2][:], in_=Pi[s2][:], scalar=4095, op=ALU.bitwise_and), [b_Pi[s2]], [b_Pi[s2]])
                    S.op(DVE, lambda e, s2=s2: e.tensor_single_scalar(out=Ci[s2][:], in_=Ci[s2][:], scalar=4095, op=ALU.bitwise_and), [b_Ci[s2]], [b_Ci[s2]])
                    S.op(ACT, lambda e, s2=s2, s3=s3: e.activation(out=tab[s3][:, 0, :], in_=Ci[s2][:], func=AF.Sin, scale=SC4096, bias=nbias[:, 0:1]), [b_Ci[s2]], [b_tab[s3]])
                    S.op(ACT, lambda e, s2=s2, s3=s3: e.activation(out=tab[s3][:, 1, :], in_=Pi[s2][:], func=AF.Sin, scale=SC4096, bias=nbias[:, 0:1]), [b_Pi[s2]], [b_tab[s3]])
                    for g in range(4):
                        pp, off, bb = accD[g]
                        S.op(PE, lambda e, kt=kt, g=g, pp=pp, off=off, s3=s3: e.matmul(pp[:, off:off + 512], lhsT=Z[:, kt, g, 0:128], rhs=tab[s3][:, 0, :], start=(kt == 0), stop=False),
                             [b_Z, b_tab[s3]], [bb], inc=False)
                        S.op(PE, lambda e, kt=kt, g=g, pp=pp, off=off, s3=s3: e.matmul(pp[:, off:off + 512], lhsT=Z[:, kt, g, 128:256], rhs=tab[s3][:, 1, :], start=False, stop=(kt == NT - 1)),
                             [b_Z, b_tab[s3]], [bb], inc=(kt == NT - 1 or g == 3))
                for g in range(4):
                    pp, off, bb = accD[g]
                    S.op(DVE, lambda e, g=g, pp=pp, off=off: e.tensor_scalar(out=Fsb[:, g, :], in0=pp[:, off:off + 512], scalar1=FT_NORM, scalar2=None, op0=ALU.mult), [bb], [b_F])
                for g in range(4):
                    bank = rotw % 3; rotw += 1
                    S.op(PE, lambda e, g=g, bank=bank: e.matmul(pc[:, bank * 512:(bank + 1) * 512], lhsT=wft[:, g, :], rhs=Fsb[:, g, :], start=True, stop=True), [b_F, b_const], [b_pc3[bank]])
                    S.op(DVE, lambda e, g=g, bank=bank, jr=jr: e.tensor_copy(out=mixT[:, 4 + g, jr * 512:(jr + 1) * 512], in_=pc[:, bank * 512:(bank + 1) * 512]), [b_pc3[bank]], [b_mix[4 + g]])
            S.barrier()
            if debug and "mixft" in debug:
                dd = S.dsem("dbgD")
                S.dma(SP, dbg["mixft"], mixT[:, 4:8, :], b_mix[4:8], [], dd); out_sems.append(dd)
                S.barrier()
            S.flush()
        S.st = st
        if debug and debug.get("_stop") == "D":
            S.barrier(); S.flush(out_sems)
            return nc
        S.barrier()
        with ExitStack() as phf:
            S.st = phf
            NX1 = 5
            Wout = S.sb("Wout", [128, KC, D], BF16)
            gfin = S.sb("gfin", [128, D], F32)
            x1 = S.sb("x1", [128, NX1, D], F32)
            xn2 = [S.sb("xn2_%d" % i, [128, D], BF16) for i in range(2)]
            h2T = S.sb("h2T", [128, KC, 514], BF16)
            aT = S.sb("aT", [128, NCH, 512], BF16)
            wup = [S.sb("wup%d" % i, [128, 2, KC, 128], BF16) for i in range(3)]
            tgv = [S.sb("tgv%d" % i, [128, 2, 256], F32) for i in range(3)]
            junkF2 = S.sb("junkF2", [128, D], BF16)
            ssE = S.sb("ssE", [128, 4], F32)
            ssF = S.sb("ssF", [128, 8], F32)
            hr = S.sb("hr", [128, KC, 8], BF16)
            b_hr = S.buf()
            b_Wout, b_gf, b_h2T = (S.buf() for _ in range(3))
            b_xn2 = [S.buf(), S.buf()]
            b_ssF = [S.buf() for _ in range(8)]
            b_x1 = [S.buf() for _ in range(NX1)]
            b_aT = [[S.buf(), S.buf()] for _ in range(NCH)]
            b_wup = [S.buf(), S.buf(), S.buf()]; b_tgv = [[S.buf(), S.buf()] for _ in range(3)]; b_junk2 = S.buf(); b_ssE = [S.buf() for _ in range(4)]
            d_gf, d_xt, d_stg = S.dsem("gf"), S.dsem("xtF"), S.dsem("stgF")
            d_wup = [S.dsem("wup0"), S.dsem("wup1"), S.dsem("wup2")]
            d_y = [S.dsem("y%d" % i) for i in range(NX1)]
            out_sems.extend(d_y)
            S.dma(SP, gfin[:], gfin_d.partition_broadcast(128), [], [b_gf], d_gf)
            b_stg = [S.buf(), S.buf()]; d_stg2 = [S.dsem("stgF0"), S.dsem("stgF1")]
            wl = 0
            for kc in range(KC):
                s_ = wl % 2; wl += 1
                S.dma(SP if s_ == 0 else POOL, x1[:, s_, :], wout_d[kc * 128:(kc + 1) * 128, :], [], [b_stg[s_]], d_stg2[s_])
                S.op(DVE if s_ == 0 else ACT, (lambda e, kc=kc, s_=s_: e.tensor_copy(out=Wout[:, kc, :], in_=x1[:, s_, :])) if s_ == 0 else
                     (lambda e, kc=kc, s_=s_: e.activation(out=Wout[:, kc, :], in_=x1[:, s_, :], func=AF.Copy)), [b_stg[s_]], [b_Wout])
            for s_ in range(2):
                b_x1[s_].w = b_stg[s_].w; b_x1[s_].r = list(b_stg[s_].r)

            fcount = [0]

            def frontA(tok0, n, x1dst, b_dst, step=1):
                k = fcount[0]; fcount[0] += 1
                tsl = slice(tok0, tok0 + n * step, step)
                sl_, sc_ = k % 2, k % 8
                pp, bpp = (pa, b_pa2) if k % 2 == 0 else (pb, b_pb2)
                S.dma(SP, x1dst, x_d[tsl, :], [], [b_dst], d_xt)
                for half in range(2):
                    for kc in range(KC):
                        S.op(PE, lambda e, half=half, kc=kc: e.matmul(pp[0:n, half * 512:(half + 1) * 512], lhsT=mixT[:, kc, tsl], rhs=Wout[:, kc, half * 512:(half + 1) * 512], start=(kc == 0), stop=(kc == KC - 1)),
                             b_mix + [b_Wout], [bpp[half]], inc=(kc == KC - 1))
                S.op(DVE, lambda e: e.tensor_tensor(out=x1dst, in0=x1dst, in1=pp[0:n, :], op=ALU.add), [b_dst] + bpp, [b_dst])
                S.op(ACT, lambda e: e.activation(out=xn2[sl_][0:n, :], in_=x1dst, func=AF.Square, accum_out=ssF[0:n, sc_:sc_ + 1]), [b_dst], [b_xn2[sl_], b_ssF[sc_]])
                emit_rstd(ssF[0:n, sc_:sc_ + 1], ssF[0:n, sc_:sc_ + 1], D, EPS, [b_ssF[sc_]])
                S.op(DVE, lambda e: e.tensor_scalar(out=xn2[sl_][0:n, :], in0=x1dst, scalar1=ssF[0:n, sc_:sc_ + 1], scalar2=None, op0=ALU.mult), [b_dst, b_ssF[sc_]], [b_xn2[sl_]])
                return sl_

            def frontB(sl_, n, col, dstT=None, b_dT=None):
                dstT = h2T if dstT is None else dstT
                b_dT = b_h2T if b_dT is None else b_dT
                for kc in range(KC):
                    S.op(PE, lambda e, kc=kc: e.transpose(pt[:, kc * 128:kc * 128 + n], xn2[sl_][0:n, kc * 128:(kc + 1) * 128], ident[0:n, 0:n]), [b_xn2[sl_], b_const], [b_pt], inc=(kc == KC - 1))
                S.op(ACT, lambda e: e.activation(out=dstT[:, :, col:col + n], in_=pt[:].rearrange("p (k m) -> p k m", k=KC)[:, :, 0:n], func=AF.Copy), [b_pt], [b_dT])

            def x1slot(g):
                return x1[:, g % NX1, :], b_x1[g % NX1]

            S.op(DVE, lambda e: e.memset(h2T[:, :, 0:1], 0.0), [], [b_h2T])
            for j in range(4):
                xs, bx = x1slot(j)
                frontB(frontA(j * 128, 128, xs, bx), 128, 1 + j * 128)
            frontB(frontA(512, 7, x1[0:7, 4, :], b_x1[4], step=512), 7, 0, hr, b_hr)
            S.op(DVE, lambda e: e.tensor_copy(out=h2T[:, :, 513:514], in_=hr[:, :, 0:1]), [b_hr, b_h2T], [b_h2T])

            cw = [0]
            cs = [0]
            for b in range(8):
                t0 = b * 512
                prev = None
                units = [(i, half) for i in range(NCH) for half in range(2)]
                for unit in units + [None]:
                    cur = None
                    if unit is not None:
                        i, half = unit
                        if half == 0:
                            n_ = cw[0]; cw[0] += 1
                            ws = n_ % 3
                            if n_ == 0:
                                for q_ in range(2):
                                    S.dma(SP, wup[q_][:].rearrange("p a k c -> p (a k c)"), wup_s[q_, :, :], [], [b_wup[q_]], d_wup[q_])
                            if n_ + 2 < 8 * NCH:
                                q_ = (n_ + 2) % 3
                                S.dma(SP, wup[q_][:].rearrange("p a k c -> p (a k c)"), wup_s[(n_ + 2) % NCH, :, :], [], [b_wup[q_]], d_wup[q_])
                        c0 = half * 256
                        u = cs[0]; cs[0] += 1
                        s_, s3 = u % 2, u % 3
                        pp, bpp = (pa, b_pa2) if s_ == 0 else (pb, b_pb2)
                        for gv in range(2):
                            for kc in range(KC):
                                S.op(PE, lambda e, gv=gv, kc=kc, ws=ws, c0=c0, pp=pp: e.matmul(pp[:, gv * 512:gv * 512 + 258], lhsT=wup[ws][:, gv, kc, :], rhs=h2T[:, kc, c0:c0 + 258], start=(kc == 0), stop=(kc == KC - 1)),
                                     [b_wup[ws], b_h2T], [bpp[gv]], inc=(kc == KC - 1))
                        for gv in range(2):
                            ch = i + gv * NCH
                            S.op(ACT, lambda e, gv=gv, ch=ch, s3=s3, pp=pp: e.activation(out=tgv[s3][:, gv, :], in_=pp[:, gv * 512 + 1:gv * 512 + 257], func=AF.Identity, scale=wcv[:, 1, ch:ch + 1], bias=wcv[:, 3, ch:ch + 1]),
                                 [bpp[gv], b_const], [b_tgv[s3][gv]])
                        for tap, off in ((0, 0), (2, 2)):
                            for gv in range(2):
                                ch = i + gv * NCH
                                S.op(DVE, lambda e, gv=gv, ch=ch, s3=s3, pp=pp, tap=tap, off=off: e.scalar_tensor_tensor(out=tgv[s3][:, gv, :], in0=pp[:, gv * 512 + off:gv * 512 + off + 256], scalar=wcv[:, tap, ch:ch + 1], in1=tgv[s3][:, gv, :], op0=ALU.mult, op1=ALU.add),
                                     [bpp[gv], b_const, b_tgv[s3][gv]], [b_tgv[s3][gv]])
                        cur = (i, half, s3, s_)
                    if prev is not None:
                        pi_, ph_, p3, p2 = prev
                        S.op(ACT, lambda e, p3=p3: e.activation(out=tgv[p3][:, 0, :], in_=tgv[p3][:, 0, :], func=AF.Silu), [b_tgv[p3][0]], [b_tgv[p3][0]])
                        S.op(POOL, lambda e, p3=p3, pi_=pi_, ph_=ph_: e.tensor_tensor(out=aT[:, pi_, ph_ * 256:ph_ * 256 + 256], in0=tgv[p3][:, 0, :], in1=tgv[p3][:, 1, :], op=ALU.mult),
                             [b_tgv[p3][0], b_tgv[p3][1]], [b_aT[pi_][ph_]])
                    prev = cur
                if b < 7:
                    S.op(DVE, lambda e: e.tensor_copy(out=h2T[:, :, 0:1], in_=h2T[:, :, 512:513]), [b_h2T], [b_h2T])
                    if b < 6:
                        S.op(DVE, lambda e, b=b: e.tensor_copy(out=h2T[:, :, 513:514], in_=hr[:, :, b + 1:b + 2]), [b_hr, b_h2T], [b_h2T])
                    else:
                        S.op(DVE, lambda e: e.memset(h2T[:, :, 513:514], 0.0), [b_h2T], [b_h2T])
                pendB = None
                for j in range(4):
                    g = 4 * b + j
                    xs, bx = x1slot(g)
                    for half in range(2):
                        for i in range(NCH):
                            S.op(PE, lambda e, j=j, half=half, i=i: e.matmul(pc[:, half * 512:(half + 1) * 512], lhsT=aT[:, i, j * 128:(j + 1) * 128], rhs=Wdn[:, i, half * 512:(half + 1) * 512], start=(i == 0), stop=(i == NCH - 1)),
                                 [b_aT[i][j // 2], b_Wdn], [b_pc3[half]], inc=(i == NCH - 1))
                    S.op(DVE, lambda e, xs=xs: e.tensor_tensor(out=xs, in0=xs, in1=pc[:, 0:1024], op=ALU.add), [b_pc3[0], b_pc3[1], bx], [bx])
                    if b < 7:
                        xs2, bx2 = x1slot(g + 4)
                        slA = frontA(t0 + 512 + j * 128, 128, xs2, bx2)
                    if pendB is not None:
                        frontB(*pendB)
                    pendB = (slA, 128, 1 + j * 128) if b < 7 else None
                    S.op(ACT, lambda e, xs=xs, j=j: e.activation(out=junkF2[:], in_=xs, func=AF.Square, accum_out=ssE[:, j:j + 1]), [bx], [b_junk2, b_ssE[j]])
                    emit_rstd(ssE[:, j:j + 1], ssE[:, j:j + 1], D, EPS, [b_ssE[j]])
                    S.op(DVE, lambda e, xs=xs, j=j: e.scalar_tensor_tensor(out=xs, in0=xs, scalar=ssE[:, j:j + 1], in1=gfin[:], op0=ALU.mult, op1=ALU.mult), [bx, b_ssE[j], b_gf], [bx])
                    S.dma(SP, y_d[t0 + j * 128:t0 + (j + 1) * 128, :], xs, [bx], [], d_y[g % NX1])
                if pendB is not None:
                    frontB(*pendB)
            S.barrier()
            S.flush(out_sems)
        S.st = st
    return nc


def kernel(**inputs):
    nc = build_program()
    x = np.asarray(inputs["x"], dtype=np.float32)
    B = x.shape[0]
    shared = {}
    for k, v in inputs.items():
        if k == "x":
            continue
        a = np.asarray(v, dtype=np.float32)
        shared[k] = np.ascontiguousarray(a[0] if k != "g_final" else a)
    in_maps = []
    for b in range(B):
        m = dict(shared)
        m["x"] = np.ascontiguousarray(x[b])
        in_maps.append(m)
    res = run_bass_kernel_spmd(nc, in_maps, core_ids=list(range(B)))
    return np.stack([np.asarray(r["y"], dtype=np.float32) for r in res.results], axis=0)
```
